# Optimizing a Trainium2 kernel written in Bass

```python
import math
import jax, jax.numpy as jnp
from jax import lax
import numpy as np


D_MODEL = 2048
BATCH = 16
SEQ = 2048
DEPTH = 4
DEC_BATCH = 32
DEC_SEQ = 32
PAST_LEN = 4096

CHUNK = 64
D_MIX = D_MODEL
HEAD_DIM = 64
D_ATTN = D_MIX // 2
D_GMLP = D_MIX - D_ATTN
N_HEADS = D_ATTN // HEAD_DIM
N_KV_HEADS = 4
GQA_GROUP = N_HEADS // N_KV_HEADS
D_KV = N_KV_HEADS * HEAD_DIM
WINDOW = 128
N_WIN_CHUNKS = WINDOW // CHUNK
N_BUCKETS = 32
MAX_DISTANCE = 128
GMLP_CHUNK = 128
N_GROUPS_B = D_GMLP // HEAD_DIM
D_PROJ = D_ATTN + 2 * D_KV + D_ATTN + 3 * D_GMLP
EPS = 1e-6
NEG_INF = -1e30

kernel_name = 'hymba_swa_sink_gmlp_stream_step'


def _rmsnorm(x, g):
    xf = x.astype(jnp.float32)
    y = xf * lax.rsqrt(jnp.mean(xf * xf, axis=-1, keepdims=True) + EPS)
    return (y * g.astype(jnp.float32)).astype(x.dtype)


def _layernorm(x, g, b):
    xf = x.astype(jnp.float32)
    mu = jnp.mean(xf, axis=-1, keepdims=True)
    xc = xf - mu
    y = xc * lax.rsqrt(jnp.mean(xc * xc, axis=-1, keepdims=True) + EPS)
    return (y * g.astype(jnp.float32) + b.astype(jnp.float32)).astype(x.dtype)


def _rel_bucket(rel):
    nb = N_BUCKETS // 2
    max_exact = nb // 2
    base = jnp.where(rel > 0, nb, 0)
    n = jnp.abs(rel)
    nf = jnp.maximum(n, 1).astype(jnp.float32)
    large = max_exact + (jnp.log(nf / max_exact) / math.log(MAX_DISTANCE / max_exact)
                         * (nb - max_exact)).astype(jnp.int32)
    large = jnp.minimum(large, nb - 1)
    return base + jnp.where(n < max_exact, n, large)


def _band_bias(rel_bias, n_q, n_past):
    i = jnp.arange(n_q)[:, None]
    j = jnp.arange(n_past + n_q)[None, :]
    b = rel_bias[_rel_bucket(j - n_past - i)]
    b = jnp.transpose(b, (2, 0, 1)).astype(jnp.float32)
    return b.reshape(N_KV_HEADS, GQA_GROUP, n_q, n_past + n_q)


def _band_attention(q, k, v, bias, valid, sinks):
    s = jnp.einsum('bnqhgd,bnkhd->bnhgqk', q, k).astype(jnp.float32) * (HEAD_DIM ** -0.5) + bias
    s = jnp.where(valid[None, :, None, None, None, :], s, NEG_INF)
    sink = sinks.astype(jnp.float32).reshape(N_KV_HEADS, GQA_GROUP)[None, None, :, :, None, None]
    m = jnp.maximum(jnp.max(s, axis=-1, keepdims=True), sink)
    p = jnp.exp(s - m)
    p = p / (jnp.sum(p, axis=-1, keepdims=True) + jnp.exp(sink - m))
    return jnp.einsum('bnhgqk,bnkhd->bnqhgd', p.astype(v.dtype), v)


def _project(x, g_in, w_in, ln_g, ln_b):
    B, S = x.shape[0], x.shape[1]
    h = _rmsnorm(x, g_in)
    p = jnp.einsum('bsd,de->bse', h, w_in)
    o1 = D_ATTN
    o2 = o1 + D_KV
    o3 = o2 + D_KV
    o4 = o3 + D_ATTN
    o5 = o4 + D_GMLP
    o6 = o5 + D_GMLP
    q = p[..., :o1].reshape(B, S, N_KV_HEADS, GQA_GROUP, HEAD_DIM)
    k = p[..., o1:o2].reshape(B, S, N_KV_HEADS, HEAD_DIM)
    v = p[..., o2:o3].reshape(B, S, N_KV_HEADS, HEAD_DIM)
    gate_a = p[..., o3:o4]
    u = jax.nn.gelu(p[..., o4:o5], approximate=False).reshape(B, S, N_GROUPS_B, HEAD_DIM)
    vg = _layernorm(jax.nn.gelu(p[..., o5:o6], approximate=False), ln_g, ln_b)
    vg = vg.reshape(B, S, N_GROUPS_B, HEAD_DIM)
    gate_b = p[..., o6:]
    return q, k, v, gate_a, u, vg, gate_b


def _attn_prompt(q, k, v, rel_bias, sinks):
    B, S = q.shape[0], q.shape[1]
    nC = S // CHUNK
    qc = q.reshape(B, nC, CHUNK, N_KV_HEADS, GQA_GROUP, HEAD_DIM)

    def band(t):
        tc = t.reshape(B, nC, CHUNK, N_KV_HEADS, HEAD_DIM)
        pad = jnp.zeros((B, N_WIN_CHUNKS, CHUNK, N_KV_HEADS, HEAD_DIM), t.dtype)
        tp = jnp.concatenate([pad, tc], axis=1)
        return jnp.concatenate([tp[:, w:w + nC] for w in range(N_WIN_CHUNKS + 1)], axis=2)

    kb = band(k)
    vb = band(v)
    key_pos = (jnp.arange(nC)[:, None] - N_WIN_CHUNKS) * CHUNK + jnp.arange(WINDOW + CHUNK)[None, :]
    valid = key_pos >= 0
    bias = _band_bias(rel_bias, CHUNK, WINDOW)
    o = _band_attention(qc, kb, vb, bias, valid, sinks)
    return o.reshape(B, S, D_ATTN), k[:, -WINDOW:], v[:, -WINDOW:]


def _attn_sample(q, k, v, ck, cv, rel_bias, sinks):
    B, T = q.shape[0], q.shape[1]
    n_past = ck.shape[1]
    kb = jnp.concatenate([ck, k.astype(ck.dtype)], axis=1)
    vb = jnp.concatenate([cv, v.astype(cv.dtype)], axis=1)
    bias = _band_bias(rel_bias, T, n_past)
    valid = jnp.ones((1, n_past + T), dtype=bool)
    o = _band_attention(q[:, None], kb[:, None], vb[:, None], bias, valid, sinks)
    return o.reshape(B, T, D_ATTN), kb[:, T:], vb[:, T:]


def _gmlp_mix(u, vg, w_s, b_s):
    L = vg.shape[2]
    ws = jnp.tril(w_s[:, :L, :L]).astype(vg.dtype)
    mix = jnp.einsum('gij,bcjgd->bcigd', ws, vg) + b_s[:, :L].T.astype(vg.dtype)[None, None, :, :, None]
    return u * mix


def _merge(attn_o, gmlp_o, gate_a, gate_b, g_attn, g_gmlp, w_out):
    ya = _rmsnorm(attn_o, g_attn) * jax.nn.silu(gate_a)
    yb = _rmsnorm(gmlp_o, g_gmlp) * jax.nn.silu(gate_b)
    return jnp.einsum('bse,ed->bsd', jnp.concatenate([ya, yb], axis=-1), w_out)


def setup_inputs(seed: int = 0) -> dict:
    key = jax.random.key(seed)
    ks = jax.random.split(key, 16)
    cache_rows = min(WINDOW, PAST_LEN)
    f32 = jnp.float32
    tril = jnp.tril(jnp.ones((GMLP_CHUNK, GMLP_CHUNK), f32))
    return {
        'x_prompt': jax.random.normal(ks[0], (BATCH, SEQ, D_MODEL), f32),
        'x_sample': jax.random.normal(ks[1], (DEC_BATCH, DEC_SEQ, D_MODEL), f32),
        'cache_k': jax.random.normal(ks[2], (DEPTH, DEC_BATCH, cache_rows, N_KV_HEADS, HEAD_DIM), f32),
        'cache_v': jax.random.normal(ks[3], (DEPTH, DEC_BATCH, cache_rows, N_KV_HEADS, HEAD_DIM), f32),
        'w_in': jax.random.normal(ks[4], (DEPTH, D_MODEL, D_PROJ), f32) * D_MODEL ** -0.5,
        'w_out': jax.random.normal(ks[5], (DEPTH, D_MIX, D_MODEL), f32) * D_MIX ** -0.5,
        'norm_in': 1.0 + 0.01 * jax.random.normal(ks[6], (DEPTH, D_MODEL), f32),
        'rel_bias': 0.1 * jax.random.normal(ks[7], (N_BUCKETS, N_HEADS), f32),
        'sinks': 0.5 * jax.random.normal(ks[8], (DEPTH, N_HEADS), f32),
        'norm_attn': 1.0 + 0.01 * jax.random.normal(ks[9], (DEPTH, D_ATTN), f32),
        'norm_gmlp': 1.0 + 0.01 * jax.random.normal(ks[10], (DEPTH, D_GMLP), f32),
        'ln_v_g': 1.0 + 0.01 * jax.random.normal(ks[11], (DEPTH, D_GMLP), f32),
        'ln_v_b': 0.01 * jax.random.normal(ks[12], (DEPTH, D_GMLP), f32),
        'w_spatial': jax.random.normal(ks[13], (DEPTH, N_GROUPS_B, GMLP_CHUNK, GMLP_CHUNK), f32)
                     * tril * GMLP_CHUNK ** -0.5,
        'b_spatial': 1.0 + 0.01 * jax.random.normal(ks[14], (DEPTH, N_GROUPS_B, GMLP_CHUNK), f32),
        'norm_final': 1.0 + 0.01 * jax.random.normal(ks[15], (D_MODEL,), f32),
    }


def reference(x_prompt, x_sample, cache_k, cache_v, w_in, w_out, norm_in, rel_bias, sinks,
              norm_attn, norm_gmlp, ln_v_g, ln_v_b, w_spatial, b_spatial, norm_final):
    xp = x_prompt
    xs = x_sample
    Bp, S = xp.shape[0], xp.shape[1]
    Bs, T = xs.shape[0], xs.shape[1]
    n_gchunks = S // GMLP_CHUNK
    kp_rows, vp_rows, ks_rows, vs_rows, vg_rows = [], [], [], [], []
    for l in range(DEPTH):
        q, k, v, ga, u, vg, gb = _project(xp, norm_in[l], w_in[l], ln_v_g[l], ln_v_b[l])
        ao, kw, vw = _attn_prompt(q, k, v, rel_bias, sinks[l])
        go = _gmlp_mix(u.reshape(Bp, n_gchunks, GMLP_CHUNK, N_GROUPS_B, HEAD_DIM),
                       vg.reshape(Bp, n_gchunks, GMLP_CHUNK, N_GROUPS_B, HEAD_DIM),
                       w_spatial[l], b_spatial[l]).reshape(Bp, S, D_GMLP)
        xp = xp + _merge(ao, go, ga, gb, norm_attn[l], norm_gmlp[l], w_out[l])
        kp_rows.append(kw)
        vp_rows.append(vw)
        q, k, v, ga, u, vg, gb = _project(xs, norm_in[l], w_in[l], ln_v_g[l], ln_v_b[l])
        ao, kw, vw = _attn_sample(q, k, v, cache_k[l], cache_v[l], rel_bias, sinks[l])
        go = _gmlp_mix(u[:, None], vg[:, None], w_spatial[l], b_spatial[l]).reshape(Bs, T, D_GMLP)
        xs = xs + _merge(ao, go, ga, gb, norm_attn[l], norm_gmlp[l], w_out[l])
        ks_rows.append(kw)
        vs_rows.append(vw)
        vg_rows.append(vg)
    y_prompt = _rmsnorm(xp, norm_final)
    y_sample = _rmsnorm(xs, norm_final)
    new_k_prompt = jnp.stack(kp_rows)
    new_v_prompt = jnp.stack(vp_rows)
    new_k_sample = jnp.stack(ks_rows)
    new_v_sample = jnp.stack(vs_rows)
    new_vgmlp_sample = jnp.stack(vg_rows)
    return (y_prompt, y_sample, new_k_prompt, new_v_prompt, new_k_sample, new_v_sample, new_vgmlp_sample)
```

```python
import contextlib
import numpy as np
import concourse.bass as bass
import concourse.mybir as mybir
from concourse.bass_utils import run_bass_kernel_spmd

F32 = mybir.dt.float32
BF16 = mybir.dt.bfloat16
AF = mybir.ActivationFunctionType
ALU = mybir.AluOpType
AX = mybir.AxisListType

D = 2048
DPROJ = 5632
NCORES = 8
EPS = 1e-6
NEG = -1e30


class Res:
    __slots__ = ("name", "w", "rs", "excl")

    def __init__(self, name, excl=False):
        self.name = name
        self.w = None
        self.rs = []
        self.excl = excl


class Sched:
    ENGS = ("pe", "act", "dve", "pool", "sp")

    def __init__(self, same_engine_sync=True):
        self.q = {e: [] for e in self.ENGS}
        self.cnt = {e: 0 for e in self.ENGS}
        self.seen = {e: {} for e in self.ENGS}
        self.dcnt = {}
        self.same = same_engine_sync

    def _deps(self, reads, writes):
        deps = {}

        def need(d):
            if d is not None:
                k, v = d
                if deps.get(k, 0) < v:
                    deps[k] = v
        for r in reads:
            need(r.w)
        for w in writes:
            need(w.w)
            for x in w.rs:
                need(x)
        return deps

    def _waits(self, eng, deps):
        waits = []
        for k, v in deps.items():
            if k == eng and (eng in ("pe", "sp") or not self.same):
                continue
            if self.seen[eng].get(k, 0) >= v:
                continue
            self.seen[eng][k] = v
            waits.append((k, v))
        return waits

    def _stamp(self, me, reads, writes):
        for r in reads:
            r.rs.append(me)
        for w in writes:
            w.w = me
            w.rs = []

    @staticmethod
    def _excl(reads, writes):
        ex = [r for r in reads if r.excl]
        if ex:
            writes = list(writes) + [r for r in ex if r not in writes]
            reads = [r for r in reads if not r.excl]
        return reads, writes

    def op(self, eng, name, kw, reads=(), writes=(), inc=True):
        reads, writes = self._excl(reads, writes)
        deps = self._deps(reads, writes)
        tick = self.cnt[eng] + 1
        if inc:
            self.cnt[eng] = tick
        waits = self._waits(eng, deps)
        self.q[eng].append((waits, (name, kw), (eng, 1) if inc else None))
        self._stamp((eng, tick), reads, writes)

    def dma(self, qeng, semkey, name, kw, reads=(), writes=()):
        deps = self._deps(reads, writes)
        prev = self.dcnt.get(semkey, 0)
        if prev and deps.get(semkey, 0) < prev:
            deps[semkey] = prev
        val = prev + 16
        self.dcnt[semkey] = val
        waits = self._waits(qeng, deps)
        self.q[qeng].append((waits, (name, kw), (semkey, 16)))
        self._stamp((semkey, val), reads, writes)

    def final_waits(self, eng):
        self.q[eng].append(([(k, v) for k, v in self.dcnt.items()], None, None))

    def replay(self, eng, e, sems):
        for waits, fn, inc in self.q[eng]:
            for k, v in waits:
                e.wait_ge(sems[k], v)
            if fn is None:
                continue
            ins = getattr(e, fn[0])(**fn[1])
            if inc is not None:
                ins.then_inc(sems[inc[0]], inc[1])


def _bucket_onehot():
    import math
    import jax
    import jax.numpy as jnp
    cpu = jax.devices("cpu")[0]
    with jax.default_device(cpu):
        jp = np.arange(128)[:, None]
        ip = np.arange(128)[None, :]
        oh = np.zeros((33, 2, 128, 128), np.float32)
        for rel in range(2):
            delta = jnp.asarray((rel - 1) * 128 + jp - ip, dtype=jnp.int32)
            nb = 16
            max_exact = 8
            base = jnp.where(delta > 0, nb, 0)
            n = jnp.abs(delta)
            nf = jnp.maximum(n, 1).astype(jnp.float32)
            large = max_exact + (jnp.log(nf / max_exact) / math.log(128 / max_exact)
                                 * (nb - max_exact)).astype(jnp.int32)
            large = jnp.minimum(large, nb - 1)
            bucket = np.asarray(base + jnp.where(n < max_exact, n, large))
            ck = (np.arange(128) // 64)[:, None]
            cq = (np.arange(128) // 64)[None, :]
            masked = (ck == 0) & (cq == 1) if rel == 0 else (ck == 1) & (cq == 0)
            for b in range(32):
                oh[b, rel] = (bucket == b).T.astype(np.float32)
            oh[32, rel] = masked.T.astype(np.float32)
    return oh.reshape(33, 2, 128 * 128)


class _Stop(Exception):
    pass


def build_program(n_layers=4, tiles=None, seq_len=2048, n_seq=2, same_engine_sync=True, stop_at=None):
    if tiles is None:
        tiles = [(s, p, (s == 0 and p == 0)) for s in range(n_seq) for p in range(seq_len // 512)]
    last_part = seq_len // 512 - 1
    L = n_layers
    nc = bass.Bass("TRN2", target_bir_lowering=False)

    def din(name, shape):
        return nc.dram_tensor(name, list(shape), F32, kind="ExternalInput").ap()

    def dout(name, shape):
        return nc.dram_tensor(name, list(shape), F32, kind="ExternalOutput").ap()

    xp = din("xp", [n_seq, seq_len, D])
    xs = din("xs", [128, D])
    ck = din("ck", [L, 4, 128, 256])
    cv = din("cv", [L, 4, 128, 256])
    w_in = din("w_in", [L, D, DPROJ])
    w_out = din("w_out", [L, D, D])
    norm_in = din("norm_in", [L, D])
    rel_bias = din("rel_bias", [32, 16])
    sinks = din("sinks", [L, 16])
    norm_attn = din("norm_attn", [L, 1024])
    norm_gmlp = din("norm_gmlp", [L, 1024])
    ln_g = din("ln_g", [L, 1024])
    ln_b = din("ln_b", [L, 1024])
    w_sp = din("w_sp", [L, 16, 128, 128])
    b_sp = din("b_sp", [L, 16, 128])
    norm_final = din("norm_final", [1, D])
    oh_d = din("oh", [33, 2, 16384])

    y_p = dout("y_p", [n_seq, seq_len, D])
    y_s = dout("y_s", [128, D])
    nkp = dout("nkp", [L, n_seq, 128, 256])
    nvp = dout("nvp", [L, n_seq, 128, 256])
    nks = dout("nks", [L, 4, 128, 256])
    nvs = dout("nvs", [L, 4, 128, 256])
    nvg = dout("nvg", [L, 128, 1024])

    S = Sched(same_engine_sync)
    NT = 640
    NSLOT = 3

    with contextlib.ExitStack() as es:
        def sb(name, shape, dt):
            return es.enter_context(nc.sbuf_tensor(name, list(shape), dt))

        x = sb("x", [128, 5, D], F32)
        hT = sb("hT", [128, 16, NT], BF16)
        zT = sb("zT", [128, 8, NT], BF16)
        kT = sb("kT", [128, 2, 1152], BF16)
        carK = sb("carK", [128, L, 2, 128], BF16)
        Vb = sb("Vb", [128, 9, 256], BF16)
        carV = sb("carV", [128, L, 256], BF16)
        BT = sb("BT", [128, 2, 16, 128], BF16)
        sgvg = sb("sgvg", [128, 5, 1024], BF16)
        tabA = sb("tabA", [128, 2048], F32)
        wsT = sb("wsT", [128, 16, 128], BF16)
        wsTs = sb("wsTs", [128, 16, 128], BF16)
        bsT = sb("bsT", [128, 8, 128], F32)
        wsl = [sb("wsl%d" % i, [128, 8192], BF16) for i in range(NSLOT)]
        xh = sb("xh", [128, D], BF16)
        junk = sb("junk", [128, 2048], BF16)
        kvo_t = sb("kvo", [128, 2, 512], F32)
        kvo = [kvo_t[:, 0, :], kvo_t[:, 1, :]]
        xhs = [xh[:, :], kvo_t[:, :, :].rearrange("p a n -> p (a n)").bitcast(BF16)]
        kTz1 = [xhs[0][:, 0:1280], xhs[1][:, 0:1280]]
        scr = sb("scr", [128, 7680], BF16)
        ckb = sb("ckb", [128, 4, 256], BF16)
        ident = sb("ident", [128, 128], BF16)
        identf = sb("identf", [128, 128], F32)
        ones = sb("ones", [128, 128], BF16)
        mhalf = sb("mhalf", [128, 16], F32)
        esink = sb("esink", [128, L * 16], F32)
        gA = sb("gA", [128, L * 8], F32)
        gG = sb("gG", [128, L * 8], F32)
        rbA = sb("rbA", [33, 16], F32)
        rbB = sb("rbB", [33, 16], BF16)
        st_ss = sb("st_ss", [128, 8], F32)
        st_r = sb("st_r", [128, 8], F32)
        st_ssa = sb("st_ssa", [128, 2, 8], F32)
        st_ra = sb("st_ra", [128, 8], F32)
        st_rb = sb("st_rb", [128, 8], F32)
        st_d = sb("st_d", [128, 2, 8], F32)
        st_bn = sb("st_bn", [128, 2, 2, 6], F32)
        st_mv = sb("st_mv", [128, 2, 2], F32)
        st_lr = sb("st_lr", [128, 2], F32)

        ps = es.enter_context(nc.psum_tensor("ps", [128, 8, 512], F32))
        psb = ps[:, :, :].bitcast(BF16)
        gv = junk[:, :].bitcast(F32)
        ohs = x[:, 0:4, :].bitcast(BF16).rearrange("p a n -> p (a n)")

        gu = scr[:, 0:2560].rearrange("p (g t) -> p g t", t=NT)
        kTz0 = scr[:, 0:2560].rearrange("p (a n) -> p a n", n=1280)
        pt = [scr[:, 2560 + 512 * i:2560 + 512 * (i + 1)] for i in range(4)]
        ao = [scr[:, 4608 + 1024 * i:4608 + 1024 * (i + 1)].bitcast(F32) for i in range(2)]
        zam = [scr[:, 6656 + 512 * i:6656 + 512 * (i + 1)] for i in range(2)]
        sgb = [scr[:, 2560 + 640 * i:2560 + 640 * (i + 1)] for i in range(2)]
        go = [scr[:, 3840 + 1280 * i:3840 + 1280 * (i + 1)].bitcast(F32) for i in range(2)]
        sqb = [scr[:, 6400 + 640 * i:6400 + 640 * (i + 1)] for i in range(2)]
        wn = junk[:, :].rearrange("p (g j) -> p g j", j=128)
        gv2 = scr[:, 3840:5888].bitcast(F32)
        tabB = scr[:, 0:4096].bitcast(F32)

        semnames = ["pe", "act", "dve", "pool"] + ["w%d" % i for i in range(NSLOT)] + ["ld%d" % i for i in range(8)] + \
                   ["st%d" % i for i in range(8)] + ["pl%d" % i for i in range(4)]
        sems = {k: es.enter_context(nc.semaphore("s_" + k)) for k in semnames}
        rr = {"ld": 0, "st": 0, "pl": 0}

        def semof(kind):
            n = {"ld": 8, "st": 8, "pl": 4}[kind]
            i = rr[kind]
            rr[kind] = (i + 1) % n
            return "%s%d" % (kind, i)

        R = {}

        def r1(name):
            if name not in R:
                R[name] = Res(name)
            return R[name]
        Rx = [r1("x%d" % i) for i in range(5)]
        RhT = [r1("hT%d" % i) for i in range(5)]
        RzT = [[r1("zT%d_%d" % (c, t)) for t in range(5)] for c in range(8)]
        RkT = [[r1("kT%d_%d" % (p, k)) for k in range(9)] for p in range(2)]
        RcarK = [r1("carK%d" % l) for l in range(L)]
        RV = [r1("V%d" % k) for k in range(9)]
        RcarV = [r1("carV%d" % l) for l in range(L)]
        Rbank = [r1("bank%d" % i) for i in range(8)]
        for rb_ in Rbank:
            rb_.excl = True
        Rsg = [r1("sgvg%d" % t) for t in range(5)]
        Rw = [r1("wsl%d" % i) for i in range(NSLOT)]
        Rkvo = [r1("kvo0"), r1("kvo1")]
        Rstd = [r1("std0"), r1("std1")]
        span = {}
        for g in range(4):
            span["qT%d" % g] = (640 * g, 640 * (g + 1))
        for i in range(4):
            span["pt%d" % i] = (2560 + 512 * i, 2560 + 512 * (i + 1))
        for i in range(2):
            span["ao%d" % i] = (4608 + 1024 * i, 4608 + 1024 * (i + 1))
            span["zam%d" % i] = (6656 + 512 * i, 6656 + 512 * (i + 1))
            span["sgb%d" % i] = (2560 + 640 * i, 2560 + 640 * (i + 1))
            span["go%d" % i] = (3840 + 1280 * i, 3840 + 1280 * (i + 1))
            span["sqb%d" % i] = (6400 + 640 * i, 6400 + 640 * (i + 1))

        span["gvB"] = (3840, 5888)
        span["tabB"] = (0, 4096)

        for (a_, b_) in span.values():
            assert a_ % 128 == 0 and b_ % 128 == 0
        segs = [r1("scrseg%d" % i) for i in range(7680 // 128)]

        def WR(name):
            a, b = span[name]
            return segs[a // 128:b // 128]

        RD = WR

        def ld(name, kw, writes, reads=()):
            S.dma("sp", semof("ld"), name, kw, reads=reads, writes=writes)

        def store(name, kw, reads):
            S.dma("sp", semof("st"), name, kw, reads=reads, writes=())

        def pool_dma(name, kw, writes, reads=(), sem=None):
            S.dma("pool", sem or semof("pl"), name, kw, reads=reads, writes=writes)

        cnts = {"pj": 0, "tm": 0, "ev": 0, "st": 0, "ob": 0, "ao": 0}

        def pj_bank():
            cnts["pj"] += 1
            return cnts["pj"] % 2

        def tm_bank():
            cnts["tm"] += 1
            return 6 + cnts["tm"] % 2

        def evac_copy(out, in_, reads, writes, scale=None):
            cnts["ev"] += 1
            if cnts["ev"] % 2 == 0:
                kw = dict(out=out, in_=in_, func=AF.Copy)
                if scale is not None:
                    kw["scale"] = scale
                S.op("act", "activation", kw, reads=reads, writes=writes)
            else:
                if scale is None:
                    S.op("dve", "tensor_copy", dict(out=out, in_=in_), reads=reads, writes=writes)
                else:
                    S.op("dve", "tensor_scalar", dict(out=out, in0=in_, scalar1=scale, scalar2=None, op0=ALU.mult), reads=reads, writes=writes)

        def rsqrt_small(dst, src, ncol, scale, reads, writes):
            S.op("pool", "tensor_scalar", dict(out=dst, in0=src, scalar1=scale, scalar2=EPS, op0=ALU.mult, op1=ALU.add),
                 reads=reads, writes=writes)
            S.op("pool", "tensor_tensor", dict(out=dst, in0=dst, in1=mhalf[:, 0:ncol], op=ALU.pow),
                 reads=list(writes) + [r1("mhalf")], writes=writes)

        S.op("pool", "memset", dict(ap=identf[:], constant=0.0), writes=[r1("identf")])
        S.op("pool", "affine_select", dict(out=identf[:], in_=identf[:], pattern=[[-1, 128]], compare_op=ALU.not_equal,
                                           fill=1.0, base=0, channel_multiplier=1), reads=[r1("identf")], writes=[r1("identf")])
        S.op("dve", "tensor_copy", dict(out=ident[:], in_=identf[:]), reads=[r1("identf")], writes=[r1("ident")])
        S.op("pool", "memset", dict(ap=ones[:], constant=1.0), writes=[r1("ones")])
        S.op("pool", "memset", dict(ap=mhalf[:], constant=-0.5), writes=[r1("mhalf")])
        ld("dma_start", dict(out=esink[:], in_=sinks.rearrange("l h -> (l h)").partition_broadcast(128)), [r1("esink")])
        S.op("act", "activation", dict(out=esink[:], in_=esink[:], func=AF.Exp), reads=[r1("esink")], writes=[r1("esink")])
        ld("dma_start", dict(out=gA[:].rearrange("p (l c) -> p l c", c=8), in_=norm_attn.rearrange("l (c p) -> p l c", p=128)), [r1("gA")])
        ld("dma_start", dict(out=gG[:].rearrange("p (l c) -> p l c", c=8), in_=norm_gmlp.rearrange("l (c p) -> p l c", p=128)), [r1("gG")])
        S.op("pool", "memset", dict(ap=rbA[:], constant=NEG), writes=[r1("rbA")])
        ld("dma_start", dict(out=rbA[0:32, :], in_=rel_bias[:, :]), [r1("rbA")])
        S.op("dve", "tensor_copy", dict(out=rbB[:], in_=rbA[:]), reads=[r1("rbA")], writes=[r1("rbB")])
        for rel in range(2):
            pool_dma("dma_start", dict(out=ohs[0:33, :], in_=oh_d[:, rel, :]), writes=Rx[0:4])
            for rnd in range(4):
                bank = rnd % 2
                for ii in range(32):
                    i_ = rnd * 32 + ii
                    S.op("pe", "matmul", dict(out=ps[:, bank, ii * 16:(ii + 1) * 16], lhsT=ohs[0:33, i_ * 128:(i_ + 1) * 128],
                                              rhs=rbB[0:33, :], start=True, stop=True),
                         reads=Rx[0:4] + [r1("rbB")], writes=[Rbank[bank]], inc=(ii == 31))
                S.op("dve", "tensor_copy", dict(out=BT[:, rel, :, rnd * 32:(rnd + 1) * 32],
                                                in_=ps[:, bank, :].rearrange("p (i h) -> p h i", h=16)),
                     reads=[Rbank[bank]], writes=[r1("BT")])
        store("dma_start", dict(out=nks[:, :, 0:96, :], in_=ck[:, :, 32:128, :]), reads=())
        store("dma_start", dict(out=nvs[:, :, 0:96, :], in_=cv[:, :, 32:128, :]), reads=())

        LAYER_UNITS = [("win", 2), ("win", 3), ("win", 4), ("win", 0), ("win", 1), ("wout", (0, 0)), ("wout", (0, 1024)),
                       ("win", 7), ("win", 8), ("win", 5), ("win", 9), ("win", 6), ("win", 10), ("wout", (1, 0)), ("wout", (1, 1024))]
        useq = [(k, l, i) for _ in tiles for l in range(L) for (k, i) in LAYER_UNITS]
        ust = {"load": 0, "use": 0, "got": {}}

        pendu = {}

        def issue_load():
            n = ust["load"]
            if n >= len(useq):
                return
            assert pendu.get(n - NSLOT, 0) == 0, "weight slot refilled while deferred readers of its old unit are pending"
            kind, l, idx = useq[n]
            si = n % NSLOT
            slot = wsl[si]
            if kind == "win":
                c0 = idx * 512
                v = slot[:, :].rearrange("p (k c) -> p k c", c=512)
                src = w_in[l, :, c0:c0 + 512].rearrange("(k p) c -> p k c", p=128)
            else:
                half, n0 = idx
                v = slot[:, :].rearrange("p (k c) -> p k c", c=1024)
                src = w_out[l, half * 1024:(half + 1) * 1024, n0:n0 + 1024].rearrange("(k p) c -> p k c", p=128)
            pool_dma("dma_start", dict(out=v, in_=src), writes=[Rw[si]], sem="w0")
            ust["got"][n] = (v, Rw[si])
            ust["load"] = n + 1

        def prefetch():
            while ust["load"] <= ust["use"] + NSLOT - 2 and ust["load"] < len(useq):
                issue_load()

        def next_unit(kind, l, idx, ahead=NSLOT - 1):
            n = ust["use"]
            assert useq[n] == (kind, l, idx), (useq[n], kind, l, idx)
            while ust["load"] <= n:
                issue_load()
            v = ust["got"].pop(n)
            ust["use"] = n + 1
            while ust["load"] <= n + ahead:
                if ust["load"] >= len(useq):
                    break
                issue_load()
            return v

        def chk(n):
            if stop_at is not None and stop_at == n:
                raise _Stop()

        try:
          chk(0)
          for ti, (seq, part, has_s) in enumerate(tiles):
              nb = 5 if has_s else 4
              ntok = 128 * nb
              tgs = [(0, 512)] + ([(512, 128)] if has_s else [])
              first = (part == 0)
              lastp = (part == last_part)
              nxt = tiles[ti + 1] if ti + 1 < len(tiles) else None
              if ti == 0:
                  for tb in range(4):
                      r0 = part * 512 + tb * 128
                      ld("dma_start", dict(out=x[:, tb, :], in_=xp[seq, r0:r0 + 128, :]), [Rx[tb]])
              if has_s:
                  ld("dma_start", dict(out=x[:, 4, :], in_=xs[:, :]), [Rx[4]])

              def hT_reads(t0, n):
                  return [RhT[t] for t in range(t0 // 128, (t0 + n) // 128)]

              xhc = {"i": 0}

              p0tab = {"B": False}
              p0pend = []
              p0a2 = []

              def phase0_A1(tb):
                  S.op("act", "activation", dict(out=junk[:], in_=x[:, tb, :], func=AF.Square, accum_out=st_ss[:, tb:tb + 1]),
                       reads=[Rx[tb]], writes=[r1("junk"), r1("st_ss%d" % tb)])
                  rsqrt_small(st_r[:, tb:tb + 1], st_ss[:, tb:tb + 1], 1, 1.0 / D, [r1("st_ss%d" % tb)], [r1("st_r%d" % tb)])
                  p0a2.append(tb)

              def phase0_A2():
                  tb = p0a2.pop(0)
                  xhc["i"] += 1
                  xi = xhc["i"] % 2
                  xw = [r1("xh%d" % xi)] + ([Rkvo[0], Rkvo[1]] if xi == 1 else [])
                  gt, gtr = (tabB, RD("tabB")) if p0tab["B"] else (tabA[:], [r1("tabA")])
                  S.op("dve", "scalar_tensor_tensor", dict(out=xhs[xi], in0=x[:, tb, :], scalar=st_r[:, tb:tb + 1], in1=gt,
                                                           op0=ALU.mult, op1=ALU.mult),
                       reads=[Rx[tb], r1("st_r%d" % tb)] + gtr, writes=xw)
                  p0pend.append((tb, xi))

              def phase0_B():
                  tb, xi = p0pend.pop(0)
                  for kg in range(4):
                      b = tm_bank()
                      for j in range(4):
                          kc = kg * 4 + j
                          S.op("pe", "transpose", dict(out=psb[:, b, j * 128:(j + 1) * 128], in_=xhs[xi][:, kc * 128:(kc + 1) * 128], identity=ident[:]),
                               reads=[r1("xh%d" % xi), r1("ident")], writes=[Rbank[b]], inc=(j == 3))
                      S.op("act", "activation", dict(out=hT[:, kg * 4:kg * 4 + 4, tb * 128:(tb + 1) * 128],
                                                     in_=psb[:, b, 0:512].rearrange("p (j t) -> p j t", t=128), func=AF.Copy),
                           reads=[Rbank[b]], writes=[RhT[tb]])

              def phase0_block(tb):
                  if p0a2:
                      if len(p0pend) == 2:
                          phase0_B()
                      phase0_A2()
                  phase0_A1(tb)

              def phase0_flush():
                  while p0a2:
                      if len(p0pend) == 2:
                          phase0_B()
                      phase0_A2()
                  while p0pend:
                      phase0_B()

              ystgs = [(sgvg[:, 0:4, :].rearrange("p a n -> p (a n)").bitcast(F32), Rsg[0:4]),
                       (hT[:, :, :].rearrange("p k t -> p (k t)")[:, 0:4096].bitcast(F32), RhT[0:5])]

              def final_block(tb):
                  S.op("act", "activation", dict(out=junk[:], in_=x[:, tb, :], func=AF.Square, accum_out=st_ss[:, tb:tb + 1]),
                       reads=[Rx[tb]], writes=[r1("junk"), r1("st_ss%d" % tb)])
                  rsqrt_small(st_r[:, tb:tb + 1], st_ss[:, tb:tb + 1], 1, 1.0 / D, [r1("st_ss%d" % tb)], [r1("st_r%d" % tb)])
                  ystg, ysr = ystgs[tb % 2]
                  S.op("dve", "scalar_tensor_tensor", dict(out=ystg, in0=x[:, tb, :], scalar=st_r[:, tb:tb + 1], in1=tabA[:],
                                                           op0=ALU.mult, op1=ALU.mult),
                       reads=[Rx[tb], r1("st_r%d" % tb), r1("tabA")], writes=ysr)
                  if tb < 4:
                      r0 = part * 512 + tb * 128
                      store("dma_start", dict(out=y_p[seq, r0:r0 + 128, :], in_=ystg), ysr)
                      if nxt is not None:
                          nseq, npart, _ = nxt
                          nr0 = npart * 512 + tb * 128
                          ld("dma_start", dict(out=x[:, tb, :], in_=xp[nseq, nr0:nr0 + 128, :]), [Rx[tb]])
                  else:
                      store("dma_start", dict(out=y_s[:, :], in_=ystg), ysr)

              for l in range(L):
                  if l == 0:
                      if ti == 0:
                          ld("dma_start", dict(out=tabA[:], in_=norm_in[0:1, :].partition_broadcast(128)), [r1("tabA")])
                      p0tab["B"] = (ti > 0)
                      for tb in range(nb):
                          phase0_block(tb)
                      phase0_flush()
                      p0tab["B"] = False
                  chk(1)
                  W, rW = next_unit("win", l, 2)
                  while p0a2:
                      if len(p0pend) == 2:
                          phase0_B()
                      phase0_A2()
                  kv_pending = [t for (t, _) in p0pend]
                  kv_order = [t for t in range(nb) if t not in kv_pending] + kv_pending
                  for tb in kv_order:
                      if tb in kv_pending and p0pend:
                          phase0_B()
                      special = (tb == 4) or (tb == 3 and lastp)
                      c0, n = (0, 512) if special else (256, 256)
                      b = tm_bank()
                      for kc in range(16):
                          S.op("pe", "matmul", dict(out=ps[:, b, 0:n], lhsT=hT[:, kc, tb * 128:(tb + 1) * 128], rhs=W[:, kc, c0:c0 + n],
                                                    start=(kc == 0), stop=(kc == 15)),
                               reads=[rW, RhT[tb]], writes=[Rbank[b]], inc=(kc == 15))
                      evac_copy(Vb[:, tb, :], ps[:, b, n - 256:n], [Rbank[b]], [RV[tb]])
                      if special:
                          ko = tb % 2
                          S.op("dve", "tensor_copy", dict(out=kvo[ko], in_=ps[:, b, :]), reads=[Rbank[b]], writes=[Rkvo[ko], r1("xh1")])
                          if tb == 4:
                              for s in range(4):
                                  store("dma_start", dict(out=nks[l, s, 96:128, :], in_=kvo[ko][32 * s:32 * s + 32, 0:256]), [Rkvo[ko]])
                                  store("dma_start", dict(out=nvs[l, s, 96:128, :], in_=kvo[ko][32 * s:32 * s + 32, 256:512]), [Rkvo[ko]])
                          else:
                              store("dma_start", dict(out=nkp[l, seq, :, :], in_=kvo[ko][:, 0:256]), [Rkvo[ko]])
                              store("dma_start", dict(out=nvp[l, seq, :, :], in_=kvo[ko][:, 256:512]), [Rkvo[ko]])
                  phase0_flush()
                  for P in range(2):
                      for (t0, n) in tgs:
                          b = pj_bank()
                          for kc in range(16):
                              S.op("pe", "matmul", dict(out=ps[:, b, 0:n], lhsT=W[:, kc, P * 128:(P + 1) * 128], rhs=hT[:, kc, t0:t0 + n],
                                                        start=(kc == 0), stop=(kc == 15)),
                                   reads=[rW] + hT_reads(t0, n), writes=[Rbank[b]], inc=(kc == 15))
                          kbs = list(range(4)) if t0 == 0 else [4]
                          evac_copy(kT[:, P, t0:t0 + n], ps[:, b, 0:n], [Rbank[b]], [RkT[P][k] for k in kbs])
                  chk(12)
                  if has_s:
                      pool_dma("dma_start", dict(out=ckb[:], in_=ck[l].rearrange("s t c -> t s c")), writes=[r1("ckb")])
                      pool_dma("dma_start", dict(out=Vb[:, 5:9, :], in_=cv[l].rearrange("s t c -> t s c")), writes=RV[5:9])
                      for P in range(2):
                          b = tm_bank()
                          for s in range(4):
                              S.op("pe", "transpose", dict(out=psb[:, b, s * 128:(s + 1) * 128], in_=ckb[:, s, P * 128:(P + 1) * 128], identity=ident[:]),
                                   reads=[r1("ckb"), r1("ident")], writes=[Rbank[b]], inc=(s == 3))
                          evac_copy(kT[:, P, 640:1152], psb[:, b, 0:512], [Rbank[b]], RkT[P][5:9])

                  wnW = [r1("junk")]
                  variants = [(wsT, "wsT", False)] + ([(wsTs, "wsTs", True)] if has_s else [])

                  def ws_stage(blockdiag):
                      if not blockdiag:
                          pool_dma("dma_start", dict(out=wn, in_=w_sp[l].rearrange("g i j -> i g j")), writes=wnW)
                      else:
                          S.op("pool", "memset", dict(ap=wn, constant=0.0), writes=wnW)
                          for s_ in range(4):
                              pool_dma("dma_start", dict(out=wn[32 * s_:32 * s_ + 32, :, 32 * s_:32 * s_ + 32],
                                                         in_=w_sp[l, :, 0:32, 0:32].rearrange("g i j -> i g j")), writes=wnW)
                      S.op("pool", "affine_select", dict(out=wn, in_=wn, pattern=[[0, 16], [-1, 128]], compare_op=ALU.is_ge,
                                                         fill=0.0, base=0, channel_multiplier=1), reads=[r1("junk")], writes=wnW)

                  def ws_transposes(dst, dname):
                      for gg in range(4):
                          b = tm_bank()
                          for j in range(4):
                              g_ = gg * 4 + j
                              S.op("pe", "transpose", dict(out=psb[:, b, j * 128:(j + 1) * 128], in_=wn[:, g_, :], identity=ident[:]),
                                   reads=[r1("junk"), r1("ident")], writes=[Rbank[b]], inc=(j == 3))
                          evac_copy(dst[:, gg * 4:gg * 4 + 4, :], psb[:, b, 0:512].rearrange("p (j t) -> p j t", t=128), [Rbank[b]], [r1(dname)])

                  ws_stage(False)
                  chk(2)
                  kz_all = []
                  kz_res_all = []

                  def build_kz():
                      kz_all[:] = [[kTz0[:, 0, :], kTz0[:, 1, :]], kTz1]
                      kz_res_all[:] = [[RD("qT0") + RD("qT1"), RD("qT2") + RD("qT3")],
                                    [[r1("xh0")], [r1("xh1"), Rkvo[0], Rkvo[1]]]]
                      kvalid = list(range(4)) + ([4, 5, 6, 7, 8] if has_s else [])
                      for P in range(2):
                          for par in range(2):
                              own = slice(par * 64, par * 64 + 64)
                              oth = slice((1 - par) * 64, (1 - par) * 64 + 64)
                              dst = kz_all[P][par]
                              S.op("pool", "memset", dict(ap=dst[oth, :], constant=0.0), writes=kz_res_all[P][par])
                              S.op("pool", "tensor_copy", dict(out=dst[own, 0:1152], in_=kT[own, P, :]),
                                   reads=[RkT[P][k] for k in kvalid], writes=kz_res_all[P][par])
                              if not first:
                                  S.op("pool", "tensor_copy", dict(out=dst[own, 1152:1280], in_=carK[own, l, P, :]),
                                       reads=[RcarK[l]], writes=kz_res_all[P][par])

                  for u in range(2):
                      W, rW = next_unit("win", l, 3 + u)
                      if u == 0:
                          build_kz()
                      for tb in range(nb):
                          b = tm_bank()
                          for kc in range(16):
                              S.op("pe", "matmul", dict(out=ps[:, b, :], lhsT=hT[:, kc, tb * 128:(tb + 1) * 128], rhs=W[:, kc, :],
                                                        start=(kc == 0), stop=(kc == 15)),
                                   reads=[rW, RhT[tb]], writes=[Rbank[b]], inc=(kc == 15))
                          S.op("act", "activation", dict(out=sgvg[:, tb, u * 512:(u + 1) * 512], in_=ps[:, b, :], func=AF.Silu),
                               reads=[Rbank[b]], writes=[Rsg[tb]])
                      if u == 0:
                          ws_transposes(wsT, "wsT")
                          if has_s:
                              ws_stage(True)
                      elif has_s:
                          ws_transposes(wsTs, "wsTs")

                  chk(3)
                  def key_blocks(tb):
                      out = []
                      if tb > 0:
                          out.append(("p", tb - 1, 0))
                      elif not first:
                          out.append(("c", None, 0))
                      out.append(("p", tb, 1))
                      return out

                  pend_tr = []

                  def emit_tr(ai, tb, P):
                      for j in range(4):
                          S.op("pe", "transpose", dict(out=psb[:, 7, j * 128:(j + 1) * 128], in_=zam[ai][:, j * 128:(j + 1) * 128], identity=ident[:]),
                               reads=RD("zam%d" % ai) + [r1("ident")], writes=[Rbank[7]], inc=(j == 3))
                      for j in range(4):
                          c = 4 * P + j
                          evac_copy(zT[:, c, tb * 128:(tb + 1) * 128], psb[:, 7, j * 128:(j + 1) * 128], [Rbank[7], r1("gA")], [RzT[c][tb]],
                                    scale=gA[:, l * 8 + c:l * 8 + c + 1])

                  def flush_tr():
                      while pend_tr:
                          ai_, tb_, P_ = pend_tr.pop(0)
                          emit_tr(ai_, tb_, P_)
                          if P_ == 1:
                              for cb2 in range(2):
                                  fresh.append(lambda tb_=tb_, cb2=cb2: opA_group(opA["W"], opA["rW"], tb_, cb2, 0))

                  def q_groups(P):
                      W, rW = next_unit("win", l, P)
                      un = ust["use"] - 1
                      W4 = W.rearrange("p k (a b) -> p k a b", b=64)
                      out = []
                      for g in range(4):
                          for (t0, n) in tgs:
                              pendu[un] = pendu.get(un, 0) + 1

                              def grp(g=g, t0=t0, n=n):
                                  pendu[un] -= 1
                                  b = pj_bank()
                                  for kc in range(16):
                                      S.op("pe", "matmul", dict(out=ps[0:64, b, 0:n], lhsT=W4[:, kc, g, :], rhs=hT[:, kc, t0:t0 + n],
                                                                start=(kc == 0), stop=(kc == 15)),
                                           reads=[rW] + hT_reads(t0, n), writes=[Rbank[b]], inc=False)
                                      S.op("pe", "matmul", dict(out=ps[64:128, b, 0:n], lhsT=W4[:, kc, 4 + g, :], rhs=hT[:, kc, t0:t0 + n],
                                                                start=(kc == 0), stop=(kc == 15), tile_position=(0, 64)),
                                           reads=[rW] + hT_reads(t0, n), writes=[Rbank[b]], inc=(kc == 15))
                                  evac_copy(zT[:, 4 * P + g, t0:t0 + n], ps[:, b, 0:n], [Rbank[b]],
                                            [RzT[4 * P + g][t] for t in range(t0 // 128, (t0 + n) // 128)], scale=0.125)
                              out.append(grp)
                      return out

                  def opA_group(W, rW, tb, cb2, n0):
                      if n0 == 0:
                          pendu[opA["un"]] -= 1
                      b = pj_bank()
                      for c in range(8):
                          S.op("pe", "matmul", dict(out=ps[:, b, :], lhsT=zT[:, c, tb * 128:(tb + 1) * 128],
                                                    rhs=W[:, c, cb2 * 512:(cb2 + 1) * 512], start=(c == 0), stop=(c == 7)),
                               reads=[rW, RzT[c][tb]], writes=[Rbank[b]], inc=(c == 7))
                      cols = slice(n0 + cb2 * 512, n0 + cb2 * 512 + 512)
                      S.op("dve", "scalar_tensor_tensor", dict(out=x[:, tb, cols], in0=ps[:, b, :], scalar=st_ra[:, tb:tb + 1],
                                                               in1=x[:, tb, cols], op0=ALU.mult, op1=ALU.add),
                           reads=[Rbank[b], r1("st_ra%d" % tb), Rx[tb]], writes=[Rx[tb]])

                  fillers = []
                  fresh = []
                  for g_ in q_groups(0):
                      g_()
                  opA = {}

                  for P in range(2):
                      qT = zT[:, 4 * P:4 * P + 4, :]
                      if P == 0:
                          fillers.extend(q_groups(1))
                      else:
                          opA["W"], opA["rW"] = next_unit("wout", l, (0, 0))
                          opA["un"] = ust["use"] - 1
                          pendu[opA["un"]] = 2 * nb

                      def allq_tb(tb):
                          return [RzT[4 * P + g][tb] for g in range(4)]

                      kz_res = kz_res_all[P]
                      kz = kz_all[P]

                      flush_tr()

                      def stage_A(tb, par):
                          groups = []
                          kvh = 2 * P + par
                          rows = slice(par * 64, par * 64 + 64)
                          if tb < 4:
                              for (kind, kb, rel) in key_blocks(tb):
                                  cnts["st"] += 1
                                  sb_ = 2 + cnts["st"] % 2
                                  pi = cnts["st"] % 4
                                  if kind == "p":
                                      kap = kz[par][:, kb * 128:(kb + 1) * 128]
                                  else:
                                      kap = kz[par][:, 1152:1280]
                                  S.op("pe", "matmul", dict(out=ps[:, sb_, :].rearrange("p (g q) -> p g q", q=128), lhsT=kap,
                                                            rhs=qT[:, :, tb * 128:(tb + 1) * 128], start=True, stop=False),
                                       reads=kz_res[par] + allq_tb(tb), writes=[Rbank[sb_]], inc=False)
                                  S.op("pe", "matmul", dict(out=ps[:, sb_, :].rearrange("p (g q) -> p g q", q=128), lhsT=ident[:],
                                                            rhs=BT[:, rel, 4 * kvh:4 * kvh + 4, :], start=False, stop=True),
                                       reads=[r1("ident"), r1("BT")], writes=[Rbank[sb_]])
                                  S.op("act", "activation", dict(out=pt[pi], in_=ps[:, sb_, :], func=AF.Exp),
                                       reads=[Rbank[sb_]], writes=WR("pt%d" % pi))
                                  groups.append((pi, kind, kb))
                          else:
                              cnts["st"] += 1
                              sb_ = 2 + cnts["st"] % 2
                              pi = cnts["st"] % 4
                              for s in range(4):
                                  S.op("pe", "matmul", dict(out=ps[:, sb_, s * 128:(s + 1) * 128].rearrange("p (g q) -> p g q", q=32),
                                                            lhsT=kz[par][:, 640 + 128 * s:640 + 128 * s + 128],
                                                            rhs=qT[:, :, 512 + 32 * s:512 + 32 * s + 32], start=True, stop=False),
                                       reads=kz_res[par] + allq_tb(4), writes=[Rbank[sb_]], inc=False)
                                  S.op("pe", "matmul", dict(out=ps[:, sb_, s * 128:(s + 1) * 128].rearrange("p (g q) -> p g q", q=32),
                                                            lhsT=ident[:], rhs=BT[:, 0, 4 * kvh:4 * kvh + 4, 0:32], start=False, stop=True),
                                       reads=[r1("ident"), r1("BT")], writes=[Rbank[sb_]], inc=(s == 3))
                              S.op("act", "activation", dict(out=pt[pi], in_=ps[:, sb_, :], func=AF.Exp),
                                   reads=[Rbank[sb_]], writes=WR("pt%d" % pi))
                              groups.append((pi, "sc", None))
                              cnts["st"] += 1
                              sb_ = 2 + cnts["st"] % 2
                              pi = cnts["st"] % 4
                              for s in range(4):
                                  S.op("pe", "matmul", dict(out=ps[32 * s:32 * s + 32, sb_, s * 128:(s + 1) * 128].rearrange("p (g q) -> p g q", q=32),
                                                            lhsT=kz[par][:, 512 + 32 * s:512 + 32 * s + 32],
                                                            rhs=qT[:, :, 512 + 32 * s:512 + 32 * s + 32], start=True, stop=False,
                                                            tile_position=(0, 32 * s)),
                                       reads=kz_res[par] + allq_tb(4), writes=[Rbank[sb_]], inc=False)
                                  S.op("pe", "matmul", dict(out=ps[32 * s:32 * s + 32, sb_, s * 128:(s + 1) * 128].rearrange("p (g q) -> p g q", q=32),
                                                            lhsT=ident[0:32, 0:32], rhs=BT[0:32, 1, 4 * kvh:4 * kvh + 4, 0:32], start=False, stop=True,
                                                            tile_position=(0, 32 * s)),
                                       reads=[r1("ident"), r1("BT")], writes=[Rbank[sb_]], inc=(s == 3))
                              for s in range(4):
                                  S.op("act", "activation", dict(out=pt[pi][32 * s:32 * s + 32, s * 128:(s + 1) * 128],
                                                                 in_=ps[32 * s:32 * s + 32, sb_, s * 128:(s + 1) * 128], func=AF.Exp),
                                       reads=[Rbank[sb_]], writes=WR("pt%d" % pi))
                              groups.append((pi, "so", None))
                          return groups

                      obof = {}

                      def stage_B(tb, par, grs):
                          if par == 0:
                              cnts["ob"] += 1
                              obof[tb] = 4 + cnts["ob"] % 2
                          ob = obof[tb]
                          di = ob - 4
                          kvh = 2 * P + par
                          O = ps[:, ob, :].rearrange("p (h d) -> p h d", d=64)
                          Dn = ps[:, 6, 0:8]
                          ng = len(grs)
                          for g in range(4):
                              hh = par * 4 + g
                              if tb < 4:
                                  for gi, (pi, kind, kb) in enumerate(grs):
                                      vap = Vb[:, kb, kvh * 64:(kvh + 1) * 64] if kind == "p" else carV[:, l, kvh * 64:(kvh + 1) * 64]
                                      vres = RV[kb] if kind == "p" else RcarV[l]
                                      S.op("pe", "matmul", dict(out=O[:, hh, :], lhsT=pt[pi][:, g * 128:(g + 1) * 128], rhs=vap,
                                                                start=(gi == 0), stop=(gi == ng - 1)),
                                           reads=RD("pt%d" % pi) + [vres], writes=[Rbank[ob]], inc=False)
                                  for gi, (pi, kind, kb) in enumerate(grs):
                                      S.op("pe", "matmul", dict(out=Dn[:, hh:hh + 1], lhsT=pt[pi][:, g * 128:(g + 1) * 128], rhs=ones[:, 0:1],
                                                                start=(gi == 0), stop=(gi == ng - 1)),
                                           reads=RD("pt%d" % pi) + [r1("ones")], writes=[Rbank[6]], inc=(gi == ng - 1))
                              else:
                                  (pc, _, _), (po, _, _) = grs
                                  for s in range(4):
                                      qs = slice(32 * s, 32 * s + 32)
                                      cs = slice(s * 128 + g * 32, s * 128 + g * 32 + 32)
                                      S.op("pe", "matmul", dict(out=O[qs, hh, :], lhsT=pt[pc][:, cs], rhs=Vb[:, 5 + s, kvh * 64:(kvh + 1) * 64],
                                                                start=True, stop=False, tile_position=(0, 32 * s)),
                                           reads=RD("pt%d" % pc) + [RV[5 + s]], writes=[Rbank[ob]], inc=False)
                                      S.op("pe", "matmul", dict(out=O[qs, hh, :], lhsT=pt[po][qs, cs], rhs=Vb[qs, 4, kvh * 64:(kvh + 1) * 64],
                                                                start=False, stop=True, tile_position=(32 * s, 32 * s)),
                                           reads=RD("pt%d" % po) + [RV[4]], writes=[Rbank[ob]], inc=False)
                                      S.op("pe", "matmul", dict(out=Dn[qs, hh:hh + 1], lhsT=pt[pc][:, cs], rhs=ones[:, 0:1],
                                                                start=True, stop=False, tile_position=(0, 32 * s)),
                                           reads=RD("pt%d" % pc) + [r1("ones")], writes=[Rbank[6]], inc=False)
                                      S.op("pe", "matmul", dict(out=Dn[qs, hh:hh + 1], lhsT=pt[po][qs, cs], rhs=ones[qs, 0:1],
                                                                start=False, stop=True, tile_position=(32 * s, 32 * s)),
                                           reads=RD("pt%d" % po) + [r1("ones")], writes=[Rbank[6]], inc=(s == 3))
                          if par == 0:
                              return
                          flush_tr()
                          cnts["ao"] += 1
                          ai = cnts["ao"] % 2
                          dd = st_d[:, di, :]
                          S.op("dve", "tensor_tensor", dict(out=dd, in0=Dn, in1=esink[:, l * 16 + 8 * P:l * 16 + 8 * P + 8], op=ALU.add),
                               reads=[Rbank[6], r1("esink")], writes=[Rstd[di]])
                          S.op("dve", "reciprocal", dict(out=dd, in_=dd), reads=[Rstd[di]], writes=[Rstd[di]])
                          S.op("dve", "tensor_tensor", dict(out=ao[ai].rearrange("p (h d) -> p h d", d=64), in0=O,
                                                            in1=dd.unsqueeze(2).broadcast_to([128, 8, 64]), op=ALU.mult),
                               reads=[Rbank[ob], Rstd[di]], writes=WR("ao%d" % ai))
                          S.op("act", "activation", dict(out=junk[:, 0:512], in_=ao[ai], func=AF.Square, accum_out=st_ssa[:, P, tb:tb + 1]),
                               reads=RD("ao%d" % ai), writes=[r1("junk"), r1("st_ssa%d_%d" % (P, tb))])
                          if P == 1:
                              S.op("pool", "tensor_tensor", dict(out=st_ra[:, tb:tb + 1], in0=st_ssa[:, 0, tb:tb + 1], in1=st_ssa[:, 1, tb:tb + 1], op=ALU.add),
                                   reads=[r1("st_ssa0_%d" % tb), r1("st_ssa1_%d" % tb)], writes=[r1("st_ra%d" % tb)])
                              rsqrt_small(st_ra[:, tb:tb + 1], st_ra[:, tb:tb + 1], 1, 1.0 / 1024, [r1("st_ra%d" % tb)], [r1("st_ra%d" % tb)])
                          S.op("dve", "tensor_tensor", dict(out=zam[ai], in0=ao[ai], in1=sgvg[:, tb, P * 512:(P + 1) * 512], op=ALU.mult),
                               reads=RD("ao%d" % ai) + [Rsg[tb]], writes=WR("zam%d" % ai))
                          pend_tr.append((ai, tb, P))

                      pend = None
                      nhu = 2 * nb
                      for tb in range(nb):
                          for par in range(2):
                              grs = stage_A(tb, par)
                              if pend is not None:
                                  stage_B(*pend)
                              pend = (tb, par, grs)
                              left = nhu - (2 * tb + par + 1)
                              if fillers and (len(fillers) > 1 + left // 2 or P == 1):
                                  fillers.pop(0)()
                              fillers.extend(fresh)
                              del fresh[:]
                      stage_B(*pend)
                      if P == 0:
                          while fillers:
                              fillers.pop(0)()

                  chk(4)
                  if not lastp:
                      for P in range(2):
                          S.op("pool", "tensor_copy", dict(out=carK[:, l, P, :], in_=kT[:, P, 384:512]), reads=[RkT[P][3]], writes=[RcarK[l]])
                      S.op("pool", "tensor_copy", dict(out=carV[:, l, :], in_=Vb[:, 3, :]), reads=[RV[3]], writes=[RcarV[l]])

                  while fillers:
                      fillers.pop(0)()
                  flush_tr()
                  fillers.extend(fresh)
                  del fresh[:]
                  Wb, rWb = next_unit("wout", l, (0, 1024), ahead=NSLOT - 2)
                  for tb in range(nb):
                      for cb2 in range(2):
                          opA_group(Wb, rWb, tb, cb2, 1024)
                      if tb == 0:
                          while fillers:
                              fillers.pop(0)()
                          prefetch()

                  def out_proj(half, rtab, rres, after_tb=None, early=None):
                      gcount = 0
                      held = []
                      for n0 in (0, 1024):
                          W, rW = next_unit("wout", l, (half, n0))
                          for tb in range(nb):
                              for cb2 in range(2):
                                  cnts["opb"] = cnts.get("opb", 0) + 1
                                  b = (0, 1, 4, 5)[cnts["opb"] % 4]
                                  for c in range(8):
                                      S.op("pe", "matmul", dict(out=ps[:, b, :], lhsT=zT[:, c, tb * 128:(tb + 1) * 128],
                                                                rhs=W[:, c, cb2 * 512:(cb2 + 1) * 512], start=(c == 0), stop=(c == 7)),
                                           reads=[rW, RzT[c][tb]], writes=[Rbank[b]], inc=(c == 7))
                                  cols = slice(n0 + cb2 * 512, n0 + cb2 * 512 + 512)

                                  def evac(b=b, tb=tb, cols=cols):
                                      S.op("dve", "scalar_tensor_tensor", dict(out=x[:, tb, cols], in0=ps[:, b, :], scalar=rtab[:, tb:tb + 1],
                                                                               in1=x[:, tb, cols], op0=ALU.mult, op1=ALU.add),
                                           reads=[Rbank[b], rres, Rx[tb]], writes=[Rx[tb]])
                                  gcount += 1
                                  if early is not None and gcount <= 3:
                                      held.append(evac)
                                      if gcount == 3:
                                          early()
                                          for ev in held:
                                              ev()
                                  else:
                                      evac()
                              if after_tb is not None and n0 == 1024:
                                  after_tb(tb)

                  chk(5)
                  ld("dma_start", dict(out=tabA[:, 0:1024], in_=ln_g[l:l + 1, :].partition_broadcast(128)), [r1("tabA")])
                  ld("dma_start", dict(out=tabA[:, 1024:2048], in_=ln_b[l:l + 1, :].partition_broadcast(128)), [r1("tabA")])
                  for par in range(2):
                      ld("dma_start", dict(out=bsT[par * 64:(par + 1) * 64, :, :],
                                           in_=b_sp[l].rearrange("(c par) i -> par c i", par=2)[par].partition_broadcast(64)), [r1("bsT")])
                  W0, rW0 = next_unit("win", l, 7)
                  W1, rW1 = next_unit("win", l, 8, ahead=NSLOT - 2)
                  for tb in range(nb):
                      pb = 6 if tb % 2 == 0 else 4
                      gvb, gvR, gvW = (gv, [r1("junk")], [r1("junk")]) if tb % 2 == 0 else (gv2, RD("gvB"), WR("gvB"))
                      bi = tb % 2
                      for u, (W, rW) in enumerate(((W0, rW0), (W1, rW1))):
                          for kc in range(16):
                              S.op("pe", "matmul", dict(out=ps[:, pb + u, :], lhsT=hT[:, kc, tb * 128:(tb + 1) * 128], rhs=W[:, kc, :],
                                                        start=(kc == 0), stop=(kc == 15)),
                                   reads=[rW, RhT[tb]], writes=[Rbank[pb + u]], inc=(kc == 15))
                      if tb == nb - 1:
                          ust["use"] += 0
                          while ust["load"] <= ust["use"] + NSLOT - 2 + 1 and ust["load"] < len(useq):
                              issue_load()
                      S.op("act", "activation", dict(out=gvb, in_=ps[:, pb:pb + 2, :].rearrange("p a n -> p (a n)"), func=AF.Gelu),
                           reads=[Rbank[pb], Rbank[pb + 1]], writes=gvW)
                      for hh in range(2):
                          S.op("dve", "bn_stats", dict(out=st_bn[:, bi, hh, :], in_=gvb[:, hh * 512:(hh + 1) * 512]),
                               reads=gvR, writes=[r1("st_bn%d" % bi)])
                      S.op("dve", "bn_aggr", dict(out=st_mv[:, bi, :], in_=st_bn[:, bi, :, :].rearrange("p a s -> p (a s)")),
                           reads=[r1("st_bn%d" % bi)], writes=[r1("st_mv%d" % bi)])
                      rsqrt_small(st_lr[:, bi:bi + 1], st_mv[:, bi, 1:2], 1, 1.0, [r1("st_mv%d" % bi)], [r1("st_lr%d" % bi)])
                      S.op("dve", "tensor_scalar", dict(out=gvb, in0=gvb, scalar1=st_mv[:, bi, 0:1], scalar2=st_lr[:, bi:bi + 1],
                                                        op0=ALU.subtract, op1=ALU.mult),
                           reads=gvR + [r1("st_mv%d" % bi), r1("st_lr%d" % bi)], writes=gvW)
                      S.op("dve", "tensor_tensor", dict(out=gvb, in0=gvb, in1=tabA[:, 0:1024], op=ALU.mult),
                           reads=gvR + [r1("tabA")], writes=gvW)
                      if tb == 4:
                          S.op("dve", "tensor_tensor", dict(out=gvb, in0=gvb, in1=tabA[:, 1024:2048], op=ALU.add),
                               reads=gvR + [r1("tabA")], writes=gvW)
                          store("dma_start", dict(out=nvg[l, :, :], in_=gvb), gvR)
                          S.op("act", "activation", dict(out=sgvg[:, tb, :], in_=gvb, func=AF.Copy), reads=gvR, writes=[Rsg[tb]])
                      else:
                          S.op("dve", "tensor_tensor", dict(out=sgvg[:, tb, :], in0=gvb, in1=tabA[:, 1024:2048], op=ALU.add),
                               reads=gvR + [r1("tabA")], writes=[Rsg[tb]])

                  chk(6)
                  pend_rb = []

                  def emit_rb(gi_, c):
                      for tb in range(nb):
                          S.op("pe", "matmul", dict(out=ps[:, 2, c * 8 + tb:c * 8 + tb + 1], lhsT=sqb[gi_][:, tb * 128:(tb + 1) * 128],
                                                    rhs=ones[:, 0:1], start=True, stop=True),
                               reads=RD("sqb%d" % gi_) + [r1("ones")], writes=[Rbank[2]], inc=(tb == nb - 1))
                  for half in range(2):
                      WU, rWU = next_unit("win", l, 5 + half)
                      for c4 in range(4):
                          for (t0, n) in tgs:
                              b = pj_bank()
                              for kc in range(16):
                                  S.op("pe", "matmul", dict(out=ps[:, b, 0:n], lhsT=WU[:, kc, c4 * 128:(c4 + 1) * 128], rhs=hT[:, kc, t0:t0 + n],
                                                            start=(kc == 0), stop=(kc == 15)),
                                       reads=[rWU] + hT_reads(t0, n), writes=[Rbank[b]], inc=(kc == 15))
                              S.op("act", "activation", dict(out=gu[:, c4, t0:t0 + n], in_=ps[:, b, 0:n], func=AF.Gelu),
                                   reads=[Rbank[b]], writes=WR("qT%d" % c4))
                      WG, rWG = next_unit("win", l, 9 + half)
                      for c4 in range(4):
                          c = half * 4 + c4
                          gi_ = c % 2
                          for (t0, n) in tgs:
                              b = pj_bank()
                              for kc in range(16):
                                  S.op("pe", "matmul", dict(out=ps[:, b, 0:n], lhsT=WG[:, kc, c4 * 128:(c4 + 1) * 128], rhs=hT[:, kc, t0:t0 + n],
                                                            start=(kc == 0), stop=(kc == 15)),
                                       reads=[rWG] + hT_reads(t0, n), writes=[Rbank[b]], inc=(kc == 15))
                              S.op("act", "activation", dict(out=sgb[gi_][:, t0:t0 + n], in_=ps[:, b, 0:n], func=AF.Silu),
                                   reads=[Rbank[b]], writes=WR("sgb%d" % gi_))
                          for tb in range(nb):
                              mb, col0 = (4, tb * 128) if tb < 4 else (5, 0)
                              wt_ = wsT if tb < 4 else wsTs
                              wres = r1("wsT") if tb < 4 else r1("wsTs")
                              for par in range(2):
                                  gidx = 2 * c + par
                                  S.op("pe", "matmul", dict(out=ps[par * 64:(par + 1) * 64, mb, col0:col0 + 128],
                                                            lhsT=sgvg[:, tb, gidx * 64:(gidx + 1) * 64], rhs=wt_[:, gidx, :],
                                                            start=True, stop=True, tile_position=(0, par * 64)),
                                       reads=[Rsg[tb], wres], writes=[Rbank[mb]], inc=(par == 1))
                          while pend_rb:
                              emit_rb(*pend_rb.pop(0))
                          S.op("dve", "tensor_tensor", dict(out=go[gi_][:, 0:512].rearrange("p (t i) -> p t i", i=128),
                                                            in0=ps[:, 4, :].rearrange("p (t i) -> p t i", i=128),
                                                            in1=bsT[:, c, :].unsqueeze(1).broadcast_to([128, 4, 128]), op=ALU.add),
                               reads=[Rbank[4], r1("bsT")], writes=WR("go%d" % gi_))
                          if has_s:
                              S.op("dve", "tensor_tensor", dict(out=go[gi_][:, 512:640].rearrange("p (s i) -> p s i", i=32),
                                                                in0=ps[:, 5, 0:128].rearrange("p (s i) -> p s i", i=32),
                                                                in1=bsT[:, c, 0:32].unsqueeze(1).broadcast_to([128, 4, 32]), op=ALU.add),
                                   reads=[Rbank[5], r1("bsT")], writes=WR("go%d" % gi_))
                          S.op("dve", "tensor_tensor", dict(out=go[gi_][:, 0:ntok], in0=go[gi_][:, 0:ntok], in1=gu[:, c4, 0:ntok], op=ALU.mult),
                               reads=RD("go%d" % gi_) + RD("qT%d" % c4), writes=WR("go%d" % gi_))
                          S.op("act", "activation", dict(out=sqb[gi_][:, 0:ntok], in_=go[gi_][:, 0:ntok], func=AF.Square),
                               reads=RD("go%d" % gi_), writes=WR("sqb%d" % gi_))
                          pend_rb.append((gi_, c))
                          S.op("dve", "scalar_tensor_tensor", dict(out=zT[:, c, 0:ntok], in0=go[gi_][:, 0:ntok], scalar=gG[:, l * 8 + c:l * 8 + c + 1],
                                                                   in1=sgb[gi_][:, 0:ntok], op0=ALU.mult, op1=ALU.mult),
                               reads=RD("go%d" % gi_) + RD("sgb%d" % gi_) + [r1("gG")], writes=[RzT[c][t] for t in range(nb)])
                  chk(7)

                  def finish_rb():
                      while pend_rb:
                          emit_rb(*pend_rb.pop(0))
                      S.op("dve", "tensor_reduce", dict(out=st_rb[:, 0:nb], in_=ps[:, 2, 0:64].rearrange("p (c t) -> p t c", t=8)[:, 0:nb, :],
                                                        axis=AX.X, op=ALU.add),
                           reads=[Rbank[2]], writes=[r1("st_rb")])
                      rsqrt_small(st_rb[:, 0:nb], st_rb[:, 0:nb], nb, 1.0 / 1024, [r1("st_rb")], [r1("st_rb")])

                  if l + 1 < L:
                      ld("dma_start", dict(out=tabA[:], in_=norm_in[l + 1:l + 2, :].partition_broadcast(128)), [r1("tabA")])
                      out_proj(1, st_rb, r1("st_rb"), after_tb=phase0_block, early=finish_rb)
                  else:
                      ld("dma_start", dict(out=tabA[:], in_=norm_final[0:1, :].partition_broadcast(128)), [r1("tabA")])
                      if nxt is not None:
                          ld("dma_start", dict(out=tabB, in_=norm_in[0:1, :].partition_broadcast(128)), WR("tabB"))
                      out_proj(1, st_rb, r1("st_rb"), after_tb=final_block, early=finish_rb)

        except _Stop:
            pass

        S.final_waits("sp")

        with nc.allow_non_contiguous_dma(reason="tiny gain / spatial-weight tables only"), nc.Block() as block:
            @block.tensor
            def _(e):
                S.replay("pe", e, sems)

            @block.scalar
            def _(e):
                S.replay("act", e, sems)

            @block.vector
            def _(e):
                S.replay("dve", e, sems)

            @block.gpsimd
            def _(e):
                S.replay("pool", e, sems)

            @block.sync
            def _(e):
                S.replay("sp", e, sems)
    return nc


_OH = None


def _core_inputs(inp, c, n_seq=2):
    global _OH
    if _OH is None:
        _OH = _bucket_onehot()
    f = lambda a: np.ascontiguousarray(np.asarray(a, dtype=np.float32))
    L = inp["w_in"].shape[0]
    return {
        "xp": f(inp["x_prompt"][n_seq * c:n_seq * (c + 1)]),
        "xs": f(inp["x_sample"][4 * c:4 * (c + 1)]).reshape(128, D),
        "ck": f(inp["cache_k"][:, 4 * c:4 * (c + 1)]).reshape(L, 4, 128, 256),
        "cv": f(inp["cache_v"][:, 4 * c:4 * (c + 1)]).reshape(L, 4, 128, 256),
        "w_in": f(inp["w_in"]), "w_out": f(inp["w_out"]), "norm_in": f(inp["norm_in"]),
        "rel_bias": f(inp["rel_bias"]), "sinks": f(inp["sinks"]),
        "norm_attn": f(inp["norm_attn"]), "norm_gmlp": f(inp["norm_gmlp"]),
        "ln_g": f(inp["ln_v_g"]), "ln_b": f(inp["ln_v_b"]),
        "w_sp": f(inp["w_spatial"]), "b_sp": f(inp["b_spatial"]),
        "norm_final": f(inp["norm_final"]).reshape(1, D), "oh": _OH,
    }


def kernel(x_prompt, x_sample, cache_k, cache_v, w_in, w_out, norm_in, rel_bias, sinks,
           norm_attn, norm_gmlp, ln_v_g, ln_v_b, w_spatial, b_spatial, norm_final):
    inp = dict(x_prompt=x_prompt, x_sample=x_sample, cache_k=cache_k, cache_v=cache_v, w_in=w_in, w_out=w_out,
               norm_in=norm_in, rel_bias=rel_bias, sinks=sinks, norm_attn=norm_attn, norm_gmlp=norm_gmlp,
               ln_v_g=ln_v_g, ln_v_b=ln_v_b, w_spatial=w_spatial, b_spatial=b_spatial, norm_final=norm_final)
    inp = {k: np.asarray(v) for k, v in inp.items()}
    L = inp["w_in"].shape[0]
    nc = build_program(n_layers=L)
    shared = None
    in_maps = []
    for c in range(NCORES):
        m = _core_inputs(inp, c)
        if shared is None:
            shared = {k: m[k] for k in ("w_in", "w_out", "norm_in", "rel_bias", "sinks", "norm_attn", "norm_gmlp",
                                        "ln_g", "ln_b", "w_sp", "b_sp", "norm_final", "oh")}
        else:
            m.update(shared)
        in_maps.append(m)
    res = run_bass_kernel_spmd(nc, in_maps, core_ids=list(range(NCORES)))
    rs = res.results
    cat = lambda key, axis: np.concatenate([np.asarray(r[key]) for r in rs], axis=axis)
    y_prompt = cat("y_p", 0)
    y_sample = cat("y_s", 0).reshape(32, 32, D)
    nkp = cat("nkp", 1).reshape(L, 16, 128, 4, 64)
    nvp = cat("nvp", 1).reshape(L, 16, 128, 4, 64)
    nks = cat("nks", 1).reshape(L, 32, 128, 4, 64)
    nvs = cat("nvs", 1).reshape(L, 32, 128, 4, 64)
    nvg = np.stack([np.asarray(r["nvg"]) for r in rs], 1).reshape(L, 32, 32, 16, 64)
    return (y_prompt.astype(np.float32), y_sample.astype(np.float32), nkp.astype(np.float32), nvp.astype(np.float32),
            nks.astype(np.float32), nvs.astype(np.float32), nvg.astype(np.float32))
```

```python
import contextlib
import numpy as np
import concourse.bass as bass
import concourse.mybir as mybir
from concourse.bass_utils import run_bass_kernel_spmd

F32 = mybir.dt.float32
BF16 = mybir.dt.bfloat16
AF = mybir.ActivationFunctionType
ALU = mybir.AluOpType
AX = mybir.AxisListType

D = 2048
DPROJ = 5632
NCORES = 8
EPS = 1e-6
NEG = -1e30


class Res:
    __slots__ = ("name", "w", "rs", "excl")

    def __init__(self, name, excl=False):
        self.name = name
        self.w = None
        self.rs = []
        self.excl = excl


class Sched:
    ENGS = ("pe", "act", "dve", "pool", "sp")

    def __init__(self, same_engine_sync=True):
        self.q = {e: [] for e in self.ENGS}
        self.cnt = {e: 0 for e in self.ENGS}
        self.seen = {e: {} for e in self.ENGS}
        self.dcnt = {}
        self.same = same_engine_sync

    def _deps(self, reads, writes):
        deps = {}

        def need(d):
            if d is not None:
                k, v = d
                if deps.get(k, 0) < v:
                    deps[k] = v
        for r in reads:
            need(r.w)
        for w in writes:
            need(w.w)
            for x in w.rs:
                need(x)
        return deps

    def _waits(self, eng, deps):
        waits = []
        for k, v in deps.items():
            if k == eng and (eng in ("pe", "sp") or not self.same):
                continue
            if self.seen[eng].get(k, 0) >= v:
                continue
            self.seen[eng][k] = v
            waits.append((k, v))
        return waits

    def _stamp(self, me, reads, writes):
        for r in reads:
            r.rs.append(me)
        for w in writes:
            w.w = me
            w.rs = []

    @staticmethod
    def _excl(reads, writes):
        ex = [r for r in reads if r.excl]
        if ex:
            writes = list(writes) + [r for r in ex if r not in writes]
            reads = [r for r in reads if not r.excl]
        return reads, writes

    def op(self, eng, name, kw, reads=(), writes=(), inc=True):
        reads, writes = self._excl(reads, writes)
        deps = self._deps(reads, writes)
        tick = self.cnt[eng] + 1
        if inc:
            self.cnt[eng] = tick
        waits = self._waits(eng, deps)
        self.q[eng].append((waits, (name, kw), (eng, 1) if inc else None))
        self._stamp((eng, tick), reads, writes)

    def dma(self, qeng, semkey, name, kw, reads=(), writes=()):
        deps = self._deps(reads, writes)
        prev = self.dcnt.get(semkey, 0)
        if prev and deps.get(semkey, 0) < prev:
            deps[semkey] = prev
        val = prev + 16
        self.dcnt[semkey] = val
        waits = self._waits(qeng, deps)
        self.q[qeng].append((waits, (name, kw), (semkey, 16)))
        self._stamp((semkey, val), reads, writes)

    def final_waits(self, eng):
        self.q[eng].append(([(k, v) for k, v in self.dcnt.items()], None, None))

    def replay(self, eng, e, sems):
        for waits, fn, inc in self.q[eng]:
            for k, v in waits:
                e.wait_ge(sems[k], v)
            if fn is None:
                continue
            ins = getattr(e, fn[0])(**fn[1])
            if inc is not None:
                ins.then_inc(sems[inc[0]], inc[1])


def _bucket_onehot():
    import math
    import jax
    import jax.numpy as jnp
    cpu = jax.devices("cpu")[0]
    with jax.default_device(cpu):
        jp = np.arange(128)[:, None]
        ip = np.arange(128)[None, :]
        oh = np.zeros((33, 2, 128, 128), np.float32)
        for rel in range(2):
            delta = jnp.asarray((rel - 1) * 128 + jp - ip, dtype=jnp.int32)
            nb = 16
            max_exact = 8
            base = jnp.where(delta > 0, nb, 0)
            n = jnp.abs(delta)
            nf = jnp.maximum(n, 1).astype(jnp.float32)
            large = max_exact + (jnp.log(nf / max_exact) / math.log(128 / max_exact)
                                 * (nb - max_exact)).astype(jnp.int32)
            large = jnp.minimum(large, nb - 1)
            bucket = np.asarray(base + jnp.where(n < max_exact, n, large))
            ck = (np.arange(128) // 64)[:, None]
            cq = (np.arange(128) // 64)[None, :]
            masked = (ck == 0) & (cq == 1) if rel == 0 else (ck == 1) & (cq == 0)
            for b in range(32):
                oh[b, rel] = (bucket == b).T.astype(np.float32)
            oh[32, rel] = masked.T.astype(np.float32)
    return oh.reshape(33, 2, 128 * 128)


class _Stop(Exception):
    pass


def build_program(n_layers=4, tiles=None, seq_len=2048, n_seq=2, same_engine_sync=True, stop_at=None):
    if tiles is None:
        tiles = [(s, p, (s == 0 and p == 0)) for s in range(n_seq) for p in range(seq_len // 512)]
    last_part = seq_len // 512 - 1
    L = n_layers
    nc = bass.Bass("TRN2", target_bir_lowering=False)

    def din(name, shape):
        return nc.dram_tensor(name, list(shape), F32, kind="ExternalInput").ap()

    def dout(name, shape):
        return nc.dram_tensor(name, list(shape), F32, kind="ExternalOutput").ap()

    xp = din("xp", [n_seq, seq_len, D])
    xs = din("xs", [128, D])
    ck = din("ck", [L, 4, 128, 256])
    cv = din("cv", [L, 4, 128, 256])
    w_in = din("w_in", [L, D, DPROJ])
    w_out = din("w_out", [L, D, D])
    norm_in = din("norm_in", [L, D])
    rel_bias = din("rel_bias", [32, 16])
    sinks = din("sinks", [L, 16])
    norm_attn = din("norm_attn", [L, 1024])
    norm_gmlp = din("norm_gmlp", [L, 1024])
    ln_g = din("ln_g", [L, 1024])
    ln_b = din("ln_b", [L, 1024])
    w_sp = din("w_sp", [L, 16, 128, 128])
    b_sp = din("b_sp", [L, 16, 128])
    norm_final = din("norm_final", [1, D])
    oh_d = din("oh", [33, 2, 16384])

    y_p = dout("y_p", [n_seq, seq_len, D])
    y_s = dout("y_s", [128, D])
    nkp = dout("nkp", [L, n_seq, 128, 256])
    nvp = dout("nvp", [L, n_seq, 128, 256])
    nks = dout("nks", [L, 4, 128, 256])
    nvs = dout("nvs", [L, 4, 128, 256])
    nvg = dout("nvg", [L, 128, 1024])

    S = Sched(same_engine_sync)
    NT = 640
    NSLOT = 3

    with contextlib.ExitStack() as es:
        def sb(name, shape, dt):
            return es.enter_context(nc.sbuf_tensor(name, list(shape), dt))

        x = sb("x", [128, 5, D], F32)
        hT = sb("hT", [128, 16, NT], BF16)
        zT = sb("zT", [128, 8, NT], BF16)
        kT = sb("kT", [128, 2, 1152], BF16)
        carK = sb("carK", [128, L, 2, 128], BF16)
        Vb = sb("Vb", [128, 9, 256], BF16)
        carV = sb("carV", [128, L, 256], BF16)
        BT = sb("BT", [128, 2, 16, 128], BF16)
        sgvg = sb("sgvg", [128, 5, 1024], BF16)
        tabA = sb("tabA", [128, 2048], F32)
        wsT = sb("wsT", [128, 16, 128], BF16)
        wsTs = sb("wsTs", [128, 16, 128], BF16)
        bsT = sb("bsT", [128, 8, 128], F32)
        wsl = [sb("wsl%d" % i, [128, 8192], BF16) for i in range(NSLOT)]
        xh = sb("xh", [128, D], BF16)
        junk = sb("junk", [128, 2048], BF16)
        kvo_t = sb("kvo", [128, 2, 512], F32)
        kvo = [kvo_t[:, 0, :], kvo_t[:, 1, :]]
        xhs = [xh[:, :], kvo_t[:, :, :].rearrange("p a n -> p (a n)").bitcast(BF16)]
        kTz1 = [xhs[0][:, 0:1280], xhs[1][:, 0:1280]]
        scr = sb("scr", [128, 7680], BF16)
        ckb = sb("ckb", [128, 4, 256], BF16)
        ident = sb("ident", [128, 128], BF16)
        identf = sb("identf", [128, 128], F32)
        ones = sb("ones", [128, 128], BF16)
        mhalf = sb("mhalf", [128, 16], F32)
        esink = sb("esink", [128, L * 16], F32)
        gA = sb("gA", [128, L * 8], F32)
        gG = sb("gG", [128, L * 8], F32)
        rbA = sb("rbA", [33, 16], F32)
        rbB = sb("rbB", [33, 16], BF16)
        st_ss = sb("st_ss", [128, 8], F32)
        st_r = sb("st_r", [128, 8], F32)
        st_ssa = sb("st_ssa", [128, 2, 8], F32)
        st_ra = sb("st_ra", [128, 8], F32)
        st_rb = sb("st_rb", [128, 8], F32)
        st_d = sb("st_d", [128, 2, 8], F32)
        st_bn = sb("st_bn", [128, 2, 2, 6], F32)
        st_mv = sb("st_mv", [128, 2, 2], F32)
        st_lr = sb("st_lr", [128, 2], F32)

        ps = es.enter_context(nc.psum_tensor("ps", [128, 8, 512], F32))
        psb = ps[:, :, :].bitcast(BF16)
        gv = junk[:, :].bitcast(F32)
        ohs = x[:, 0:4, :].bitcast(BF16).rearrange("p a n -> p (a n)")

        gu = scr[:, 0:2560].rearrange("p (g t) -> p g t", t=NT)
        kTz0 = scr[:, 0:2560].rearrange("p (a n) -> p a n", n=1280)
        pt = [scr[:, 2560 + 512 * i:2560 + 512 * (i + 1)] for i in range(4)]
        ao = [scr[:, 4608 + 1024 * i:4608 + 1024 * (i + 1)].bitcast(F32) for i in range(2)]
        zam = [scr[:, 6656 + 512 * i:6656 + 512 * (i + 1)] for i in range(2)]
        sgb = [scr[:, 2560 + 640 * i:2560 + 640 * (i + 1)] for i in range(2)]
        go = [scr[:, 3840 + 1280 * i:3840 + 1280 * (i + 1)].bitcast(F32) for i in range(2)]
        sqb = [scr[:, 6400 + 640 * i:6400 + 640 * (i + 1)] for i in range(2)]
        wn = junk[:, :].rearrange("p (g j) -> p g j", j=128)
        gv2 = scr[:, 3840:5888].bitcast(F32)
        tabB = scr[:, 0:4096].bitcast(F32)

        semnames = ["pe", "act", "dve", "pool"] + ["w%d" % i for i in range(NSLOT)] + ["ld%d" % i for i in range(8)] + \
                   ["st%d" % i for i in range(8)] + ["pl%d" % i for i in range(4)]
        sems = {k: es.enter_context(nc.semaphore("s_" + k)) for k in semnames}
        rr = {"ld": 0, "st": 0, "pl": 0}

        def semof(kind):
            n = {"ld": 8, "st": 8, "pl": 4}[kind]
            i = rr[kind]
            rr[kind] = (i + 1) % n
            return "%s%d" % (kind, i)

        R = {}

        def r1(name):
            if name not in R:
                R[name] = Res(name)
            return R[name]
        Rx = [r1("x%d" % i) for i in range(5)]
        RhT = [r1("hT%d" % i) for i in range(5)]
        RzT = [[r1("zT%d_%d" % (c, t)) for t in range(5)] for c in range(8)]
        RkT = [[r1("kT%d_%d" % (p, k)) for k in range(9)] for p in range(2)]
        RcarK = [r1("carK%d" % l) for l in range(L)]
        RV = [r1("V%d" % k) for k in range(9)]
        RcarV = [r1("carV%d" % l) for l in range(L)]
        Rbank = [r1("bank%d" % i) for i in range(8)]
        for rb_ in Rbank:
            rb_.excl = True
        Rsg = [r1("sgvg%d" % t) for t in range(5)]
        Rw = [r1("wsl%d" % i) for i in range(NSLOT)]
        Rkvo = [r1("kvo0"), r1("kvo1")]
        Rstd = [r1("std0"), r1("std1")]
        span = {}
        for g in range(4):
            span["qT%d" % g] = (640 * g, 640 * (g + 1))
        for i in range(4):
            span["pt%d" % i] = (2560 + 512 * i, 2560 + 512 * (i + 1))
        for i in range(2):
            span["ao%d" % i] = (4608 + 1024 * i, 4608 + 1024 * (i + 1))
            span["zam%d" % i] = (6656 + 512 * i, 6656 + 512 * (i + 1))
            span["sgb%d" % i] = (2560 + 640 * i, 2560 + 640 * (i + 1))
            span["go%d" % i] = (3840 + 1280 * i, 3840 + 1280 * (i + 1))
            span["sqb%d" % i] = (6400 + 640 * i, 6400 + 640 * (i + 1))

        span["gvB"] = (3840, 5888)
        span["tabB"] = (0, 4096)

        for (a_, b_) in span.values():
            assert a_ % 128 == 0 and b_ % 128 == 0
        segs = [r1("scrseg%d" % i) for i in range(7680 // 128)]

        def WR(name):
            a, b = span[name]
            return segs[a // 128:b // 128]

        RD = WR

        def ld(name, kw, writes, reads=()):
            S.dma("sp", semof("ld"), name, kw, reads=reads, writes=writes)

        def store(name, kw, reads):
            S.dma("sp", semof("st"), name, kw, reads=reads, writes=())

        def pool_dma(name, kw, writes, reads=(), sem=None):
            S.dma("pool", sem or semof("pl"), name, kw, reads=reads, writes=writes)

        cnts = {"pj": 0, "tm": 0, "ev": 0, "st": 0, "ob": 0, "ao": 0}

        def pj_bank():
            cnts["pj"] += 1
            return cnts["pj"] % 2

        def tm_bank():
            cnts["tm"] += 1
            return 6 + cnts["tm"] % 2

        def evac_copy(out, in_, reads, writes, scale=None):
            cnts["ev"] += 1
            if cnts["ev"] % 2 == 0:
                kw = dict(out=out, in_=in_, func=AF.Copy)
                if scale is not None:
                    kw["scale"] = scale
                S.op("act", "activation", kw, reads=reads, writes=writes)
            else:
                if scale is None:
                    S.op("dve", "tensor_copy", dict(out=out, in_=in_), reads=reads, writes=writes)
                else:
                    S.op("dve", "tensor_scalar", dict(out=out, in0=in_, scalar1=scale, scalar2=None, op0=ALU.mult), reads=reads, writes=writes)

        def rsqrt_small(dst, src, ncol, scale, reads, writes):
            S.op("pool", "tensor_scalar", dict(out=dst, in0=src, scalar1=scale, scalar2=EPS, op0=ALU.mult, op1=ALU.add),
                 reads=reads, writes=writes)
            S.op("pool", "tensor_tensor", dict(out=dst, in0=dst, in1=mhalf[:, 0:ncol], op=ALU.pow),
                 reads=list(writes) + [r1("mhalf")], writes=writes)

        S.op("pool", "memset", dict(ap=identf[:], constant=0.0), writes=[r1("identf")])
        S.op("pool", "affine_select", dict(out=identf[:], in_=identf[:], pattern=[[-1, 128]], compare_op=ALU.not_equal,
                                           fill=1.0, base=0, channel_multiplier=1), reads=[r1("identf")], writes=[r1("identf")])
        S.op("dve", "tensor_copy", dict(out=ident[:], in_=identf[:]), reads=[r1("identf")], writes=[r1("ident")])
        S.op("pool", "memset", dict(ap=ones[:], constant=1.0), writes=[r1("ones")])
        S.op("pool", "memset", dict(ap=mhalf[:], constant=-0.5), writes=[r1("mhalf")])
        ld("dma_start", dict(out=esink[:], in_=sinks.rearrange("l h -> (l h)").partition_broadcast(128)), [r1("esink")])
        S.op("act", "activation", dict(out=esink[:], in_=esink[:], func=AF.Exp), reads=[r1("esink")], writes=[r1("esink")])
        ld("dma_start", dict(out=gA[:].rearrange("p (l c) -> p l c", c=8), in_=norm_attn.rearrange("l (c p) -> p l c", p=128)), [r1("gA")])
        ld("dma_start", dict(out=gG[:].rearrange("p (l c) -> p l c", c=8), in_=norm_gmlp.rearrange("l (c p) -> p l c", p=128)), [r1("gG")])
        S.op("pool", "memset", dict(ap=rbA[:], constant=NEG), writes=[r1("rbA")])
        ld("dma_start", dict(out=rbA[0:32, :], in_=rel_bias[:, :]), [r1("rbA")])
        S.op("dve", "tensor_copy", dict(out=rbB[:], in_=rbA[:]), reads=[r1("rbA")], writes=[r1("rbB")])
        for rel in range(2):
            pool_dma("dma_start", dict(out=ohs[0:33, :], in_=oh_d[:, rel, :]), writes=Rx[0:4])
            for rnd in range(4):
                bank = rnd % 2
                for ii in range(32):
                    i_ = rnd * 32 + ii
                    S.op("pe", "matmul", dict(out=ps[:, bank, ii * 16:(ii + 1) * 16], lhsT=ohs[0:33, i_ * 128:(i_ + 1) * 128],
                                              rhs=rbB[0:33, :], start=True, stop=True),
                         reads=Rx[0:4] + [r1("rbB")], writes=[Rbank[bank]], inc=(ii == 31))
                S.op("dve", "tensor_copy", dict(out=BT[:, rel, :, rnd * 32:(rnd + 1) * 32],
                                                in_=ps[:, bank, :].rearrange("p (i h) -> p h i", h=16)),
                     reads=[Rbank[bank]], writes=[r1("BT")])
        store("dma_start", dict(out=nks[:, :, 0:96, :], in_=ck[:, :, 32:128, :]), reads=())
        store("dma_start", dict(out=nvs[:, :, 0:96, :], in_=cv[:, :, 32:128, :]), reads=())

        LAYER_UNITS = [("win", 2), ("win", 3), ("win", 4), ("win", 0), ("win", 1), ("wout", (0, 0)), ("wout", (0, 1024)),
                       ("win", 7), ("win", 8), ("win", 5), ("win", 9), ("win", 6), ("win", 10), ("wout", (1, 0)), ("wout", (1, 1024))]
        useq = [(k, l, i) for _ in tiles for l in range(L) for (k, i) in LAYER_UNITS]
        ust = {"load": 0, "use": 0, "got": {}}

        pendu = {}

        def issue_load():
            n = ust["load"]
            if n >= len(useq):
                return
            assert pendu.get(n - NSLOT, 0) == 0, "weight slot refilled while deferred readers of its old unit are pending"
            kind, l, idx = useq[n]
            si = n % NSLOT
            slot = wsl[si]
            if kind == "win":
                c0 = idx * 512
                v = slot[:, :].rearrange("p (k c) -> p k c", c=512)
                src = w_in[l, :, c0:c0 + 512].rearrange("(k p) c -> p k c", p=128)
            else:
                half, n0 = idx
                v = slot[:, :].rearrange("p (k c) -> p k c", c=1024)
                src = w_out[l, half * 1024:(half + 1) * 1024, n0:n0 + 1024].rearrange("(k p) c -> p k c", p=128)
            pool_dma("dma_start", dict(out=v, in_=src), writes=[Rw[si]], sem="w0")
            ust["got"][n] = (v, Rw[si])
            ust["load"] = n + 1

        def prefetch():
            while ust["load"] <= ust["use"] + NSLOT - 2 and ust["load"] < len(useq):
                issue_load()

        def next_unit(kind, l, idx, ahead=NSLOT - 1):
            n = ust["use"]
            assert useq[n] == (kind, l, idx), (useq[n], kind, l, idx)
            while ust["load"] <= n:
                issue_load()
            v = ust["got"].pop(n)
            ust["use"] = n + 1
            while ust["load"] <= n + ahead:
                if ust["load"] >= len(useq):
                    break
                issue_load()
            return v

        def chk(n):
            if stop_at is not None and stop_at == n:
                raise _Stop()

        try:
          chk(0)
          for ti, (seq, part, has_s) in enumerate(tiles):
              nb = 5 if has_s else 4
              ntok = 128 * nb
              tgs = [(0, 512)] + ([(512, 128)] if has_s else [])
              first = (part == 0)
              lastp = (part == last_part)
              nxt = tiles[ti + 1] if ti + 1 < len(tiles) else None
              if ti == 0:
                  for tb in range(4):
                      r0 = part * 512 + tb * 128
                      ld("dma_start", dict(out=x[:, tb, :], in_=xp[seq, r0:r0 + 128, :]), [Rx[tb]])
              if has_s:
                  ld("dma_start", dict(out=x[:, 4, :], in_=xs[:, :]), [Rx[4]])

              def hT_reads(t0, n):
                  return [RhT[t] for t in range(t0 // 128, (t0 + n) // 128)]

              xhc = {"i": 0}

              p0tab = {"B": False}
              p0pend = []
              p0a2 = []

              def phase0_A1(tb):
                  S.op("act", "activation", dict(out=junk[:], in_=x[:, tb, :], func=AF.Square, accum_out=st_ss[:, tb:tb + 1]),
                       reads=[Rx[tb]], writes=[r1("junk"), r1("st_ss%d" % tb)])
                  rsqrt_small(st_r[:, tb:tb + 1], st_ss[:, tb:tb + 1], 1, 1.0 / D, [r1("st_ss%d" % tb)], [r1("st_r%d" % tb)])
                  p0a2.append(tb)

              def phase0_A2():
                  tb = p0a2.pop(0)
                  xhc["i"] += 1
                  xi = xhc["i"] % 2
                  xw = [r1("xh%d" % xi)] + ([Rkvo[0], Rkvo[1]] if xi == 1 else [])
                  gt, gtr = (tabB, RD("tabB")) if p0tab["B"] else (tabA[:], [r1("tabA")])
                  S.op("dve", "scalar_tensor_tensor", dict(out=xhs[xi], in0=x[:, tb, :], scalar=st_r[:, tb:tb + 1], in1=gt,
                                                           op0=ALU.mult, op1=ALU.mult),
                       reads=[Rx[tb], r1("st_r%d" % tb)] + gtr, writes=xw)
                  p0pend.append((tb, xi))

              def phase0_B():
                  tb, xi = p0pend.pop(0)
                  for kg in range(4):
                      b = tm_bank()
                      for j in range(4):
                          kc = kg * 4 + j
                          S.op("pe", "transpose", dict(out=psb[:, b, j * 128:(j + 1) * 128], in_=xhs[xi][:, kc * 128:(kc + 1) * 128], identity=ident[:]),
                               reads=[r1("xh%d" % xi), r1("ident")], writes=[Rbank[b]], inc=(j == 3))
                      S.op("act", "activation", dict(out=hT[:, kg * 4:kg * 4 + 4, tb * 128:(tb + 1) * 128],
                                                     in_=psb[:, b, 0:512].rearrange("p (j t) -> p j t", t=128), func=AF.Copy),
                           reads=[Rbank[b]], writes=[RhT[tb]])

              def phase0_block(tb):
                  if p0a2:
                      if len(p0pend) == 2:
                          phase0_B()
                      phase0_A2()
                  phase0_A1(tb)

              def phase0_flush():
                  while p0a2:
                      if len(p0pend) == 2:
                          phase0_B()
                      phase0_A2()
                  while p0pend:
                      phase0_B()

              ystgs = [(sgvg[:, 0:4, :].rearrange("p a n -> p (a n)").bitcast(F32), Rsg[0:4]),
                       (hT[:, :, :].rearrange("p k t -> p (k t)")[:, 0:4096].bitcast(F32), RhT[0:5])]

              def final_block(tb):
                  S.op("act", "activation", dict(out=junk[:], in_=x[:, tb, :], func=AF.Square, accum_out=st_ss[:, tb:tb + 1]),
                       reads=[Rx[tb]], writes=[r1("junk"), r1("st_ss%d" % tb)])
                  rsqrt_small(st_r[:, tb:tb + 1], st_ss[:, tb:tb + 1], 1, 1.0 / D, [r1("st_ss%d" % tb)], [r1("st_r%d" % tb)])
                  ystg, ysr = ystgs[tb % 2]
                  S.op("dve", "scalar_tensor_tensor", dict(out=ystg, in0=x[:, tb, :], scalar=st_r[:, tb:tb + 1], in1=tabA[:],
                                                           op0=ALU.mult, op1=ALU.mult),
                       reads=[Rx[tb], r1("st_r%d" % tb), r1("tabA")], writes=ysr)
                  if tb < 4:
                      r0 = part * 512 + tb * 128
                      store("dma_start", dict(out=y_p[seq, r0:r0 + 128, :], in_=ystg), ysr)
                      if nxt is not None:
                          nseq, npart, _ = nxt
                          nr0 = npart * 512 + tb * 128
                          ld("dma_start", dict(out=x[:, tb, :], in_=xp[nseq, nr0:nr0 + 128, :]), [Rx[tb]])
                  else:
                      store("dma_start", dict(out=y_s[:, :], in_=ystg), ysr)

              for l in range(L):
                  if l == 0:
                      if ti == 0:
                          ld("dma_start", dict(out=tabA[:], in_=norm_in[0:1, :].partition_broadcast(128)), [r1("tabA")])
                      p0tab["B"] = (ti > 0)
                      for tb in range(nb):
                          phase0_block(tb)
                      phase0_flush()
                      p0tab["B"] = False
                  chk(1)
                  W, rW = next_unit("win", l, 2)
                  while p0a2:
                      if len(p0pend) == 2:
                          phase0_B()
                      phase0_A2()
                  kv_pending = [t for (t, _) in p0pend]
                  kv_order = [t for t in range(nb) if t not in kv_pending] + kv_pending
                  for tb in kv_order:
                      if tb in kv_pending and p0pend:
                          phase0_B()
                      special = (tb == 4) or (tb == 3 and lastp)
                      c0, n = (0, 512) if special else (256, 256)
                      b = tm_bank()
                      for kc in range(16):
                          S.op("pe", "matmul", dict(out=ps[:, b, 0:n], lhsT=hT[:, kc, tb * 128:(tb + 1) * 128], rhs=W[:, kc, c0:c0 + n],
                                                    start=(kc == 0), stop=(kc == 15)),
                               reads=[rW, RhT[tb]], writes=[Rbank[b]], inc=(kc == 15))
                      evac_copy(Vb[:, tb, :], ps[:, b, n - 256:n], [Rbank[b]], [RV[tb]])
                      if special:
                          ko = tb % 2
                          S.op("dve", "tensor_copy", dict(out=kvo[ko], in_=ps[:, b, :]), reads=[Rbank[b]], writes=[Rkvo[ko], r1("xh1")])
                          if tb == 4:
                              for s in range(4):
                                  store("dma_start", dict(out=nks[l, s, 96:128, :], in_=kvo[ko][32 * s:32 * s + 32, 0:256]), [Rkvo[ko]])
                                  store("dma_start", dict(out=nvs[l, s, 96:128, :], in_=kvo[ko][32 * s:32 * s + 32, 256:512]), [Rkvo[ko]])
                          else:
                              store("dma_start", dict(out=nkp[l, seq, :, :], in_=kvo[ko][:, 0:256]), [Rkvo[ko]])
                              store("dma_start", dict(out=nvp[l, seq, :, :], in_=kvo[ko][:, 256:512]), [Rkvo[ko]])
                  phase0_flush()
                  for P in range(2):
                      for (t0, n) in tgs:
                          b = pj_bank()
                          for kc in range(16):
                              S.op("pe", "matmul", dict(out=ps[:, b, 0:n], lhsT=W[:, kc, P * 128:(P + 1) * 128], rhs=hT[:, kc, t0:t0 + n],
                                                        start=(kc == 0), stop=(kc == 15)),
                                   reads=[rW] + hT_reads(t0, n), writes=[Rbank[b]], inc=(kc == 15))
                          kbs = list(range(4)) if t0 == 0 else [4]
                          evac_copy(kT[:, P, t0:t0 + n], ps[:, b, 0:n], [Rbank[b]], [RkT[P][k] for k in kbs])
                  chk(12)
                  if has_s:
                      pool_dma("dma_start", dict(out=ckb[:], in_=ck[l].rearrange("s t c -> t s c")), writes=[r1("ckb")])
                      pool_dma("dma_start", dict(out=Vb[:, 5:9, :], in_=cv[l].rearrange("s t c -> t s c")), writes=RV[5:9])
                      for P in range(2):
                          b = tm_bank()
                          for s in range(4):
                              S.op("pe", "transpose", dict(out=psb[:, b, s * 128:(s + 1) * 128], in_=ckb[:, s, P * 128:(P + 1) * 128], identity=ident[:]),
                                   reads=[r1("ckb"), r1("ident")], writes=[Rbank[b]], inc=(s == 3))
                          evac_copy(kT[:, P, 640:1152], psb[:, b, 0:512], [Rbank[b]], RkT[P][5:9])

                  wnW = [r1("junk")]
                  variants = [(wsT, "wsT", False)] + ([(wsTs, "wsTs", True)] if has_s else [])

                  def ws_stage(blockdiag):
                      if not blockdiag:
                          pool_dma("dma_start", dict(out=wn, in_=w_sp[l].rearrange("g i j -> i g j")), writes=wnW)
                      else:
                          S.op("pool", "memset", dict(ap=wn, constant=0.0), writes=wnW)
                          for s_ in range(4):
                              pool_dma("dma_start", dict(out=wn[32 * s_:32 * s_ + 32, :, 32 * s_:32 * s_ + 32],
                                                         in_=w_sp[l, :, 0:32, 0:32].rearrange("g i j -> i g j")), writes=wnW)
                      S.op("pool", "affine_select", dict(out=wn, in_=wn, pattern=[[0, 16], [-1, 128]], compare_op=ALU.is_ge,
                                                         fill=0.0, base=0, channel_multiplier=1), reads=[r1("junk")], writes=wnW)

                  def ws_transposes(dst, dname):
                      for gg in range(4):
                          b = tm_bank()
                          for j in range(4):
                              g_ = gg * 4 + j
                              S.op("pe", "transpose", dict(out=psb[:, b, j * 128:(j + 1) * 128], in_=wn[:, g_, :], identity=ident[:]),
                                   reads=[r1("junk"), r1("ident")], writes=[Rbank[b]], inc=(j == 3))
                          evac_copy(dst[:, gg * 4:gg * 4 + 4, :], psb[:, b, 0:512].rearrange("p (j t) -> p j t", t=128), [Rbank[b]], [r1(dname)])

                  ws_stage(False)
                  chk(2)
                  kz_all = []
                  kz_res_all = []

                  def build_kz():
                      kz_all[:] = [[kTz0[:, 0, :], kTz0[:, 1, :]], kTz1]
                      kz_res_all[:] = [[RD("qT0") + RD("qT1"), RD("qT2") + RD("qT3")],
                                    [[r1("xh0")], [r1("xh1"), Rkvo[0], Rkvo[1]]]]
                      kvalid = list(range(4)) + ([4, 5, 6, 7, 8] if has_s else [])
                      for P in range(2):
                          for par in range(2):
                              own = slice(par * 64, par * 64 + 64)
                              oth = slice((1 - par) * 64, (1 - par) * 64 + 64)
                              dst = kz_all[P][par]
                              S.op("pool", "memset", dict(ap=dst[oth, :], constant=0.0), writes=kz_res_all[P][par])
                              S.op("pool", "tensor_copy", dict(out=dst[own, 0:1152], in_=kT[own, P, :]),
                                   reads=[RkT[P][k] for k in kvalid], writes=kz_res_all[P][par])
                              if not first:
                                  S.op("pool", "tensor_copy", dict(out=dst[own, 1152:1280], in_=carK[own, l, P, :]),
                                       reads=[RcarK[l]], writes=kz_res_all[P][par])

                  for u in range(2):
                      W, rW = next_unit("win", l, 3 + u)
                      if u == 0:
                          build_kz()
                      for tb in range(nb):
                          b = tm_bank()
                          for kc in range(16):
                              S.op("pe", "matmul", dict(out=ps[:, b, :], lhsT=hT[:, kc, tb * 128:(tb + 1) * 128], rhs=W[:, kc, :],
                                                        start=(kc == 0), stop=(kc == 15)),
                                   reads=[rW, RhT[tb]], writes=[Rbank[b]], inc=(kc == 15))
                          S.op("act", "activation", dict(out=sgvg[:, tb, u * 512:(u + 1) * 512], in_=ps[:, b, :], func=AF.Silu),
                               reads=[Rbank[b]], writes=[Rsg[tb]])
                      if u == 0:
                          ws_transposes(wsT, "wsT")
                          if has_s:
                              ws_stage(True)
                      elif has_s:
                          ws_transposes(wsTs, "wsTs")

                  chk(3)
                  def key_blocks(tb):
                      out = []
                      if tb > 0:
                          out.append(("p", tb - 1, 0))
                      elif not first:
                          out.append(("c", None, 0))
                      out.append(("p", tb, 1))
                      return out

                  pend_tr = []

                  def emit_tr(ai, tb, P):
                      for j in range(4):
                          S.op("pe", "transpose", dict(out=psb[:, 7, j * 128:(j + 1) * 128], in_=zam[ai][:, j * 128:(j + 1) * 128], identity=ident[:]),
                               reads=RD("zam%d" % ai) + [r1("ident")], writes=[Rbank[7]], inc=(j == 3))
                      for j in range(4):
                          c = 4 * P + j
                          evac_copy(zT[:, c, tb * 128:(tb + 1) * 128], psb[:, 7, j * 128:(j + 1) * 128], [Rbank[7], r1("gA")], [RzT[c][tb]],
                                    scale=gA[:, l * 8 + c:l * 8 + c + 1])

                  def flush_tr():
                      while pend_tr:
                          ai_, tb_, P_ = pend_tr.pop(0)
                          emit_tr(ai_, tb_, P_)
                          if P_ == 1:
                              for cb2 in range(2):
                                  fresh.append(lambda tb_=tb_, cb2=cb2: opA_group(opA["W"], opA["rW"], tb_, cb2, 0))

                  def q_groups(P):
                      W, rW = next_unit("win", l, P)
                      un = ust["use"] - 1
                      W4 = W.rearrange("p k (a b) -> p k a b", b=64)
                      out = []
                      for g in range(4):
                          for (t0, n) in tgs:
                              pendu[un] = pendu.get(un, 0) + 1

                              def grp(g=g, t0=t0, n=n):
                                  pendu[un] -= 1
                                  b = pj_bank()
                                  for kc in range(16):
                                      S.op("pe", "matmul", dict(out=ps[0:64, b, 0:n], lhsT=W4[:, kc, g, :], rhs=hT[:, kc, t0:t0 + n],
                                                                start=(kc == 0), stop=(kc == 15)),
                                           reads=[rW] + hT_reads(t0, n), writes=[Rbank[b]], inc=False)
                                      S.op("pe", "matmul", dict(out=ps[64:128, b, 0:n], lhsT=W4[:, kc, 4 + g, :], rhs=hT[:, kc, t0:t0 + n],
                                                                start=(kc == 0), stop=(kc == 15), tile_position=(0, 64)),
                                           reads=[rW] + hT_reads(t0, n), writes=[Rbank[b]], inc=(kc == 15))
                                  evac_copy(zT[:, 4 * P + g, t0:t0 + n], ps[:, b, 0:n], [Rbank[b]],
                                            [RzT[4 * P + g][t] for t in range(t0 // 128, (t0 + n) // 128)], scale=0.125)
                              out.append(grp)
                      return out

                  def opA_group(W, rW, tb, cb2, n0):
                      if n0 == 0:
                          pendu[opA["un"]] -= 1
                      b = pj_bank()
                      for c in range(8):
                          S.op("pe", "matmul", dict(out=ps[:, b, :], lhsT=zT[:, c, tb * 128:(tb + 1) * 128],
                                                    rhs=W[:, c, cb2 * 512:(cb2 + 1) * 512], start=(c == 0), stop=(c == 7)),
                               reads=[rW, RzT[c][tb]], writes=[Rbank[b]], inc=(c == 7))
                      cols = slice(n0 + cb2 * 512, n0 + cb2 * 512 + 512)
                      S.op("dve", "scalar_tensor_tensor", dict(out=x[:, tb, cols], in0=ps[:, b, :], scalar=st_ra[:, tb:tb + 1],
                                                               in1=x[:, tb, cols], op0=ALU.mult, op1=ALU.add),
                           reads=[Rbank[b], r1("st_ra%d" % tb), Rx[tb]], writes=[Rx[tb]])

                  fillers = []
                  fresh = []
                  for g_ in q_groups(0):
                      g_()
                  opA = {}

                  for P in range(2):
                      qT = zT[:, 4 * P:4 * P + 4, :]
                      if P == 0:
                          fillers.extend(q_groups(1))
                      else:
                          opA["W"], opA["rW"] = next_unit("wout", l, (0, 0))
                          opA["un"] = ust["use"] - 1
                          pendu[opA["un"]] = 2 * nb

                      def allq_tb(tb):
                          return [RzT[4 * P + g][tb] for g in range(4)]

                      kz_res = kz_res_all[P]
                      kz = kz_all[P]

                      flush_tr()

                      def stage_A(tb, par):
                          groups = []
                          kvh = 2 * P + par
                          rows = slice(par * 64, par * 64 + 64)
                          if tb < 4:
                              for (kind, kb, rel) in key_blocks(tb):
                                  cnts["st"] += 1
                                  sb_ = 2 + cnts["st"] % 2
                                  pi = cnts["st"] % 4
                                  if kind == "p":
                                      kap = kz[par][:, kb * 128:(kb + 1) * 128]
                                  else:
                                      kap = kz[par][:, 1152:1280]
                                  S.op("pe", "matmul", dict(out=ps[:, sb_, :].rearrange("p (g q) -> p g q", q=128), lhsT=kap,
                                                            rhs=qT[:, :, tb * 128:(tb + 1) * 128], start=True, stop=False),
                                       reads=kz_res[par] + allq_tb(tb), writes=[Rbank[sb_]], inc=False)
                                  S.op("pe", "matmul", dict(out=ps[:, sb_, :].rearrange("p (g q) -> p g q", q=128), lhsT=ident[:],
                                                            rhs=BT[:, rel, 4 * kvh:4 * kvh + 4, :], start=False, stop=True),
                                       reads=[r1("ident"), r1("BT")], writes=[Rbank[sb_]])
                                  S.op("act", "activation", dict(out=pt[pi], in_=ps[:, sb_, :], func=AF.Exp),
                                       reads=[Rbank[sb_]], writes=WR("pt%d" % pi))
                                  groups.append((pi, kind, kb))
                          else:
                              cnts["st"] += 1
                              sb_ = 2 + cnts["st"] % 2
                              pi = cnts["st"] % 4
                              for s in range(4):
                                  S.op("pe", "matmul", dict(out=ps[:, sb_, s * 128:(s + 1) * 128].rearrange("p (g q) -> p g q", q=32),
                                                            lhsT=kz[par][:, 640 + 128 * s:640 + 128 * s + 128],
                                                            rhs=qT[:, :, 512 + 32 * s:512 + 32 * s + 32], start=True, stop=False),
                                       reads=kz_res[par] + allq_tb(4), writes=[Rbank[sb_]], inc=False)
                                  S.op("pe", "matmul", dict(out=ps[:, sb_, s * 128:(s + 1) * 128].rearrange("p (g q) -> p g q", q=32),
                                                            lhsT=ident[:], rhs=BT[:, 0, 4 * kvh:4 * kvh + 4, 0:32], start=False, stop=True),
                                       reads=[r1("ident"), r1("BT")], writes=[Rbank[sb_]], inc=(s == 3))
                              S.op("act", "activation", dict(out=pt[pi], in_=ps[:, sb_, :], func=AF.Exp),
                                   reads=[Rbank[sb_]], writes=WR("pt%d" % pi))
                              groups.append((pi, "sc", None))
                              cnts["st"] += 1
                              sb_ = 2 + cnts["st"] % 2
                              pi = cnts["st"] % 4
                              for s in range(4):
                                  S.op("pe", "matmul", dict(out=ps[32 * s:32 * s + 32, sb_, s * 128:(s + 1) * 128].rearrange("p (g q) -> p g q", q=32),
                                                            lhsT=kz[par][:, 512 + 32 * s:512 + 32 * s + 32],
                                                            rhs=qT[:, :, 512 + 32 * s:512 + 32 * s + 32], start=True, stop=False,
                                                            tile_position=(0, 32 * s)),
                                       reads=kz_res[par] + allq_tb(4), writes=[Rbank[sb_]], inc=False)
                                  S.op("pe", "matmul", dict(out=ps[32 * s:32 * s + 32, sb_, s * 128:(s + 1) * 128].rearrange("p (g q) -> p g q", q=32),
                                                            lhsT=ident[0:32, 0:32], rhs=BT[0:32, 1, 4 * kvh:4 * kvh + 4, 0:32], start=False, stop=True,
                                                            tile_position=(0, 32 * s)),
                                       reads=[r1("ident"), r1("BT")], writes=[Rbank[sb_]], inc=(s == 3))
                              for s in range(4):
                                  S.op("act", "activation", dict(out=pt[pi][32 * s:32 * s + 32, s * 128:(s + 1) * 128],
                                                                 in_=ps[32 * s:32 * s + 32, sb_, s * 128:(s + 1) * 128], func=AF.Exp),
                                       reads=[Rbank[sb_]], writes=WR("pt%d" % pi))
                              groups.append((pi, "so", None))
                          return groups

                      obof = {}

                      def stage_B(tb, par, grs):
                          if par == 0:
                              cnts["ob"] += 1
                              obof[tb] = 4 + cnts["ob"] % 2
                          ob = obof[tb]
                          di = ob - 4
                          kvh = 2 * P + par
                          O = ps[:, ob, :].rearrange("p (h d) -> p h d", d=64)
                          Dn = ps[:, 6, 0:8]
                          ng = len(grs)
                          for g in range(4):
                              hh = par * 4 + g
                              if tb < 4:
                                  for gi, (pi, kind, kb) in enumerate(grs):
                                      vap = Vb[:, kb, kvh * 64:(kvh + 1) * 64] if kind == "p" else carV[:, l, kvh * 64:(kvh + 1) * 64]
                                      vres = RV[kb] if kind == "p" else RcarV[l]
                                      S.op("pe", "matmul", dict(out=O[:, hh, :], lhsT=pt[pi][:, g * 128:(g + 1) * 128], rhs=vap,
                                                                start=(gi == 0), stop=(gi == ng - 1)),
                                           reads=RD("pt%d" % pi) + [vres], writes=[Rbank[ob]], inc=False)
                                  for gi, (pi, kind, kb) in enumerate(grs):
                                      S.op("pe", "matmul", dict(out=Dn[:, hh:hh + 1], lhsT=pt[pi][:, g * 128:(g + 1) * 128], rhs=ones[:, 0:1],
                                                                start=(gi == 0), stop=(gi == ng - 1)),
                                           reads=RD("pt%d" % pi) + [r1("ones")], writes=[Rbank[6]], inc=(gi == ng - 1))
                              else:
                                  (pc, _, _), (po, _, _) = grs
                                  for s in range(4):
                                      qs = slice(32 * s, 32 * s + 32)
                                      cs = slice(s * 128 + g * 32, s * 128 + g * 32 + 32)
                                      S.op("pe", "matmul", dict(out=O[qs, hh, :], lhsT=pt[pc][:, cs], rhs=Vb[:, 5 + s, kvh * 64:(kvh + 1) * 64],
                                                                start=True, stop=False, tile_position=(0, 32 * s)),
                                           reads=RD("pt%d" % pc) + [RV[5 + s]], writes=[Rbank[ob]], inc=False)
                                      S.op("pe", "matmul", dict(out=O[qs, hh, :], lhsT=pt[po][qs, cs], rhs=Vb[qs, 4, kvh * 64:(kvh + 1) * 64],
                                                                start=False, stop=True, tile_position=(32 * s, 32 * s)),
                                           reads=RD("pt%d" % po) + [RV[4]], writes=[Rbank[ob]], inc=False)
                                      S.op("pe", "matmul", dict(out=Dn[qs, hh:hh + 1], lhsT=pt[pc][:, cs], rhs=ones[:, 0:1],
                                                                start=True, stop=False, tile_position=(0, 32 * s)),
                                           reads=RD("pt%d" % pc) + [r1("ones")], writes=[Rbank[6]], inc=False)
                                      S.op("pe", "matmul", dict(out=Dn[qs, hh:hh + 1], lhsT=pt[po][qs, cs], rhs=ones[qs, 0:1],
                                                                start=False, stop=True, tile_position=(32 * s, 32 * s)),
                                           reads=RD("pt%d" % po) + [r1("ones")], writes=[Rbank[6]], inc=(s == 3))
                          if par == 0:
                              return
                          flush_tr()
                          cnts["ao"] += 1
                          ai = cnts["ao"] % 2
                          dd = st_d[:, di, :]
                          S.op("dve", "tensor_tensor", dict(out=dd, in0=Dn, in1=esink[:, l * 16 + 8 * P:l * 16 + 8 * P + 8], op=ALU.add),
                               reads=[Rbank[6], r1("esink")], writes=[Rstd[di]])
                          S.op("dve", "reciprocal", dict(out=dd, in_=dd), reads=[Rstd[di]], writes=[Rstd[di]])
                          S.op("dve", "tensor_tensor", dict(out=ao[ai].rearrange("p (h d) -> p h d", d=64), in0=O,
                                                            in1=dd.unsqueeze(2).broadcast_to([128, 8, 64]), op=ALU.mult),
                               reads=[Rbank[ob], Rstd[di]], writes=WR("ao%d" % ai))
                          S.op("act", "activation", dict(out=junk[:, 0:512], in_=ao[ai], func=AF.Square, accum_out=st_ssa[:, P, tb:tb + 1]),
                               reads=RD("ao%d" % ai), writes=[r1("junk"), r1("st_ssa%d_%d" % (P, tb))])
                          if P == 1:
                              S.op("pool", "tensor_tensor", dict(out=st_ra[:, tb:tb + 1], in0=st_ssa[:, 0, tb:tb + 1], in1=st_ssa[:, 1, tb:tb + 1], op=ALU.add),
                                   reads=[r1("st_ssa0_%d" % tb), r1("st_ssa1_%d" % tb)], writes=[r1("st_ra%d" % tb)])
                              rsqrt_small(st_ra[:, tb:tb + 1], st_ra[:, tb:tb + 1], 1, 1.0 / 1024, [r1("st_ra%d" % tb)], [r1("st_ra%d" % tb)])
                          S.op("dve", "tensor_tensor", dict(out=zam[ai], in0=ao[ai], in1=sgvg[:, tb, P * 512:(P + 1) * 512], op=ALU.mult),
                               reads=RD("ao%d" % ai) + [Rsg[tb]], writes=WR("zam%d" % ai))
                          pend_tr.append((ai, tb, P))

                      pend = None
                      nhu = 2 * nb
                      for tb in range(nb):
                          for par in range(2):
                              grs = stage_A(tb, par)
                              if pend is not None:
                                  stage_B(*pend)
                              pend = (tb, par, grs)
                              left = nhu - (2 * tb + par + 1)
                              if fillers and (len(fillers) > 1 + left // 2 or P == 1):
                                  fillers.pop(0)()
                              fillers.extend(fresh)
                              del fresh[:]
                      stage_B(*pend)
                      if P == 0:
                          while fillers:
                              fillers.pop(0)()

                  chk(4)
                  if not lastp:
                      for P in range(2):
                          S.op("pool", "tensor_copy", dict(out=carK[:, l, P, :], in_=kT[:, P, 384:512]), reads=[RkT[P][3]], writes=[RcarK[l]])
                      S.op("pool", "tensor_copy", dict(out=carV[:, l, :], in_=Vb[:, 3, :]), reads=[RV[3]], writes=[RcarV[l]])

                  while fillers:
                      fillers.pop(0)()
                  flush_tr()
                  fillers.extend(fresh)
                  del fresh[:]
                  Wb, rWb = next_unit("wout", l, (0, 1024), ahead=NSLOT - 2)
                  for tb in range(nb):
                      for cb2 in range(2):
                          opA_group(Wb, rWb, tb, cb2, 1024)
                      if tb == 0:
                          while fillers:
                              fillers.pop(0)()
                          prefetch()

                  def out_proj(half, rtab, rres, after_tb=None, early=None):
                      gcount = 0
                      held = []
                      for n0 in (0, 1024):
                          W, rW = next_unit("wout", l, (half, n0))
                          for tb in range(nb):
                              for cb2 in range(2):
                                  cnts["opb"] = cnts.get("opb", 0) + 1
                                  b = (0, 1, 4, 5)[cnts["opb"] % 4]
                                  for c in range(8):
                                      S.op("pe", "matmul", dict(out=ps[:, b, :], lhsT=zT[:, c, tb * 128:(tb + 1) * 128],
                                                                rhs=W[:, c, cb2 * 512:(cb2 + 1) * 512], start=(c == 0), stop=(c == 7)),
                                           reads=[rW, RzT[c][tb]], writes=[Rbank[b]], inc=(c == 7))
                                  cols = slice(n0 + cb2 * 512, n0 + cb2 * 512 + 512)

                                  def evac(b=b, tb=tb, cols=cols):
                                      S.op("dve", "scalar_tensor_tensor", dict(out=x[:, tb, cols], in0=ps[:, b, :], scalar=rtab[:, tb:tb + 1],
                                                                               in1=x[:, tb, cols], op0=ALU.mult, op1=ALU.add),
                                           reads=[Rbank[b], rres, Rx[tb]], writes=[Rx[tb]])
                                  gcount += 1
                                  if early is not None and gcount <= 4:
                                      held.append(evac)
                                      if gcount == 4:
                                          early()
                                          for ev in held:
                                              ev()
                                  else:
                                      evac()
                              if after_tb is not None and n0 == 1024:
                                  after_tb(tb)

                  chk(5)
                  ld("dma_start", dict(out=tabA[:, 0:1024], in_=ln_g[l:l + 1, :].partition_broadcast(128)), [r1("tabA")])
                  ld("dma_start", dict(out=tabA[:, 1024:2048], in_=ln_b[l:l + 1, :].partition_broadcast(128)), [r1("tabA")])
                  for par in range(2):
                      ld("dma_start", dict(out=bsT[par * 64:(par + 1) * 64, :, :],
                                           in_=b_sp[l].rearrange("(c par) i -> par c i", par=2)[par].partition_broadcast(64)), [r1("bsT")])
                  W0, rW0 = next_unit("win", l, 7)
                  W1, rW1 = next_unit("win", l, 8, ahead=NSLOT - 2)
                  for tb in range(nb):
                      pb = 6 if tb % 2 == 0 else 4
                      gvb, gvR, gvW = (gv, [r1("junk")], [r1("junk")]) if tb % 2 == 0 else (gv2, RD("gvB"), WR("gvB"))
                      bi = tb % 2
                      for u, (W, rW) in enumerate(((W0, rW0), (W1, rW1))):
                          for kc in range(16):
                              S.op("pe", "matmul", dict(out=ps[:, pb + u, :], lhsT=hT[:, kc, tb * 128:(tb + 1) * 128], rhs=W[:, kc, :],
                                                        start=(kc == 0), stop=(kc == 15)),
                                   reads=[rW, RhT[tb]], writes=[Rbank[pb + u]], inc=(kc == 15))
                      if tb == nb - 1:
                          ust["use"] += 0
                          while ust["load"] <= ust["use"] + NSLOT - 2 + 1 and ust["load"] < len(useq):
                              issue_load()
                      S.op("act", "activation", dict(out=gvb, in_=ps[:, pb:pb + 2, :].rearrange("p a n -> p (a n)"), func=AF.Gelu),
                           reads=[Rbank[pb], Rbank[pb + 1]], writes=gvW)
                      for hh in range(2):
                          S.op("dve", "bn_stats", dict(out=st_bn[:, bi, hh, :], in_=gvb[:, hh * 512:(hh + 1) * 512]),
                               reads=gvR, writes=[r1("st_bn%d" % bi)])
                      S.op("dve", "bn_aggr", dict(out=st_mv[:, bi, :], in_=st_bn[:, bi, :, :].rearrange("p a s -> p (a s)")),
                           reads=[r1("st_bn%d" % bi)], writes=[r1("st_mv%d" % bi)])
                      rsqrt_small(st_lr[:, bi:bi + 1], st_mv[:, bi, 1:2], 1, 1.0, [r1("st_mv%d" % bi)], [r1("st_lr%d" % bi)])
                      S.op("dve", "tensor_scalar", dict(out=gvb, in0=gvb, scalar1=st_mv[:, bi, 0:1], scalar2=st_lr[:, bi:bi + 1],
                                                        op0=ALU.subtract, op1=ALU.mult),
                           reads=gvR + [r1("st_mv%d" % bi), r1("st_lr%d" % bi)], writes=gvW)
                      S.op("dve", "tensor_tensor", dict(out=gvb, in0=gvb, in1=tabA[:, 0:1024], op=ALU.mult),
                           reads=gvR + [r1("tabA")], writes=gvW)
                      if tb == 4:
                          S.op("dve", "tensor_tensor", dict(out=gvb, in0=gvb, in1=tabA[:, 1024:2048], op=ALU.add),
                               reads=gvR + [r1("tabA")], writes=gvW)
                          store("dma_start", dict(out=nvg[l, :, :], in_=gvb), gvR)
                          S.op("act", "activation", dict(out=sgvg[:, tb, :], in_=gvb, func=AF.Copy), reads=gvR, writes=[Rsg[tb]])
                      else:
                          S.op("dve", "tensor_tensor", dict(out=sgvg[:, tb, :], in0=gvb, in1=tabA[:, 1024:2048], op=ALU.add),
                               reads=gvR + [r1("tabA")], writes=[Rsg[tb]])

                  chk(6)
                  pend_rb = []

                  def emit_rb(gi_, c):
                      for tb in range(nb):
                          S.op("pe", "matmul", dict(out=ps[:, 2, c * 8 + tb:c * 8 + tb + 1], lhsT=sqb[gi_][:, tb * 128:(tb + 1) * 128],
                                                    rhs=ones[:, 0:1], start=True, stop=True),
                               reads=RD("sqb%d" % gi_) + [r1("ones")], writes=[Rbank[2]], inc=(tb == nb - 1))
                  for half in range(2):
                      WU, rWU = next_unit("win", l, 5 + half)
                      for c4 in range(4):
                          for (t0, n) in tgs:
                              b = pj_bank()
                              for kc in range(16):
                                  S.op("pe", "matmul", dict(out=ps[:, b, 0:n], lhsT=WU[:, kc, c4 * 128:(c4 + 1) * 128], rhs=hT[:, kc, t0:t0 + n],
                                                            start=(kc == 0), stop=(kc == 15)),
                                       reads=[rWU] + hT_reads(t0, n), writes=[Rbank[b]], inc=(kc == 15))
                              S.op("act", "activation", dict(out=gu[:, c4, t0:t0 + n], in_=ps[:, b, 0:n], func=AF.Gelu),
                                   reads=[Rbank[b]], writes=WR("qT%d" % c4))
                      WG, rWG = next_unit("win", l, 9 + half)
                      for c4 in range(4):
                          c = half * 4 + c4
                          gi_ = c % 2
                          for (t0, n) in tgs:
                              b = pj_bank()
                              for kc in range(16):
                                  S.op("pe", "matmul", dict(out=ps[:, b, 0:n], lhsT=WG[:, kc, c4 * 128:(c4 + 1) * 128], rhs=hT[:, kc, t0:t0 + n],
                                                            start=(kc == 0), stop=(kc == 15)),
                                       reads=[rWG] + hT_reads(t0, n), writes=[Rbank[b]], inc=(kc == 15))
                              S.op("act", "activation", dict(out=sgb[gi_][:, t0:t0 + n], in_=ps[:, b, 0:n], func=AF.Silu),
                                   reads=[Rbank[b]], writes=WR("sgb%d" % gi_))
                          for tb in range(nb):
                              mb, col0 = (4, tb * 128) if tb < 4 else (5, 0)
                              wt_ = wsT if tb < 4 else wsTs
                              wres = r1("wsT") if tb < 4 else r1("wsTs")
                              for par in range(2):
                                  gidx = 2 * c + par
                                  S.op("pe", "matmul", dict(out=ps[par * 64:(par + 1) * 64, mb, col0:col0 + 128],
                                                            lhsT=sgvg[:, tb, gidx * 64:(gidx + 1) * 64], rhs=wt_[:, gidx, :],
                                                            start=True, stop=True, tile_position=(0, par * 64)),
                                       reads=[Rsg[tb], wres], writes=[Rbank[mb]], inc=(par == 1))
                          while pend_rb:
                              emit_rb(*pend_rb.pop(0))
                          S.op("dve", "tensor_tensor", dict(out=go[gi_][:, 0:512].rearrange("p (t i) -> p t i", i=128),
                                                            in0=ps[:, 4, :].rearrange("p (t i) -> p t i", i=128),
                                                            in1=bsT[:, c, :].unsqueeze(1).broadcast_to([128, 4, 128]), op=ALU.add),
                               reads=[Rbank[4], r1("bsT")], writes=WR("go%d" % gi_))
                          if has_s:
                              S.op("dve", "tensor_tensor", dict(out=go[gi_][:, 512:640].rearrange("p (s i) -> p s i", i=32),
                                                                in0=ps[:, 5, 0:128].rearrange("p (s i) -> p s i", i=32),
                                                                in1=bsT[:, c, 0:32].unsqueeze(1).broadcast_to([128, 4, 32]), op=ALU.add),
                                   reads=[Rbank[5], r1("bsT")], writes=WR("go%d" % gi_))
                          S.op("dve", "tensor_tensor", dict(out=go[gi_][:, 0:ntok], in0=go[gi_][:, 0:ntok], in1=gu[:, c4, 0:ntok], op=ALU.mult),
                               reads=RD("go%d" % gi_) + RD("qT%d" % c4), writes=WR("go%d" % gi_))
                          S.op("act", "activation", dict(out=sqb[gi_][:, 0:ntok], in_=go[gi_][:, 0:ntok], func=AF.Square),
                               reads=RD("go%d" % gi_), writes=WR("sqb%d" % gi_))
                          pend_rb.append((gi_, c))
                          S.op("dve", "scalar_tensor_tensor", dict(out=zT[:, c, 0:ntok], in0=go[gi_][:, 0:ntok], scalar=gG[:, l * 8 + c:l * 8 + c + 1],
                                                                   in1=sgb[gi_][:, 0:ntok], op0=ALU.mult, op1=ALU.mult),
                               reads=RD("go%d" % gi_) + RD("sgb%d" % gi_) + [r1("gG")], writes=[RzT[c][t] for t in range(nb)])
                  chk(7)

                  def finish_rb():
                      while pend_rb:
                          emit_rb(*pend_rb.pop(0))
                      S.op("dve", "tensor_reduce", dict(out=st_rb[:, 0:nb], in_=ps[:, 2, 0:64].rearrange("p (c t) -> p t c", t=8)[:, 0:nb, :],
                                                        axis=AX.X, op=ALU.add),
                           reads=[Rbank[2]], writes=[r1("st_rb")])
                      rsqrt_small(st_rb[:, 0:nb], st_rb[:, 0:nb], nb, 1.0 / 1024, [r1("st_rb")], [r1("st_rb")])

                  if l + 1 < L:
                      ld("dma_start", dict(out=tabA[:], in_=norm_in[l + 1:l + 2, :].partition_broadcast(128)), [r1("tabA")])
                      out_proj(1, st_rb, r1("st_rb"), after_tb=phase0_block, early=finish_rb)
                  else:
                      ld("dma_start", dict(out=tabA[:], in_=norm_final[0:1, :].partition_broadcast(128)), [r1("tabA")])
                      if nxt is not None:
                          ld("dma_start", dict(out=tabB, in_=norm_in[0:1, :].partition_broadcast(128)), WR("tabB"))
                      out_proj(1, st_rb, r1("st_rb"), after_tb=final_block, early=finish_rb)

        except _Stop:
            pass

        S.final_waits("sp")

        with nc.allow_non_contiguous_dma(reason="tiny gain / spatial-weight tables only"), nc.Block() as block:
            @block.tensor
            def _(e):
                S.replay("pe", e, sems)

            @block.scalar
            def _(e):
                S.replay("act", e, sems)

            @block.vector
            def _(e):
                S.replay("dve", e, sems)

            @block.gpsimd
            def _(e):
                S.replay("pool", e, sems)

            @block.sync
            def _(e):
                S.replay("sp", e, sems)
    return nc


_OH = None


def _core_inputs(inp, c, n_seq=2):
    global _OH
    if _OH is None:
        _OH = _bucket_onehot()
    f = lambda a: np.ascontiguousarray(np.asarray(a, dtype=np.float32))
    L = inp["w_in"].shape[0]
    return {
        "xp": f(inp["x_prompt"][n_seq * c:n_seq * (c + 1)]),
        "xs": f(inp["x_sample"][4 * c:4 * (c + 1)]).reshape(128, D),
        "ck": f(inp["cache_k"][:, 4 * c:4 * (c + 1)]).reshape(L, 4, 128, 256),
        "cv": f(inp["cache_v"][:, 4 * c:4 * (c + 1)]).reshape(L, 4, 128, 256),
        "w_in": f(inp["w_in"]), "w_out": f(inp["w_out"]), "norm_in": f(inp["norm_in"]),
        "rel_bias": f(inp["rel_bias"]), "sinks": f(inp["sinks"]),
        "norm_attn": f(inp["norm_attn"]), "norm_gmlp": f(inp["norm_gmlp"]),
        "ln_g": f(inp["ln_v_g"]), "ln_b": f(inp["ln_v_b"]),
        "w_sp": f(inp["w_spatial"]), "b_sp": f(inp["b_spatial"]),
        "norm_final": f(inp["norm_final"]).reshape(1, D), "oh": _OH,
    }


def kernel(x_prompt, x_sample, cache_k, cache_v, w_in, w_out, norm_in, rel_bias, sinks,
           norm_attn, norm_gmlp, ln_v_g, ln_v_b, w_spatial, b_spatial, norm_final):
    inp = dict(x_prompt=x_prompt, x_sample=x_sample, cache_k=cache_k, cache_v=cache_v, w_in=w_in, w_out=w_out,
               norm_in=norm_in, rel_bias=rel_bias, sinks=sinks, norm_attn=norm_attn, norm_gmlp=norm_gmlp,
               ln_v_g=ln_v_g, ln_v_b=ln_v_b, w_spatial=w_spatial, b_spatial=b_spatial, norm_final=norm_final)
    inp = {k: np.asarray(v) for k, v in inp.items()}
    L = inp["w_in"].shape[0]
    nc = build_program(n_layers=L)
    shared = None
    in_maps = []
    for c in range(NCORES):
        m = _core_inputs(inp, c)
        if shared is None:
            shared = {k: m[k] for k in ("w_in", "w_out", "norm_in", "rel_bias", "sinks", "norm_attn", "norm_gmlp",
                                        "ln_g", "ln_b", "w_sp", "b_sp", "norm_final", "oh")}
        else:
            m.update(shared)
        in_maps.append(m)
    res = run_bass_kernel_spmd(nc, in_maps, core_ids=list(range(NCORES)))
    rs = res.results
    cat = lambda key, axis: np.concatenate([np.asarray(r[key]) for r in rs], axis=axis)
    y_prompt = cat("y_p", 0)
    y_sample = cat("y_s", 0).reshape(32, 32, D)
    nkp = cat("nkp", 1).reshape(L, 16, 128, 4, 64)
    nvp = cat("nvp", 1).reshape(L, 16, 128, 4, 64)
    nks = cat("nks", 1).reshape(L, 32, 128, 4, 64)
    nvs = cat("nvs", 1).reshape(L, 32, 128, 4, 64)
    nvg = np.stack([np.asarray(r["nvg"]) for r in rs], 1).reshape(L, 32, 32, 16, 64)
    return (y_prompt.astype(np.float32), y_sample.astype(np.float32), nkp.astype(np.float32), nvp.astype(np.float32),
            nks.astype(np.float32), nvs.astype(np.float32), nvg.astype(np.float32))
```

```python
import contextlib
import numpy as np
import concourse.bass as bass
import concourse.mybir as mybir
from concourse.bass_utils import run_bass_kernel_spmd

F32 = mybir.dt.float32
BF16 = mybir.dt.bfloat16
AF = mybir.ActivationFunctionType
ALU = mybir.AluOpType
AX = mybir.AxisListType

D = 2048
DPROJ = 5632
NCORES = 8
EPS = 1e-6
NEG = -1e30


class Res:
    __slots__ = ("name", "w", "rs", "excl")

    def __init__(self, name, excl=False):
        self.name = name
        self.w = None
        self.rs = []
        self.excl = excl


class Sched:
    ENGS = ("pe", "act", "dve", "pool", "sp")

    def __init__(self, same_engine_sync=True):
        self.q = {e: [] for e in self.ENGS}
        self.cnt = {e: 0 for e in self.ENGS}
        self.seen = {e: {} for e in self.ENGS}
        self.dcnt = {}
        self.same = same_engine_sync

    def _deps(self, reads, writes):
        deps = {}

        def need(d):
            if d is not None:
                k, v = d
                if deps.get(k, 0) < v:
                    deps[k] = v
        for r in reads:
            need(r.w)
        for w in writes:
            need(w.w)
            for x in w.rs:
                need(x)
        return deps

    def _waits(self, eng, deps):
        waits = []
        for k, v in deps.items():
            if k == eng and (eng in ("pe", "sp") or not self.same):
                continue
            if self.seen[eng].get(k, 0) >= v:
                continue
            self.seen[eng][k] = v
            waits.append((k, v))
        return waits

    def _stamp(self, me, reads, writes):
        for r in reads:
            r.rs.append(me)
        for w in writes:
            w.w = me
            w.rs = []

    @staticmethod
    def _excl(reads, writes):
        ex = [r for r in reads if r.excl]
        if ex:
            writes = list(writes) + [r for r in ex if r not in writes]
            reads = [r for r in reads if not r.excl]
        return reads, writes

    def op(self, eng, name, kw, reads=(), writes=(), inc=True):
        reads, writes = self._excl(reads, writes)
        deps = self._deps(reads, writes)
        tick = self.cnt[eng] + 1
        if inc:
            self.cnt[eng] = tick
        waits = self._waits(eng, deps)
        self.q[eng].append((waits, (name, kw), (eng, 1) if inc else None))
        self._stamp((eng, tick), reads, writes)

    def dma(self, qeng, semkey, name, kw, reads=(), writes=()):
        deps = self._deps(reads, writes)
        prev = self.dcnt.get(semkey, 0)
        if prev and deps.get(semkey, 0) < prev:
            deps[semkey] = prev
        val = prev + 16
        self.dcnt[semkey] = val
        waits = self._waits(qeng, deps)
        self.q[qeng].append((waits, (name, kw), (semkey, 16)))
        self._stamp((semkey, val), reads, writes)

    def final_waits(self, eng):
        self.q[eng].append(([(k, v) for k, v in self.dcnt.items()], None, None))

    def replay(self, eng, e, sems):
        for waits, fn, inc in self.q[eng]:
            for k, v in waits:
                e.wait_ge(sems[k], v)
            if fn is None:
                continue
            ins = getattr(e, fn[0])(**fn[1])
            if inc is not None:
                ins.then_inc(sems[inc[0]], inc[1])


def _bucket_onehot():
    import math
    import jax
    import jax.numpy as jnp
    cpu = jax.devices("cpu")[0]
    with jax.default_device(cpu):
        jp = np.arange(128)[:, None]
        ip = np.arange(128)[None, :]
        oh = np.zeros((33, 2, 128, 128), np.float32)
        for rel in range(2):
            delta = jnp.asarray((rel - 1) * 128 + jp - ip, dtype=jnp.int32)
            nb = 16
            max_exact = 8
            base = jnp.where(delta > 0, nb, 0)
            n = jnp.abs(delta)
            nf = jnp.maximum(n, 1).astype(jnp.float32)
            large = max_exact + (jnp.log(nf / max_exact) / math.log(128 / max_exact)
                                 * (nb - max_exact)).astype(jnp.int32)
            large = jnp.minimum(large, nb - 1)
            bucket = np.asarray(base + jnp.where(n < max_exact, n, large))
            ck = (np.arange(128) // 64)[:, None]
            cq = (np.arange(128) // 64)[None, :]
            masked = (ck == 0) & (cq == 1) if rel == 0 else (ck == 1) & (cq == 0)
            for b in range(32):
                oh[b, rel] = (bucket == b).T.astype(np.float32)
            oh[32, rel] = masked.T.astype(np.float32)
    return oh.reshape(33, 2, 128 * 128)


class _Stop(Exception):
    pass


def build_program(n_layers=4, tiles=None, seq_len=2048, n_seq=2, same_engine_sync=True, stop_at=None):
    if tiles is None:
        tiles = [(s, p, (s == 0 and p == 0)) for s in range(n_seq) for p in range(seq_len // 512)]
    last_part = seq_len // 512 - 1
    L = n_layers
    nc = bass.Bass("TRN2", target_bir_lowering=False)

    def din(name, shape):
        return nc.dram_tensor(name, list(shape), F32, kind="ExternalInput").ap()

    def dout(name, shape):
        return nc.dram_tensor(name, list(shape), F32, kind="ExternalOutput").ap()

    xp = din("xp", [n_seq, seq_len, D])
    xs = din("xs", [128, D])
    ck = din("ck", [L, 4, 128, 256])
    cv = din("cv", [L, 4, 128, 256])
    w_in = din("w_in", [L, D, DPROJ])
    w_out = din("w_out", [L, D, D])
    norm_in = din("norm_in", [L, D])
    rel_bias = din("rel_bias", [32, 16])
    sinks = din("sinks", [L, 16])
    norm_attn = din("norm_attn", [L, 1024])
    norm_gmlp = din("norm_gmlp", [L, 1024])
    ln_g = din("ln_g", [L, 1024])
    ln_b = din("ln_b", [L, 1024])
    w_sp = din("w_sp", [L, 16, 128, 128])
    b_sp = din("b_sp", [L, 16, 128])
    norm_final = din("norm_final", [1, D])
    oh_d = din("oh", [33, 2, 16384])

    y_p = dout("y_p", [n_seq, seq_len, D])
    y_s = dout("y_s", [128, D])
    nkp = dout("nkp", [L, n_seq, 128, 256])
    nvp = dout("nvp", [L, n_seq, 128, 256])
    nks = dout("nks", [L, 4, 128, 256])
    nvs = dout("nvs", [L, 4, 128, 256])
    nvg = dout("nvg", [L, 128, 1024])

    S = Sched(same_engine_sync)
    NT = 640
    NSLOT = 3

    with contextlib.ExitStack() as es:
        def sb(name, shape, dt):
            return es.enter_context(nc.sbuf_tensor(name, list(shape), dt))

        x = sb("x", [128, 5, D], F32)
        hT = sb("hT", [128, 16, NT], BF16)
        zT = sb("zT", [128, 8, NT], BF16)
        kT = sb("kT", [128, 2, 1152], BF16)
        carK = sb("carK", [128, L, 2, 128], BF16)
        Vb = sb("Vb", [128, 9, 256], BF16)
        carV = sb("carV", [128, L, 256], BF16)
        BT = sb("BT", [128, 2, 16, 128], BF16)
        sgvg = sb("sgvg", [128, 5, 1024], BF16)
        tabA = sb("tabA", [128, 2048], F32)
        wsT = sb("wsT", [128, 16, 128], BF16)
        wsTs = sb("wsTs", [128, 16, 128], BF16)
        bsT = sb("bsT", [128, 8, 128], F32)
        wsl = [sb("wsl%d" % i, [128, 8192], BF16) for i in range(NSLOT)]
        xh = sb("xh", [128, D], BF16)
        junk = sb("junk", [128, 2048], BF16)
        kvo_t = sb("kvo", [128, 2, 512], F32)
        kvo = [kvo_t[:, 0, :], kvo_t[:, 1, :]]
        xhs = [xh[:, :], kvo_t[:, :, :].rearrange("p a n -> p (a n)").bitcast(BF16)]
        kTz1 = [xhs[0][:, 0:1280], xhs[1][:, 0:1280]]
        scr = sb("scr", [128, 7680], BF16)
        ckb = sb("ckb", [128, 4, 256], BF16)
        ident = sb("ident", [128, 128], BF16)
        identf = sb("identf", [128, 128], F32)
        ones = sb("ones", [128, 128], BF16)
        mhalf = sb("mhalf", [128, 16], F32)
        esink = sb("esink", [128, L * 16], F32)
        gA = sb("gA", [128, L * 8], F32)
        gG = sb("gG", [128, L * 8], F32)
        rbA = sb("rbA", [33, 16], F32)
        rbB = sb("rbB", [33, 16], BF16)
        st_ss = sb("st_ss", [128, 8], F32)
        st_r = sb("st_r", [128, 8], F32)
        st_ssa = sb("st_ssa", [128, 2, 8], F32)
        st_ra = sb("st_ra", [128, 8], F32)
        st_rb = sb("st_rb", [128, 8], F32)
        st_d = sb("st_d", [128, 2, 8], F32)
        st_bn = sb("st_bn", [128, 2, 2, 6], F32)
        st_mv = sb("st_mv", [128, 2, 2], F32)
        st_lr = sb("st_lr", [128, 2], F32)

        ps = es.enter_context(nc.psum_tensor("ps", [128, 8, 512], F32))
        psb = ps[:, :, :].bitcast(BF16)
        gv = junk[:, :].bitcast(F32)
        ohs = x[:, 0:4, :].bitcast(BF16).rearrange("p a n -> p (a n)")

        gu = scr[:, 0:2560].rearrange("p (g t) -> p g t", t=NT)
        kTz0 = scr[:, 0:2560].rearrange("p (a n) -> p a n", n=1280)
        pt = [scr[:, 2560 + 512 * i:2560 + 512 * (i + 1)] for i in range(4)]
        ao = [scr[:, 4608 + 1024 * i:4608 + 1024 * (i + 1)].bitcast(F32) for i in range(2)]
        zam = [scr[:, 6656 + 512 * i:6656 + 512 * (i + 1)] for i in range(2)]
        sgb = [scr[:, 2560 + 640 * i:2560 + 640 * (i + 1)] for i in range(2)]
        go = [scr[:, 3840 + 1280 * i:3840 + 1280 * (i + 1)].bitcast(F32) for i in range(2)]
        sqb = [scr[:, 6400 + 640 * i:6400 + 640 * (i + 1)] for i in range(2)]
        wn = junk[:, :].rearrange("p (g j) -> p g j", j=128)
        gv2 = scr[:, 3840:5888].bitcast(F32)
        tabB = scr[:, 0:4096].bitcast(F32)

        semnames = ["pe", "act", "dve", "pool"] + ["w%d" % i for i in range(NSLOT)] + ["ld%d" % i for i in range(8)] + \
                   ["st%d" % i for i in range(8)] + ["pl%d" % i for i in range(4)]
        sems = {k: es.enter_context(nc.semaphore("s_" + k)) for k in semnames}
        rr = {"ld": 0, "st": 0, "pl": 0}

        def semof(kind):
            n = {"ld": 8, "st": 8, "pl": 4}[kind]
            i = rr[kind]
            rr[kind] = (i + 1) % n
            return "%s%d" % (kind, i)

        R = {}

        def r1(name):
            if name not in R:
                R[name] = Res(name)
            return R[name]
        Rx = [r1("x%d" % i) for i in range(5)]
        RhT = [r1("hT%d" % i) for i in range(5)]
        RzT = [[r1("zT%d_%d" % (c, t)) for t in range(5)] for c in range(8)]
        RkT = [[r1("kT%d_%d" % (p, k)) for k in range(9)] for p in range(2)]
        RcarK = [r1("carK%d" % l) for l in range(L)]
        RV = [r1("V%d" % k) for k in range(9)]
        RcarV = [r1("carV%d" % l) for l in range(L)]
        Rbank = [r1("bank%d" % i) for i in range(8)]
        for rb_ in Rbank:
            rb_.excl = True
        Rsg = [r1("sgvg%d" % t) for t in range(5)]
        Rw = [r1("wsl%d" % i) for i in range(NSLOT)]
        Rkvo = [r1("kvo0"), r1("kvo1")]
        Rstd = [r1("std0"), r1("std1")]
        span = {}
        for g in range(4):
            span["qT%d" % g] = (640 * g, 640 * (g + 1))
        for i in range(4):
            span["pt%d" % i] = (2560 + 512 * i, 2560 + 512 * (i + 1))
        for i in range(2):
            span["ao%d" % i] = (4608 + 1024 * i, 4608 + 1024 * (i + 1))
            span["zam%d" % i] = (6656 + 512 * i, 6656 + 512 * (i + 1))
            span["sgb%d" % i] = (2560 + 640 * i, 2560 + 640 * (i + 1))
            span["go%d" % i] = (3840 + 1280 * i, 3840 + 1280 * (i + 1))
            span["sqb%d" % i] = (6400 + 640 * i, 6400 + 640 * (i + 1))

        span["gvB"] = (3840, 5888)
        span["tabB"] = (0, 4096)

        for (a_, b_) in span.values():
            assert a_ % 128 == 0 and b_ % 128 == 0
        segs = [r1("scrseg%d" % i) for i in range(7680 // 128)]

        def WR(name):
            a, b = span[name]
            return segs[a // 128:b // 128]

        RD = WR

        def ld(name, kw, writes, reads=()):
            S.dma("sp", semof("ld"), name, kw, reads=reads, writes=writes)

        def store(name, kw, reads):
            S.dma("sp", semof("st"), name, kw, reads=reads, writes=())

        def pool_dma(name, kw, writes, reads=(), sem=None):
            S.dma("pool", sem or semof("pl"), name, kw, reads=reads, writes=writes)

        cnts = {"pj": 0, "tm": 0, "ev": 0, "st": 0, "ob": 0, "ao": 0}

        def pj_bank():
            cnts["pj"] += 1
            return cnts["pj"] % 2

        def tm_bank():
            cnts["tm"] += 1
            return 6 + cnts["tm"] % 2

        def evac_copy(out, in_, reads, writes, scale=None):
            cnts["ev"] += 1
            if cnts["ev"] % 2 == 0:
                kw = dict(out=out, in_=in_, func=AF.Copy)
                if scale is not None:
                    kw["scale"] = scale
                S.op("act", "activation", kw, reads=reads, writes=writes)
            else:
                if scale is None:
                    S.op("dve", "tensor_copy", dict(out=out, in_=in_), reads=reads, writes=writes)
                else:
                    S.op("dve", "tensor_scalar", dict(out=out, in0=in_, scalar1=scale, scalar2=None, op0=ALU.mult), reads=reads, writes=writes)

        def rsqrt_small(dst, src, ncol, scale, reads, writes):
            S.op("pool", "tensor_scalar", dict(out=dst, in0=src, scalar1=scale, scalar2=EPS, op0=ALU.mult, op1=ALU.add),
                 reads=reads, writes=writes)
            S.op("pool", "tensor_tensor", dict(out=dst, in0=dst, in1=mhalf[:, 0:ncol], op=ALU.pow),
                 reads=list(writes) + [r1("mhalf")], writes=writes)

        S.op("pool", "memset", dict(ap=identf[:], constant=0.0), writes=[r1("identf")])
        S.op("pool", "affine_select", dict(out=identf[:], in_=identf[:], pattern=[[-1, 128]], compare_op=ALU.not_equal,
                                           fill=1.0, base=0, channel_multiplier=1), reads=[r1("identf")], writes=[r1("identf")])
        S.op("dve", "tensor_copy", dict(out=ident[:], in_=identf[:]), reads=[r1("identf")], writes=[r1("ident")])
        S.op("pool", "memset", dict(ap=ones[:], constant=1.0), writes=[r1("ones")])
        S.op("pool", "memset", dict(ap=mhalf[:], constant=-0.5), writes=[r1("mhalf")])
        ld("dma_start", dict(out=esink[:], in_=sinks.rearrange("l h -> (l h)").partition_broadcast(128)), [r1("esink")])
        S.op("act", "activation", dict(out=esink[:], in_=esink[:], func=AF.Exp), reads=[r1("esink")], writes=[r1("esink")])
        ld("dma_start", dict(out=gA[:].rearrange("p (l c) -> p l c", c=8), in_=norm_attn.rearrange("l (c p) -> p l c", p=128)), [r1("gA")])
        ld("dma_start", dict(out=gG[:].rearrange("p (l c) -> p l c", c=8), in_=norm_gmlp.rearrange("l (c p) -> p l c", p=128)), [r1("gG")])
        S.op("pool", "memset", dict(ap=rbA[:], constant=NEG), writes=[r1("rbA")])
        ld("dma_start", dict(out=rbA[0:32, :], in_=rel_bias[:, :]), [r1("rbA")])
        S.op("dve", "tensor_copy", dict(out=rbB[:], in_=rbA[:]), reads=[r1("rbA")], writes=[r1("rbB")])
        for rel in range(2):
            pool_dma("dma_start", dict(out=ohs[0:33, :], in_=oh_d[:, rel, :]), writes=Rx[0:4])
            for rnd in range(4):
                bank = rnd % 2
                for ii in range(32):
                    i_ = rnd * 32 + ii
                    S.op("pe", "matmul", dict(out=ps[:, bank, ii * 16:(ii + 1) * 16], lhsT=ohs[0:33, i_ * 128:(i_ + 1) * 128],
                                              rhs=rbB[0:33, :], start=True, stop=True),
                         reads=Rx[0:4] + [r1("rbB")], writes=[Rbank[bank]], inc=(ii == 31))
                S.op("dve", "tensor_copy", dict(out=BT[:, rel, :, rnd * 32:(rnd + 1) * 32],
                                                in_=ps[:, bank, :].rearrange("p (i h) -> p h i", h=16)),
                     reads=[Rbank[bank]], writes=[r1("BT")])
        store("dma_start", dict(out=nks[:, :, 0:96, :], in_=ck[:, :, 32:128, :]), reads=())
        store("dma_start", dict(out=nvs[:, :, 0:96, :], in_=cv[:, :, 32:128, :]), reads=())

        LAYER_UNITS = [("win", 2), ("win", 3), ("win", 4), ("win", 0), ("win", 1), ("wout", (0, 0)), ("wout", (0, 1024)),
                       ("win", 7), ("win", 8), ("win", 5), ("win", 9), ("win", 6), ("win", 10), ("wout", (1, 0)), ("wout", (1, 1024))]
        useq = [(k, l, i) for _ in tiles for l in range(L) for (k, i) in LAYER_UNITS]
        ust = {"load": 0, "use": 0, "got": {}}

        pendu = {}

        def issue_load():
            n = ust["load"]
            if n >= len(useq):
                return
            assert pendu.get(n - NSLOT, 0) == 0, "weight slot refilled while deferred readers of its old unit are pending"
            kind, l, idx = useq[n]
            si = n % NSLOT
            slot = wsl[si]
            if kind == "win":
                c0 = idx * 512
                v = slot[:, :].rearrange("p (k c) -> p k c", c=512)
                src = w_in[l, :, c0:c0 + 512].rearrange("(k p) c -> p k c", p=128)
            else:
                half, n0 = idx
                v = slot[:, :].rearrange("p (k c) -> p k c", c=1024)
                src = w_out[l, half * 1024:(half + 1) * 1024, n0:n0 + 1024].rearrange("(k p) c -> p k c", p=128)
            pool_dma("dma_start", dict(out=v, in_=src), writes=[Rw[si]], sem="w0")
            ust["got"][n] = (v, Rw[si])
            ust["load"] = n + 1

        def prefetch():
            while ust["load"] <= ust["use"] + NSLOT - 2 and ust["load"] < len(useq):
                issue_load()

        def next_unit(kind, l, idx, ahead=NSLOT - 1):
            n = ust["use"]
            assert useq[n] == (kind, l, idx), (useq[n], kind, l, idx)
            while ust["load"] <= n:
                issue_load()
            v = ust["got"].pop(n)
            ust["use"] = n + 1
            while ust["load"] <= n + ahead:
                if ust["load"] >= len(useq):
                    break
                issue_load()
            return v

        def chk(n):
            if stop_at is not None and stop_at == n:
                raise _Stop()

        try:
          chk(0)
          for ti, (seq, part, has_s) in enumerate(tiles):
              nb = 5 if has_s else 4
              ntok = 128 * nb
              tgs = [(0, 512)] + ([(512, 128)] if has_s else [])
              first = (part == 0)
              lastp = (part == last_part)
              nxt = tiles[ti + 1] if ti + 1 < len(tiles) else None
              if ti == 0:
                  for tb in range(4):
                      r0 = part * 512 + tb * 128
                      ld("dma_start", dict(out=x[:, tb, :], in_=xp[seq, r0:r0 + 128, :]), [Rx[tb]])
              if has_s:
                  ld("dma_start", dict(out=x[:, 4, :], in_=xs[:, :]), [Rx[4]])

              def hT_reads(t0, n):
                  return [RhT[t] for t in range(t0 // 128, (t0 + n) // 128)]

              xhc = {"i": 0}

              p0tab = {"B": False}
              p0pend = []
              p0a2 = []

              def phase0_A1(tb):
                  S.op("act", "activation", dict(out=junk[:], in_=x[:, tb, :], func=AF.Square, accum_out=st_ss[:, tb:tb + 1]),
                       reads=[Rx[tb]], writes=[r1("junk"), r1("st_ss%d" % tb)])
                  rsqrt_small(st_r[:, tb:tb + 1], st_ss[:, tb:tb + 1], 1, 1.0 / D, [r1("st_ss%d" % tb)], [r1("st_r%d" % tb)])
                  p0a2.append(tb)

              def phase0_A2():
                  tb = p0a2.pop(0)
                  xhc["i"] += 1
                  xi = xhc["i"] % 2
                  xw = [r1("xh%d" % xi)] + ([Rkvo[0], Rkvo[1]] if xi == 1 else [])
                  gt, gtr = (tabB, RD("tabB")) if p0tab["B"] else (tabA[:], [r1("tabA")])
                  S.op("dve", "scalar_tensor_tensor", dict(out=xhs[xi], in0=x[:, tb, :], scalar=st_r[:, tb:tb + 1], in1=gt,
                                                           op0=ALU.mult, op1=ALU.mult),
                       reads=[Rx[tb], r1("st_r%d" % tb)] + gtr, writes=xw)
                  p0pend.append((tb, xi))

              def phase0_B():
                  tb, xi = p0pend.pop(0)
                  for kg in range(4):
                      b = tm_bank()
                      for j in range(4):
                          kc = kg * 4 + j
                          S.op("pe", "transpose", dict(out=psb[:, b, j * 128:(j + 1) * 128], in_=xhs[xi][:, kc * 128:(kc + 1) * 128], identity=ident[:]),
                               reads=[r1("xh%d" % xi), r1("ident")], writes=[Rbank[b]], inc=(j == 3))
                      S.op("act", "activation", dict(out=hT[:, kg * 4:kg * 4 + 4, tb * 128:(tb + 1) * 128],
                                                     in_=psb[:, b, 0:512].rearrange("p (j t) -> p j t", t=128), func=AF.Copy),
                           reads=[Rbank[b]], writes=[RhT[tb]])

              def phase0_block(tb):
                  if p0a2:
                      if len(p0pend) == 2:
                          phase0_B()
                      phase0_A2()
                  phase0_A1(tb)

              def phase0_flush():
                  while p0a2:
                      if len(p0pend) == 2:
                          phase0_B()
                      phase0_A2()
                  while p0pend:
                      phase0_B()

              ystgs = [(sgvg[:, 0:4, :].rearrange("p a n -> p (a n)").bitcast(F32), Rsg[0:4]),
                       (hT[:, :, :].rearrange("p k t -> p (k t)")[:, 0:4096].bitcast(F32), RhT[0:5])]

              def final_block(tb):
                  S.op("act", "activation", dict(out=junk[:], in_=x[:, tb, :], func=AF.Square, accum_out=st_ss[:, tb:tb + 1]),
                       reads=[Rx[tb]], writes=[r1("junk"), r1("st_ss%d" % tb)])
                  rsqrt_small(st_r[:, tb:tb + 1], st_ss[:, tb:tb + 1], 1, 1.0 / D, [r1("st_ss%d" % tb)], [r1("st_r%d" % tb)])
                  ystg, ysr = ystgs[tb % 2]
                  S.op("dve", "scalar_tensor_tensor", dict(out=ystg, in0=x[:, tb, :], scalar=st_r[:, tb:tb + 1], in1=tabA[:],
                                                           op0=ALU.mult, op1=ALU.mult),
                       reads=[Rx[tb], r1("st_r%d" % tb), r1("tabA")], writes=ysr)
                  if tb < 4:
                      r0 = part * 512 + tb * 128
                      store("dma_start", dict(out=y_p[seq, r0:r0 + 128, :], in_=ystg), ysr)
                      if nxt is not None:
                          nseq, npart, _ = nxt
                          nr0 = npart * 512 + tb * 128
                          ld("dma_start", dict(out=x[:, tb, :], in_=xp[nseq, nr0:nr0 + 128, :]), [Rx[tb]])
                  else:
                      store("dma_start", dict(out=y_s[:, :], in_=ystg), ysr)

              for l in range(L):
                  if l == 0:
                      if ti == 0:
                          ld("dma_start", dict(out=tabA[:], in_=norm_in[0:1, :].partition_broadcast(128)), [r1("tabA")])
                      p0tab["B"] = (ti > 0)
                      for tb in range(nb):
                          phase0_block(tb)
                      phase0_flush()
                      p0tab["B"] = False
                  chk(1)
                  W, rW = next_unit("win", l, 2)
                  while p0a2:
                      if len(p0pend) == 2:
                          phase0_B()
                      phase0_A2()
                  kv_pending = [t for (t, _) in p0pend]
                  kv_order = [t for t in range(nb) if t not in kv_pending] + kv_pending
                  for tb in kv_order:
                      if tb in kv_pending and p0pend:
                          phase0_B()
                      special = (tb == 4) or (tb == 3 and lastp)
                      c0, n = (0, 512) if special else (256, 256)
                      b = tm_bank()
                      for kc in range(16):
                          S.op("pe", "matmul", dict(out=ps[:, b, 0:n], lhsT=hT[:, kc, tb * 128:(tb + 1) * 128], rhs=W[:, kc, c0:c0 + n],
                                                    start=(kc == 0), stop=(kc == 15)),
                               reads=[rW, RhT[tb]], writes=[Rbank[b]], inc=(kc == 15))
                      evac_copy(Vb[:, tb, :], ps[:, b, n - 256:n], [Rbank[b]], [RV[tb]])
                      if special:
                          ko = tb % 2
                          S.op("dve", "tensor_copy", dict(out=kvo[ko], in_=ps[:, b, :]), reads=[Rbank[b]], writes=[Rkvo[ko], r1("xh1")])
                          if tb == 4:
                              for s in range(4):
                                  store("dma_start", dict(out=nks[l, s, 96:128, :], in_=kvo[ko][32 * s:32 * s + 32, 0:256]), [Rkvo[ko]])
                                  store("dma_start", dict(out=nvs[l, s, 96:128, :], in_=kvo[ko][32 * s:32 * s + 32, 256:512]), [Rkvo[ko]])
                          else:
                              store("dma_start", dict(out=nkp[l, seq, :, :], in_=kvo[ko][:, 0:256]), [Rkvo[ko]])
                              store("dma_start", dict(out=nvp[l, seq, :, :], in_=kvo[ko][:, 256:512]), [Rkvo[ko]])
                  phase0_flush()
                  for P in range(2):
                      for (t0, n) in tgs:
                          b = pj_bank()
                          for kc in range(16):
                              S.op("pe", "matmul", dict(out=ps[:, b, 0:n], lhsT=W[:, kc, P * 128:(P + 1) * 128], rhs=hT[:, kc, t0:t0 + n],
                                                        start=(kc == 0), stop=(kc == 15)),
                                   reads=[rW] + hT_reads(t0, n), writes=[Rbank[b]], inc=(kc == 15))
                          kbs = list(range(4)) if t0 == 0 else [4]
                          evac_copy(kT[:, P, t0:t0 + n], ps[:, b, 0:n], [Rbank[b]], [RkT[P][k] for k in kbs])
                  chk(12)
                  if has_s:
                      pool_dma("dma_start", dict(out=ckb[:], in_=ck[l].rearrange("s t c -> t s c")), writes=[r1("ckb")])
                      pool_dma("dma_start", dict(out=Vb[:, 5:9, :], in_=cv[l].rearrange("s t c -> t s c")), writes=RV[5:9])
                      for P in range(2):
                          b = tm_bank()
                          for s in range(4):
                              S.op("pe", "transpose", dict(out=psb[:, b, s * 128:(s + 1) * 128], in_=ckb[:, s, P * 128:(P + 1) * 128], identity=ident[:]),
                                   reads=[r1("ckb"), r1("ident")], writes=[Rbank[b]], inc=(s == 3))
                          evac_copy(kT[:, P, 640:1152], psb[:, b, 0:512], [Rbank[b]], RkT[P][5:9])

                  wnW = [r1("junk")]
                  variants = [(wsT, "wsT", False)] + ([(wsTs, "wsTs", True)] if has_s else [])

                  def ws_stage(blockdiag):
                      if not blockdiag:
                          pool_dma("dma_start", dict(out=wn, in_=w_sp[l].rearrange("g i j -> i g j")), writes=wnW)
                      else:
                          S.op("pool", "memset", dict(ap=wn, constant=0.0), writes=wnW)
                          for s_ in range(4):
                              pool_dma("dma_start", dict(out=wn[32 * s_:32 * s_ + 32, :, 32 * s_:32 * s_ + 32],
                                                         in_=w_sp[l, :, 0:32, 0:32].rearrange("g i j -> i g j")), writes=wnW)
                      S.op("pool", "affine_select", dict(out=wn, in_=wn, pattern=[[0, 16], [-1, 128]], compare_op=ALU.is_ge,
                                                         fill=0.0, base=0, channel_multiplier=1), reads=[r1("junk")], writes=wnW)

                  def ws_transposes(dst, dname):
                      for gg in range(4):
                          b = tm_bank()
                          for j in range(4):
                              g_ = gg * 4 + j
                              S.op("pe", "transpose", dict(out=psb[:, b, j * 128:(j + 1) * 128], in_=wn[:, g_, :], identity=ident[:]),
                                   reads=[r1("junk"), r1("ident")], writes=[Rbank[b]], inc=(j == 3))
                          evac_copy(dst[:, gg * 4:gg * 4 + 4, :], psb[:, b, 0:512].rearrange("p (j t) -> p j t", t=128), [Rbank[b]], [r1(dname)])

                  ws_stage(False)
                  chk(2)
                  kz_all = []
                  kz_res_all = []

                  def build_kz():
                      kz_all[:] = [[kTz0[:, 0, :], kTz0[:, 1, :]], kTz1]
                      kz_res_all[:] = [[RD("qT0") + RD("qT1"), RD("qT2") + RD("qT3")],
                                    [[r1("xh0")], [r1("xh1"), Rkvo[0], Rkvo[1]]]]
                      kvalid = list(range(4)) + ([4, 5, 6, 7, 8] if has_s else [])
                      for P in range(2):
                          for par in range(2):
                              own = slice(par * 64, par * 64 + 64)
                              oth = slice((1 - par) * 64, (1 - par) * 64 + 64)
                              dst = kz_all[P][par]
                              S.op("pool", "memset", dict(ap=dst[oth, :], constant=0.0), writes=kz_res_all[P][par])
                              S.op("pool", "tensor_copy", dict(out=dst[own, 0:1152], in_=kT[own, P, :]),
                                   reads=[RkT[P][k] for k in kvalid], writes=kz_res_all[P][par])
                              if not first:
                                  S.op("pool", "tensor_copy", dict(out=dst[own, 1152:1280], in_=carK[own, l, P, :]),
                                       reads=[RcarK[l]], writes=kz_res_all[P][par])

                  for u in range(2):
                      W, rW = next_unit("win", l, 3 + u)
                      if u == 0:
                          build_kz()
                      for tb in range(nb):
                          b = tm_bank()
                          for kc in range(16):
                              S.op("pe", "matmul", dict(out=ps[:, b, :], lhsT=hT[:, kc, tb * 128:(tb + 1) * 128], rhs=W[:, kc, :],
                                                        start=(kc == 0), stop=(kc == 15)),
                                   reads=[rW, RhT[tb]], writes=[Rbank[b]], inc=(kc == 15))
                          S.op("act", "activation", dict(out=sgvg[:, tb, u * 512:(u + 1) * 512], in_=ps[:, b, :], func=AF.Silu),
                               reads=[Rbank[b]], writes=[Rsg[tb]])
                      if u == 0:
                          ws_transposes(wsT, "wsT")
                          if has_s:
                              ws_stage(True)
                      elif has_s:
                          ws_transposes(wsTs, "wsTs")

                  chk(3)
                  def key_blocks(tb):
                      out = []
                      if tb > 0:
                          out.append(("p", tb - 1, 0))
                      elif not first:
                          out.append(("c", None, 0))
                      out.append(("p", tb, 1))
                      return out

                  pend_tr = []

                  def emit_tr(ai, tb, P):
                      for j in range(4):
                          S.op("pe", "transpose", dict(out=psb[:, 7, j * 128:(j + 1) * 128], in_=zam[ai][:, j * 128:(j + 1) * 128], identity=ident[:]),
                               reads=RD("zam%d" % ai) + [r1("ident")], writes=[Rbank[7]], inc=(j == 3))
                      for j in range(4):
                          c = 4 * P + j
                          S.op("dve", "tensor_scalar", dict(out=zT[:, c, tb * 128:(tb + 1) * 128], in0=psb[:, 7, j * 128:(j + 1) * 128],
                                                            scalar1=gA[:, l * 8 + c:l * 8 + c + 1], scalar2=None, op0=ALU.mult),
                               reads=[Rbank[7], r1("gA")], writes=[RzT[c][tb]])

                  def flush_tr():
                      while pend_tr:
                          ai_, tb_, P_ = pend_tr.pop(0)
                          emit_tr(ai_, tb_, P_)
                          if P_ == 1:
                              for cb2 in range(2):
                                  fresh.append(lambda tb_=tb_, cb2=cb2: opA_group(opA["W"], opA["rW"], tb_, cb2, 0))

                  def q_groups(P):
                      W, rW = next_unit("win", l, P)
                      un = ust["use"] - 1
                      W4 = W.rearrange("p k (a b) -> p k a b", b=64)
                      out = []
                      for g in range(4):
                          for (t0, n) in tgs:
                              pendu[un] = pendu.get(un, 0) + 1

                              def grp(g=g, t0=t0, n=n):
                                  pendu[un] -= 1
                                  b = pj_bank()
                                  for kc in range(16):
                                      S.op("pe", "matmul", dict(out=ps[0:64, b, 0:n], lhsT=W4[:, kc, g, :], rhs=hT[:, kc, t0:t0 + n],
                                                                start=(kc == 0), stop=(kc == 15)),
                                           reads=[rW] + hT_reads(t0, n), writes=[Rbank[b]], inc=False)
                                      S.op("pe", "matmul", dict(out=ps[64:128, b, 0:n], lhsT=W4[:, kc, 4 + g, :], rhs=hT[:, kc, t0:t0 + n],
                                                                start=(kc == 0), stop=(kc == 15), tile_position=(0, 64)),
                                           reads=[rW] + hT_reads(t0, n), writes=[Rbank[b]], inc=(kc == 15))
                                  evac_copy(zT[:, 4 * P + g, t0:t0 + n], ps[:, b, 0:n], [Rbank[b]],
                                            [RzT[4 * P + g][t] for t in range(t0 // 128, (t0 + n) // 128)], scale=0.125)
                              out.append(grp)
                      return out

                  def opA_group(W, rW, tb, cb2, n0):
                      if n0 == 0:
                          pendu[opA["un"]] -= 1
                      b = pj_bank()
                      for c in range(8):
                          S.op("pe", "matmul", dict(out=ps[:, b, :], lhsT=zT[:, c, tb * 128:(tb + 1) * 128],
                                                    rhs=W[:, c, cb2 * 512:(cb2 + 1) * 512], start=(c == 0), stop=(c == 7)),
                               reads=[rW, RzT[c][tb]], writes=[Rbank[b]], inc=(c == 7))
                      cols = slice(n0 + cb2 * 512, n0 + cb2 * 512 + 512)
                      S.op("dve", "scalar_tensor_tensor", dict(out=x[:, tb, cols], in0=ps[:, b, :], scalar=st_ra[:, tb:tb + 1],
                                                               in1=x[:, tb, cols], op0=ALU.mult, op1=ALU.add),
                           reads=[Rbank[b], r1("st_ra%d" % tb), Rx[tb]], writes=[Rx[tb]])

                  fillers = []
                  fresh = []
                  for g_ in q_groups(0):
                      g_()
                  opA = {}

                  for P in range(2):
                      qT = zT[:, 4 * P:4 * P + 4, :]
                      if P == 0:
                          fillers.extend(q_groups(1))
                      else:
                          opA["W"], opA["rW"] = next_unit("wout", l, (0, 0))
                          opA["un"] = ust["use"] - 1
                          pendu[opA["un"]] = 2 * nb

                      def allq_tb(tb):
                          return [RzT[4 * P + g][tb] for g in range(4)]

                      kz_res = kz_res_all[P]
                      kz = kz_all[P]

                      flush_tr()

                      def stage_A(tb, par):
                          groups = []
                          kvh = 2 * P + par
                          rows = slice(par * 64, par * 64 + 64)
                          if tb < 4:
                              for (kind, kb, rel) in key_blocks(tb):
                                  cnts["st"] += 1
                                  sb_ = 2 + cnts["st"] % 2
                                  pi = cnts["st"] % 4
                                  if kind == "p":
                                      kap = kz[par][:, kb * 128:(kb + 1) * 128]
                                  else:
                                      kap = kz[par][:, 1152:1280]
                                  S.op("pe", "matmul", dict(out=ps[:, sb_, :].rearrange("p (g q) -> p g q", q=128), lhsT=kap,
                                                            rhs=qT[:, :, tb * 128:(tb + 1) * 128], start=True, stop=False),
                                       reads=kz_res[par] + allq_tb(tb), writes=[Rbank[sb_]], inc=False)
                                  S.op("pe", "matmul", dict(out=ps[:, sb_, :].rearrange("p (g q) -> p g q", q=128), lhsT=ident[:],
                                                            rhs=BT[:, rel, 4 * kvh:4 * kvh + 4, :], start=False, stop=True),
                                       reads=[r1("ident"), r1("BT")], writes=[Rbank[sb_]])
                                  S.op("act", "activation", dict(out=pt[pi], in_=ps[:, sb_, :], func=AF.Exp),
                                       reads=[Rbank[sb_]], writes=WR("pt%d" % pi))
                                  groups.append((pi, kind, kb))
                          else:
                              cnts["st"] += 1
                              sb_ = 2 + cnts["st"] % 2
                              pi = cnts["st"] % 4
                              for s in range(4):
                                  S.op("pe", "matmul", dict(out=ps[:, sb_, s * 128:(s + 1) * 128].rearrange("p (g q) -> p g q", q=32),
                                                            lhsT=kz[par][:, 640 + 128 * s:640 + 128 * s + 128],
                                                            rhs=qT[:, :, 512 + 32 * s:512 + 32 * s + 32], start=True, stop=False),
                                       reads=kz_res[par] + allq_tb(4), writes=[Rbank[sb_]], inc=False)
                                  S.op("pe", "matmul", dict(out=ps[:, sb_, s * 128:(s + 1) * 128].rearrange("p (g q) -> p g q", q=32),
                                                            lhsT=ident[:], rhs=BT[:, 0, 4 * kvh:4 * kvh + 4, 0:32], start=False, stop=True),
                                       reads=[r1("ident"), r1("BT")], writes=[Rbank[sb_]], inc=(s == 3))
                              S.op("act", "activation", dict(out=pt[pi], in_=ps[:, sb_, :], func=AF.Exp),
                                   reads=[Rbank[sb_]], writes=WR("pt%d" % pi))
                              groups.append((pi, "sc", None))
                              cnts["st"] += 1
                              sb_ = 2 + cnts["st"] % 2
                              pi = cnts["st"] % 4
                              for s in range(4):
                                  S.op("pe", "matmul", dict(out=ps[32 * s:32 * s + 32, sb_, s * 128:(s + 1) * 128].rearrange("p (g q) -> p g q", q=32),
                                                            lhsT=kz[par][:, 512 + 32 * s:512 + 32 * s + 32],
                                                            rhs=qT[:, :, 512 + 32 * s:512 + 32 * s + 32], start=True, stop=False,
                                                            tile_position=(0, 32 * s)),
                                       reads=kz_res[par] + allq_tb(4), writes=[Rbank[sb_]], inc=False)
                                  S.op("pe", "matmul", dict(out=ps[32 * s:32 * s + 32, sb_, s * 128:(s + 1) * 128].rearrange("p (g q) -> p g q", q=32),
                                                            lhsT=ident[0:32, 0:32], rhs=BT[0:32, 1, 4 * kvh:4 * kvh + 4, 0:32], start=False, stop=True,
                                                            tile_position=(0, 32 * s)),
                                       reads=[r1("ident"), r1("BT")], writes=[Rbank[sb_]], inc=(s == 3))
                              for s in range(4):
                                  S.op("act", "activation", dict(out=pt[pi][32 * s:32 * s + 32, s * 128:(s + 1) * 128],
                                                                 in_=ps[32 * s:32 * s + 32, sb_, s * 128:(s + 1) * 128], func=AF.Exp),
                                       reads=[Rbank[sb_]], writes=WR("pt%d" % pi))
                              groups.append((pi, "so", None))
                          return groups

                      obof = {}

                      def stage_B(tb, par, grs):
                          if par == 0:
                              cnts["ob"] += 1
                              obof[tb] = 4 + cnts["ob"] % 2
                          ob = obof[tb]
                          di = ob - 4
                          kvh = 2 * P + par
                          O = ps[:, ob, :].rearrange("p (h d) -> p h d", d=64)
                          Dn = ps[:, 6, 0:8]
                          ng = len(grs)
                          for g in range(4):
                              hh = par * 4 + g
                              if tb < 4:
                                  for gi, (pi, kind, kb) in enumerate(grs):
                                      vap = Vb[:, kb, kvh * 64:(kvh + 1) * 64] if kind == "p" else carV[:, l, kvh * 64:(kvh + 1) * 64]
                                      vres = RV[kb] if kind == "p" else RcarV[l]
                                      S.op("pe", "matmul", dict(out=O[:, hh, :], lhsT=pt[pi][:, g * 128:(g + 1) * 128], rhs=vap,
                                                                start=(gi == 0), stop=(gi == ng - 1)),
                                           reads=RD("pt%d" % pi) + [vres], writes=[Rbank[ob]], inc=False)
                                  for gi, (pi, kind, kb) in enumerate(grs):
                                      S.op("pe", "matmul", dict(out=Dn[:, hh:hh + 1], lhsT=pt[pi][:, g * 128:(g + 1) * 128], rhs=ones[:, 0:1],
                                                                start=(gi == 0), stop=(gi == ng - 1)),
                                           reads=RD("pt%d" % pi) + [r1("ones")], writes=[Rbank[6]], inc=(gi == ng - 1))
                              else:
                                  (pc, _, _), (po, _, _) = grs
                                  for s in range(4):
                                      qs = slice(32 * s, 32 * s + 32)
                                      cs = slice(s * 128 + g * 32, s * 128 + g * 32 + 32)
                                      S.op("pe", "matmul", dict(out=O[qs, hh, :], lhsT=pt[pc][:, cs], rhs=Vb[:, 5 + s, kvh * 64:(kvh + 1) * 64],
                                                                start=True, stop=False, tile_position=(0, 32 * s)),
                                           reads=RD("pt%d" % pc) + [RV[5 + s]], writes=[Rbank[ob]], inc=False)
                                      S.op("pe", "matmul", dict(out=O[qs, hh, :], lhsT=pt[po][qs, cs], rhs=Vb[qs, 4, kvh * 64:(kvh + 1) * 64],
                                                                start=False, stop=True, tile_position=(32 * s, 32 * s)),
                                           reads=RD("pt%d" % po) + [RV[4]], writes=[Rbank[ob]], inc=False)
                                      S.op("pe", "matmul", dict(out=Dn[qs, hh:hh + 1], lhsT=pt[pc][:, cs], rhs=ones[:, 0:1],
                                                                start=True, stop=False, tile_position=(0, 32 * s)),
                                           reads=RD("pt%d" % pc) + [r1("ones")], writes=[Rbank[6]], inc=False)
                                      S.op("pe", "matmul", dict(out=Dn[qs, hh:hh + 1], lhsT=pt[po][qs, cs], rhs=ones[qs, 0:1],
                                                                start=False, stop=True, tile_position=(32 * s, 32 * s)),
                                           reads=RD("pt%d" % po) + [r1("ones")], writes=[Rbank[6]], inc=(s == 3))
                          if par == 0:
                              return
                          flush_tr()
                          cnts["ao"] += 1
                          ai = cnts["ao"] % 2
                          dd = st_d[:, di, :]
                          S.op("dve", "tensor_tensor", dict(out=dd, in0=Dn, in1=esink[:, l * 16 + 8 * P:l * 16 + 8 * P + 8], op=ALU.add),
                               reads=[Rbank[6], r1("esink")], writes=[Rstd[di]])
                          S.op("dve", "reciprocal", dict(out=dd, in_=dd), reads=[Rstd[di]], writes=[Rstd[di]])
                          S.op("dve", "tensor_tensor", dict(out=ao[ai].rearrange("p (h d) -> p h d", d=64), in0=O,
                                                            in1=dd.unsqueeze(2).broadcast_to([128, 8, 64]), op=ALU.mult),
                               reads=[Rbank[ob], Rstd[di]], writes=WR("ao%d" % ai))
                          S.op("act", "activation", dict(out=junk[:, 0:512], in_=ao[ai], func=AF.Square, accum_out=st_ssa[:, P, tb:tb + 1]),
                               reads=RD("ao%d" % ai), writes=[r1("junk"), r1("st_ssa%d_%d" % (P, tb))])
                          if P == 1:
                              S.op("pool", "tensor_tensor", dict(out=st_ra[:, tb:tb + 1], in0=st_ssa[:, 0, tb:tb + 1], in1=st_ssa[:, 1, tb:tb + 1], op=ALU.add),
                                   reads=[r1("st_ssa0_%d" % tb), r1("st_ssa1_%d" % tb)], writes=[r1("st_ra%d" % tb)])
                              rsqrt_small(st_ra[:, tb:tb + 1], st_ra[:, tb:tb + 1], 1, 1.0 / 1024, [r1("st_ra%d" % tb)], [r1("st_ra%d" % tb)])
                          S.op("dve", "tensor_tensor", dict(out=zam[ai], in0=ao[ai], in1=sgvg[:, tb, P * 512:(P + 1) * 512], op=ALU.mult),
                               reads=RD("ao%d" % ai) + [Rsg[tb]], writes=WR("zam%d" % ai))
                          pend_tr.append((ai, tb, P))

                      pend = None
                      nhu = 2 * nb
                      for tb in range(nb):
                          for par in range(2):
                              grs = stage_A(tb, par)
                              if pend is not None:
                                  stage_B(*pend)
                              pend = (tb, par, grs)
                              left = nhu - (2 * tb + par + 1)
                              if fillers and (len(fillers) > 1 + left // 2 or P == 1):
                                  fillers.pop(0)()
                              fillers.extend(fresh)
                              del fresh[:]
                      stage_B(*pend)
                      if P == 0:
                          while fillers:
                              fillers.pop(0)()

                  chk(4)
                  if not lastp:
                      for P in range(2):
                          S.op("pool", "tensor_copy", dict(out=carK[:, l, P, :], in_=kT[:, P, 384:512]), reads=[RkT[P][3]], writes=[RcarK[l]])
                      S.op("pool", "tensor_copy", dict(out=carV[:, l, :], in_=Vb[:, 3, :]), reads=[RV[3]], writes=[RcarV[l]])

                  while fillers:
                      fillers.pop(0)()
                  flush_tr()
                  fillers.extend(fresh)
                  del fresh[:]
                  Wb, rWb = next_unit("wout", l, (0, 1024), ahead=NSLOT - 2)
                  for tb in range(nb):
                      for cb2 in range(2):
                          opA_group(Wb, rWb, tb, cb2, 1024)
                      if tb == 0:
                          while fillers:
                              fillers.pop(0)()
                          prefetch()

                  def out_proj(half, rtab, rres, after_tb=None, early=None):
                      gcount = 0
                      held = []
                      for n0 in (0, 1024):
                          W, rW = next_unit("wout", l, (half, n0))
                          for tb in range(nb):
                              for cb2 in range(2):
                                  cnts["opb"] = cnts.get("opb", 0) + 1
                                  b = (0, 1, 4, 5)[cnts["opb"] % 4]
                                  for c in range(8):
                                      S.op("pe", "matmul", dict(out=ps[:, b, :], lhsT=zT[:, c, tb * 128:(tb + 1) * 128],
                                                                rhs=W[:, c, cb2 * 512:(cb2 + 1) * 512], start=(c == 0), stop=(c == 7)),
                                           reads=[rW, RzT[c][tb]], writes=[Rbank[b]], inc=(c == 7))
                                  cols = slice(n0 + cb2 * 512, n0 + cb2 * 512 + 512)

                                  def evac(b=b, tb=tb, cols=cols):
                                      S.op("dve", "scalar_tensor_tensor", dict(out=x[:, tb, cols], in0=ps[:, b, :], scalar=rtab[:, tb:tb + 1],
                                                                               in1=x[:, tb, cols], op0=ALU.mult, op1=ALU.add),
                                           reads=[Rbank[b], rres, Rx[tb]], writes=[Rx[tb]])
                                  gcount += 1
                                  if early is not None and gcount <= 3:
                                      held.append(evac)
                                      if gcount == 3:
                                          early()
                                          for ev in held:
                                              ev()
                                  else:
                                      evac()
                              if after_tb is not None and n0 == 1024:
                                  after_tb(tb)

                  chk(5)
                  ld("dma_start", dict(out=tabA[:, 0:1024], in_=ln_g[l:l + 1, :].partition_broadcast(128)), [r1("tabA")])
                  ld("dma_start", dict(out=tabA[:, 1024:2048], in_=ln_b[l:l + 1, :].partition_broadcast(128)), [r1("tabA")])
                  for par in range(2):
                      ld("dma_start", dict(out=bsT[par * 64:(par + 1) * 64, :, :],
                                           in_=b_sp[l].rearrange("(c par) i -> par c i", par=2)[par].partition_broadcast(64)), [r1("bsT")])
                  W0, rW0 = next_unit("win", l, 7)
                  W1, rW1 = next_unit("win", l, 8, ahead=NSLOT - 2)
                  for tb in range(nb):
                      pb = 6 if tb % 2 == 0 else 4
                      gvb, gvR, gvW = (gv, [r1("junk")], [r1("junk")]) if tb % 2 == 0 else (gv2, RD("gvB"), WR("gvB"))
                      bi = tb % 2
                      for u, (W, rW) in enumerate(((W0, rW0), (W1, rW1))):
                          for kc in range(16):
                              S.op("pe", "matmul", dict(out=ps[:, pb + u, :], lhsT=hT[:, kc, tb * 128:(tb + 1) * 128], rhs=W[:, kc, :],
                                                        start=(kc == 0), stop=(kc == 15)),
                                   reads=[rW, RhT[tb]], writes=[Rbank[pb + u]], inc=(kc == 15))
                      if tb == nb - 1:
                          ust["use"] += 0
                          while ust["load"] <= ust["use"] + NSLOT - 2 + 1 and ust["load"] < len(useq):
                              issue_load()
                      S.op("act", "activation", dict(out=gvb, in_=ps[:, pb:pb + 2, :].rearrange("p a n -> p (a n)"), func=AF.Gelu),
                           reads=[Rbank[pb], Rbank[pb + 1]], writes=gvW)
                      for hh in range(2):
                          S.op("dve", "bn_stats", dict(out=st_bn[:, bi, hh, :], in_=gvb[:, hh * 512:(hh + 1) * 512]),
                               reads=gvR, writes=[r1("st_bn%d" % bi)])
                      S.op("dve", "bn_aggr", dict(out=st_mv[:, bi, :], in_=st_bn[:, bi, :, :].rearrange("p a s -> p (a s)")),
                           reads=[r1("st_bn%d" % bi)], writes=[r1("st_mv%d" % bi)])
                      rsqrt_small(st_lr[:, bi:bi + 1], st_mv[:, bi, 1:2], 1, 1.0, [r1("st_mv%d" % bi)], [r1("st_lr%d" % bi)])
                      S.op("dve", "tensor_scalar", dict(out=gvb, in0=gvb, scalar1=st_mv[:, bi, 0:1], scalar2=st_lr[:, bi:bi + 1],
                                                        op0=ALU.subtract, op1=ALU.mult),
                           reads=gvR + [r1("st_mv%d" % bi), r1("st_lr%d" % bi)], writes=gvW)
                      S.op("dve", "tensor_tensor", dict(out=gvb, in0=gvb, in1=tabA[:, 0:1024], op=ALU.mult),
                           reads=gvR + [r1("tabA")], writes=gvW)
                      if tb == 4:
                          S.op("dve", "tensor_tensor", dict(out=gvb, in0=gvb, in1=tabA[:, 1024:2048], op=ALU.add),
                               reads=gvR + [r1("tabA")], writes=gvW)
                          store("dma_start", dict(out=nvg[l, :, :], in_=gvb), gvR)
                          S.op("act", "activation", dict(out=sgvg[:, tb, :], in_=gvb, func=AF.Copy), reads=gvR, writes=[Rsg[tb]])
                      else:
                          S.op("dve", "tensor_tensor", dict(out=sgvg[:, tb, :], in0=gvb, in1=tabA[:, 1024:2048], op=ALU.add),
                               reads=gvR + [r1("tabA")], writes=[Rsg[tb]])

                  chk(6)
                  pend_rb = []

                  def emit_rb(gi_, c):
                      for tb in range(nb):
                          S.op("pe", "matmul", dict(out=ps[:, 2, c * 8 + tb:c * 8 + tb + 1], lhsT=sqb[gi_][:, tb * 128:(tb + 1) * 128],
                                                    rhs=ones[:, 0:1], start=True, stop=True),
                               reads=RD("sqb%d" % gi_) + [r1("ones")], writes=[Rbank[2]], inc=(tb == nb - 1))
                  for half in range(2):
                      WU, rWU = next_unit("win", l, 5 + half)
                      for c4 in range(4):
                          for (t0, n) in tgs:
                              b = pj_bank()
                              for kc in range(16):
                                  S.op("pe", "matmul", dict(out=ps[:, b, 0:n], lhsT=WU[:, kc, c4 * 128:(c4 + 1) * 128], rhs=hT[:, kc, t0:t0 + n],
                                                            start=(kc == 0), stop=(kc == 15)),
                                       reads=[rWU] + hT_reads(t0, n), writes=[Rbank[b]], inc=(kc == 15))
                              S.op("act", "activation", dict(out=gu[:, c4, t0:t0 + n], in_=ps[:, b, 0:n], func=AF.Gelu),
                                   reads=[Rbank[b]], writes=WR("qT%d" % c4))
                      WG, rWG = next_unit("win", l, 9 + half)
                      for c4 in range(4):
                          c = half * 4 + c4
                          gi_ = c % 2
                          for (t0, n) in tgs:
                              b = pj_bank()
                              for kc in range(16):
                                  S.op("pe", "matmul", dict(out=ps[:, b, 0:n], lhsT=WG[:, kc, c4 * 128:(c4 + 1) * 128], rhs=hT[:, kc, t0:t0 + n],
                                                            start=(kc == 0), stop=(kc == 15)),
                                       reads=[rWG] + hT_reads(t0, n), writes=[Rbank[b]], inc=(kc == 15))
                              S.op("act", "activation", dict(out=sgb[gi_][:, t0:t0 + n], in_=ps[:, b, 0:n], func=AF.Silu),
                                   reads=[Rbank[b]], writes=WR("sgb%d" % gi_))
                          for tb in range(nb):
                              mb, col0 = (4, tb * 128) if tb < 4 else (5, 0)
                              wt_ = wsT if tb < 4 else wsTs
                              wres = r1("wsT") if tb < 4 else r1("wsTs")
                              for par in range(2):
                                  gidx = 2 * c + par
                                  S.op("pe", "matmul", dict(out=ps[par * 64:(par + 1) * 64, mb, col0:col0 + 128],
                                                            lhsT=sgvg[:, tb, gidx * 64:(gidx + 1) * 64], rhs=wt_[:, gidx, :],
                                                            start=True, stop=True, tile_position=(0, par * 64)),
                                       reads=[Rsg[tb], wres], writes=[Rbank[mb]], inc=(par == 1))
                          while pend_rb:
                              emit_rb(*pend_rb.pop(0))
                          S.op("dve", "tensor_tensor", dict(out=go[gi_][:, 0:512].rearrange("p (t i) -> p t i", i=128),
                                                            in0=ps[:, 4, :].rearrange("p (t i) -> p t i", i=128),
                                                            in1=bsT[:, c, :].unsqueeze(1).broadcast_to([128, 4, 128]), op=ALU.add),
                               reads=[Rbank[4], r1("bsT")], writes=WR("go%d" % gi_))
                          if has_s:
                              S.op("dve", "tensor_tensor", dict(out=go[gi_][:, 512:640].rearrange("p (s i) -> p s i", i=32),
                                                                in0=ps[:, 5, 0:128].rearrange("p (s i) -> p s i", i=32),
                                                                in1=bsT[:, c, 0:32].unsqueeze(1).broadcast_to([128, 4, 32]), op=ALU.add),
                                   reads=[Rbank[5], r1("bsT")], writes=WR("go%d" % gi_))
                          S.op("dve", "tensor_tensor", dict(out=go[gi_][:, 0:ntok], in0=go[gi_][:, 0:ntok], in1=gu[:, c4, 0:ntok], op=ALU.mult),
                               reads=RD("go%d" % gi_) + RD("qT%d" % c4), writes=WR("go%d" % gi_))
                          S.op("act", "activation", dict(out=sqb[gi_][:, 0:ntok], in_=go[gi_][:, 0:ntok], func=AF.Square),
                               reads=RD("go%d" % gi_), writes=WR("sqb%d" % gi_))
                          pend_rb.append((gi_, c))
                          S.op("dve", "scalar_tensor_tensor", dict(out=zT[:, c, 0:ntok], in0=go[gi_][:, 0:ntok], scalar=gG[:, l * 8 + c:l * 8 + c + 1],
                                                                   in1=sgb[gi_][:, 0:ntok], op0=ALU.mult, op1=ALU.mult),
                               reads=RD("go%d" % gi_) + RD("sgb%d" % gi_) + [r1("gG")], writes=[RzT[c][t] for t in range(nb)])
                  chk(7)

                  def finish_rb():
                      while pend_rb:
                          emit_rb(*pend_rb.pop(0))
                      S.op("dve", "tensor_reduce", dict(out=st_rb[:, 0:nb], in_=ps[:, 2, 0:64].rearrange("p (c t) -> p t c", t=8)[:, 0:nb, :],
                                                        axis=AX.X, op=ALU.add),
                           reads=[Rbank[2]], writes=[r1("st_rb")])
                      rsqrt_small(st_rb[:, 0:nb], st_rb[:, 0:nb], nb, 1.0 / 1024, [r1("st_rb")], [r1("st_rb")])

                  if l + 1 < L:
                      ld("dma_start", dict(out=tabA[:], in_=norm_in[l + 1:l + 2, :].partition_broadcast(128)), [r1("tabA")])
                      out_proj(1, st_rb, r1("st_rb"), after_tb=phase0_block, early=finish_rb)
                  else:
                      ld("dma_start", dict(out=tabA[:], in_=norm_final[0:1, :].partition_broadcast(128)), [r1("tabA")])
                      if nxt is not None:
                          ld("dma_start", dict(out=tabB, in_=norm_in[0:1, :].partition_broadcast(128)), WR("tabB"))
                      out_proj(1, st_rb, r1("st_rb"), after_tb=final_block, early=finish_rb)

        except _Stop:
            pass

        S.final_waits("sp")

        with nc.allow_non_contiguous_dma(reason="tiny gain / spatial-weight tables only"), nc.Block() as block:
            @block.tensor
            def _(e):
                S.replay("pe", e, sems)

            @block.scalar
            def _(e):
                S.replay("act", e, sems)

            @block.vector
            def _(e):
                S.replay("dve", e, sems)

            @block.gpsimd
            def _(e):
                S.replay("pool", e, sems)

            @block.sync
            def _(e):
                S.replay("sp", e, sems)
    return nc


_OH = None


def _core_inputs(inp, c, n_seq=2):
    global _OH
    if _OH is None:
        _OH = _bucket_onehot()
    f = lambda a: np.ascontiguousarray(np.asarray(a, dtype=np.float32))
    L = inp["w_in"].shape[0]
    return {
        "xp": f(inp["x_prompt"][n_seq * c:n_seq * (c + 1)]),
        "xs": f(inp["x_sample"][4 * c:4 * (c + 1)]).reshape(128, D),
        "ck": f(inp["cache_k"][:, 4 * c:4 * (c + 1)]).reshape(L, 4, 128, 256),
        "cv": f(inp["cache_v"][:, 4 * c:4 * (c + 1)]).reshape(L, 4, 128, 256),
        "w_in": f(inp["w_in"]), "w_out": f(inp["w_out"]), "norm_in": f(inp["norm_in"]),
        "rel_bias": f(inp["rel_bias"]), "sinks": f(inp["sinks"]),
        "norm_attn": f(inp["norm_attn"]), "norm_gmlp": f(inp["norm_gmlp"]),
        "ln_g": f(inp["ln_v_g"]), "ln_b": f(inp["ln_v_b"]),
        "w_sp": f(inp["w_spatial"]), "b_sp": f(inp["b_spatial"]),
        "norm_final": f(inp["norm_final"]).reshape(1, D), "oh": _OH,
    }


def kernel(x_prompt, x_sample, cache_k, cache_v, w_in, w_out, norm_in, rel_bias, sinks,
           norm_attn, norm_gmlp, ln_v_g, ln_v_b, w_spatial, b_spatial, norm_final):
    inp = dict(x_prompt=x_prompt, x_sample=x_sample, cache_k=cache_k, cache_v=cache_v, w_in=w_in, w_out=w_out,
               norm_in=norm_in, rel_bias=rel_bias, sinks=sinks, norm_attn=norm_attn, norm_gmlp=norm_gmlp,
               ln_v_g=ln_v_g, ln_v_b=ln_v_b, w_spatial=w_spatial, b_spatial=b_spatial, norm_final=norm_final)
    inp = {k: np.asarray(v) for k, v in inp.items()}
    L = inp["w_in"].shape[0]
    nc = build_program(n_layers=L)
    shared = None
    in_maps = []
    for c in range(NCORES):
        m = _core_inputs(inp, c)
        if shared is None:
            shared = {k: m[k] for k in ("w_in", "w_out", "norm_in", "rel_bias", "sinks", "norm_attn", "norm_gmlp",
                                        "ln_g", "ln_b", "w_sp", "b_sp", "norm_final", "oh")}
        else:
            m.update(shared)
        in_maps.append(m)
    res = run_bass_kernel_spmd(nc, in_maps, core_ids=list(range(NCORES)))
    rs = res.results
    cat = lambda key, axis: np.concatenate([np.asarray(r[key]) for r in rs], axis=axis)
    y_prompt = cat("y_p", 0)
    y_sample = cat("y_s", 0).reshape(32, 32, D)
    nkp = cat("nkp", 1).reshape(L, 16, 128, 4, 64)
    nvp = cat("nvp", 1).reshape(L, 16, 128, 4, 64)
    nks = cat("nks", 1).reshape(L, 32, 128, 4, 64)
    nvs = cat("nvs", 1).reshape(L, 32, 128, 4, 64)
    nvg = np.stack([np.asarray(r["nvg"]) for r in rs], 1).reshape(L, 32, 32, 16, 64)
    return (y_prompt.astype(np.float32), y_sample.astype(np.float32), nkp.astype(np.float32), nvp.astype(np.float32),
            nks.astype(np.float32), nvs.astype(np.float32), nvg.astype(np.float32))
```
